# Optimizing a Trainium2 kernel written in Bass

```python
import math
import functools
import numpy as np
import jax
import jax.numpy as jnp
from jax import lax

D_MODEL = 1024
BATCH = 4
SEQ = 4096
DEPTH = 4
DEC_BATCH = 32
DEC_SEQ = 1
PAST_LEN = 8192
PAGE_SIZE = 128

EPS = 1e-6
N_HYB = (DEPTH + 1) // 2
N_GDN = DEPTH // 2

A_HEADS = 8
A_HEAD_DIM = 64
A_WIDTH = A_HEADS * A_HEAD_DIM
A_PATTERNS = ((128, 1), (512, 4), (2048, 16))
A_WIN_MAX = 2048
A_QBLOCK = 128
REL_BUCKETS = 32
REL_MAX_DIST = 2048

SSM_D_INNER = 1024
SSM_HEAD_DIM = 64
SSM_HEADS = SSM_D_INNER // SSM_HEAD_DIM
SSM_GROUPS = 2
SSM_STATE = 128
SSM_CONV = 4
SSM_CHUNK = 128
SSM_XBC = SSM_D_INNER + 2 * SSM_GROUPS * SSM_STATE

HYB_IN = 3 * A_WIDTH + SSM_D_INNER + SSM_XBC + SSM_HEADS
HYB_MIX = A_WIDTH + SSM_D_INNER

GDN_QK_HEADS = 8
GDN_V_HEADS = 16
GDN_DK = 128
GDN_DV = 128
GDN_CONV = 4
GDN_CHUNK = 64
GDN_QK_W = GDN_QK_HEADS * GDN_DK
GDN_VW = GDN_V_HEADS * GDN_DV
GDN_QKV = 2 * GDN_QK_W + GDN_VW
GDN_IN = GDN_QKV + GDN_VW + 2 * GDN_V_HEADS

D_FF = -(-8 * D_MODEL // (3 * 256)) * 256

kernel_name = 'hybrid_dilated_ssd_gdn_decoder_step'


def rmsnorm(x, g):
    xf = x.astype(jnp.float32)
    y = xf * lax.rsqrt(jnp.mean(xf * xf, axis=-1, keepdims=True) + EPS)
    return (y * g.astype(jnp.float32)).astype(x.dtype)


def l2norm(x):
    return x * lax.rsqrt(jnp.sum(x * x, axis=-1, keepdims=True) + EPS)


def causal_conv(x, buf, w):
    k = w.shape[0]
    L = x.shape[1]
    xp = jnp.concatenate([buf.astype(x.dtype), x], axis=1)
    y = xp[:, 0:L] * w[0]
    for i in range(1, k):
        y = y + xp[:, i:i + L] * w[i]
    return y, xp[:, L:]


def rel_buckets(dist):
    max_exact = REL_BUCKETS // 2
    n = np.maximum(dist, 1).astype(np.float32)
    large = max_exact + (np.log(n / max_exact) / math.log(REL_MAX_DIST / max_exact)
                         * (REL_BUCKETS - max_exact)).astype(np.int32)
    large = np.minimum(large, REL_BUCKETS - 1)
    return np.where(dist < max_exact, dist, large).astype(np.int32)


def dilated_window_attention(q, k_all, v_all, prefix, rel_bias):
    bsz, S, H, dh = q.shape
    qb = A_QBLOCK if S % A_QBLOCK == 0 else S
    nb = S // qb
    scale = dh ** -0.5
    pats = []
    for (w, d) in A_PATTERNS:
        offs = (np.arange(w // d + 1) * d).astype(np.int32)
        bias = rel_bias[rel_buckets(offs)].astype(jnp.float32)
        pats.append((offs, bias.T[None, :, None, :]))
    qblocks = jnp.moveaxis(q.reshape(bsz, nb, qb, H, dh), 1, 0)

    def block(args):
        qblk, b0 = args
        rows = prefix + b0 * qb + jnp.arange(qb, dtype=jnp.int32)
        qf = qblk.astype(jnp.float32) * scale
        ms, ss, nums = [], [], []
        for offs, bias in pats:
            idx = rows[:, None] - offs[None, :]
            valid = idx >= 0
            idxc = jnp.maximum(idx, 0)
            kg = jnp.take(k_all, idxc, axis=1).astype(jnp.float32)
            vg = jnp.take(v_all, idxc, axis=1).astype(jnp.float32)
            logits = jnp.einsum('bqhd,bqjhd->bhqj', qf, kg) + bias
            logits = jnp.where(valid[None, None], logits, -1e30)
            m = jnp.max(logits, axis=-1, keepdims=True)
            p = jnp.exp(logits - m)
            ms.append(m)
            ss.append(jnp.sum(p, axis=-1, keepdims=True))
            nums.append(jnp.einsum('bhqj,bqjhd->bhqd', p, vg))
        m_all = functools.reduce(jnp.maximum, ms)
        wts = [jnp.exp(m - m_all) for m in ms]
        num = functools.reduce(jnp.add, [w * n for w, n in zip(wts, nums)])
        den = functools.reduce(jnp.add, [w * s for w, s in zip(wts, ss)])
        return jnp.swapaxes(num / den, 1, 2)

    out = lax.map(block, (qblocks, jnp.arange(nb, dtype=jnp.int32)))
    return jnp.moveaxis(out, 0, 1).reshape(bsz, S, H, dh).astype(q.dtype)


def ssd_scan(x, dt, bm, cm, a_neg, h0):
    f32 = jnp.float32
    bsz, L = x.shape[0], x.shape[1]
    ch = min(SSM_CHUNK, L)
    nc = -(-L // ch)
    pad = nc * ch - L

    def padl(a):
        return jnp.pad(a.astype(f32), [(0, 0), (0, pad)] + [(0, 0)] * (a.ndim - 2))

    G, HG, P, N = SSM_GROUPS, SSM_HEADS // SSM_GROUPS, SSM_HEAD_DIM, SSM_STATE
    xc = padl(x).reshape(bsz, nc, ch, G, HG, P)
    dtc = padl(dt).reshape(bsz, nc, ch, G, HG)
    bc = padl(bm).reshape(bsz, nc, ch, G, N)
    cc = padl(cm).reshape(bsz, nc, ch, G, N)
    cs = jnp.cumsum(dtc * a_neg.reshape(G, HG), axis=2)
    cst = jnp.moveaxis(cs, 2, -1)
    causal = np.tril(np.ones((ch, ch), dtype=bool))
    seg = jnp.exp(jnp.where(causal, cst[..., :, None] - cst[..., None, :], -jnp.inf))
    cb = jnp.einsum('bcign,bcjgn->bcgij', cc, bc)
    mix = cb[:, :, :, None] * seg * jnp.moveaxis(dtc, 2, -1)[..., None, :]
    y_intra = jnp.einsum('bcghij,bcjghp->bcighp', mix, xc)
    last = cs[:, :, -1]
    w_end = jnp.exp(last[:, :, None] - cs) * dtc
    st = jnp.einsum('bcjgh,bcjgn,bcjghp->bcghpn', w_end, bc, xc)

    def step(h, inp):
        dec, s = inp
        return h * dec[..., None, None] + s, h

    h_fin, h_start = lax.scan(step, h0.astype(f32).reshape(bsz, G, HG, P, N),
                              (jnp.moveaxis(jnp.exp(last), 1, 0), jnp.moveaxis(st, 1, 0)))
    h_start = jnp.moveaxis(h_start, 0, 1)
    y_inter = jnp.einsum('bcign,bcghpn->bcighp', cc, h_start) * jnp.exp(cs)[..., None]
    y = (y_intra + y_inter).reshape(bsz, nc * ch, SSM_HEADS, P)[:, :L]
    return y, h_fin.reshape(bsz, SSM_HEADS, P, N)


def gated_delta_rule(q, k, v, g, beta, s0):
    f32 = jnp.float32
    bsz, L, H = v.shape[0], v.shape[1], v.shape[2]
    dv = v.shape[-1]
    ch = min(GDN_CHUNK, L)
    nc = -(-L // ch)
    pad = nc * ch - L

    def chunks(a):
        a = jnp.pad(a.astype(f32), [(0, 0), (0, pad)] + [(0, 0)] * (a.ndim - 2))
        return jnp.moveaxis(a.reshape((bsz, nc, ch) + a.shape[2:]), 3, 1)

    qc, kc, vc, gc, bc = (chunks(a) for a in (q, k, v, g, beta))
    gcum = jnp.cumsum(gc, axis=-1)
    incl = np.tril(np.ones((ch, ch), dtype=bool))
    strict = np.tril(np.ones((ch, ch), dtype=bool), k=-1)
    dec_incl = jnp.exp(jnp.where(incl, gcum[..., :, None] - gcum[..., None, :], -jnp.inf))
    dec_strict = jnp.where(strict, dec_incl, 0.0)
    kb = kc * bc[..., None]
    a_mat = jnp.einsum('bhcid,bhcjd->bhcij', kb, kc) * dec_strict + jnp.eye(ch, dtype=f32)
    rhs = jnp.concatenate([vc * bc[..., None], kb * jnp.exp(gcum)[..., None]], axis=-1)
    sol = lax.linalg.triangular_solve(a_mat, rhs, left_side=True, lower=True, unit_diagonal=True)
    u0, kcd = sol[..., :dv], sol[..., dv:]
    attn = jnp.einsum('bhcid,bhcjd->bhcij', qc, kc) * dec_incl
    qg = qc * jnp.exp(gcum)[..., None]
    kdec = kc * jnp.exp(gcum[..., -1:] - gcum)[..., None]
    glast = jnp.exp(gcum[..., -1])

    def step(s, inp):
        u0_c, kcd_c, attn_c, qg_c, kdec_c, gl_c = inp
        u = u0_c - jnp.einsum('bhik,bhkv->bhiv', kcd_c, s)
        o = jnp.einsum('bhik,bhkv->bhiv', qg_c, s) + jnp.einsum('bhij,bhjv->bhiv', attn_c, u)
        s = s * gl_c[..., None, None] + jnp.einsum('bhjk,bhjv->bhkv', kdec_c, u)
        return s, o

    xs = tuple(jnp.moveaxis(a, 2, 0) for a in (u0, kcd, attn, qg, kdec, glast))
    s_fin, o = lax.scan(step, s0.astype(f32), xs)
    o = jnp.moveaxis(jnp.moveaxis(o, 0, 2), 1, 3).reshape(bsz, nc * ch, H, dv)[:, :L]
    return o, s_fin


def hybrid_mixer(h, k_pre, v_pre, conv_buf, h0, keep, rel_bias, w_in, conv_w, conv_b,
                 dt_bias, a_log, d_skip, norm_w, w_out):
    f32 = jnp.float32
    bsz, L, _ = h.shape
    proj = h @ w_in
    q, k, v, z, xbc, dt_raw = jnp.split(
        proj, [A_WIDTH, 2 * A_WIDTH, 3 * A_WIDTH, 3 * A_WIDTH + SSM_D_INNER,
               3 * A_WIDTH + SSM_D_INNER + SSM_XBC], axis=-1)
    shp = (bsz, L, A_HEADS, A_HEAD_DIM)
    k_all = jnp.concatenate([k_pre.astype(h.dtype), k.reshape(shp)], axis=1)
    v_all = jnp.concatenate([v_pre.astype(h.dtype), v.reshape(shp)], axis=1)
    o_attn = dilated_window_attention(q.reshape(shp), k_all, v_all, k_pre.shape[1], rel_bias)
    xbc, new_conv = causal_conv(xbc, conv_buf, conv_w)
    xbc = jax.nn.silu((xbc + conv_b).astype(f32))
    xs, bm, cm = jnp.split(xbc, [SSM_D_INNER, SSM_D_INNER + SSM_GROUPS * SSM_STATE], axis=-1)
    xs = xs.reshape(bsz, L, SSM_HEADS, SSM_HEAD_DIM)
    dt = jax.nn.softplus(dt_raw.astype(f32) + dt_bias.astype(f32))
    y, h_fin = ssd_scan(xs, dt, bm.reshape(bsz, L, SSM_GROUPS, SSM_STATE),
                        cm.reshape(bsz, L, SSM_GROUPS, SSM_STATE), -jnp.exp(a_log.astype(f32)), h0)
    y = (y + d_skip.astype(f32)[:, None] * xs).reshape(bsz, L, SSM_D_INNER) * jax.nn.silu(z.astype(f32))
    y = y.reshape(bsz, L, SSM_GROUPS, SSM_D_INNER // SSM_GROUPS)
    y = y * lax.rsqrt(jnp.mean(y * y, axis=-1, keepdims=True) + EPS)
    y = y.reshape(bsz, L, SSM_D_INNER) * norm_w.astype(f32)
    mixed = jnp.concatenate([o_attn.reshape(bsz, L, A_WIDTH), y.astype(h.dtype)], axis=-1)
    return mixed @ w_out, k_all[:, -keep:], v_all[:, -keep:], new_conv, h_fin


def gdn_mixer(h, conv_buf, s0, w_in, conv_w, dt_bias, a_log, norm_w, w_out):
    f32 = jnp.float32
    bsz, L, _ = h.shape
    proj = h @ w_in
    qkv, z, b_raw, a_raw = jnp.split(
        proj, [GDN_QKV, GDN_QKV + GDN_VW, GDN_QKV + GDN_VW + GDN_V_HEADS], axis=-1)
    qkv, new_conv = causal_conv(qkv, conv_buf, conv_w)
    qkv = jax.nn.silu(qkv.astype(f32))
    q, k, v = jnp.split(qkv, [GDN_QK_W, 2 * GDN_QK_W], axis=-1)
    rep = GDN_V_HEADS // GDN_QK_HEADS
    q = jnp.repeat(l2norm(q.reshape(bsz, L, GDN_QK_HEADS, GDN_DK)) * GDN_DK ** -0.5, rep, axis=2)
    k = jnp.repeat(l2norm(k.reshape(bsz, L, GDN_QK_HEADS, GDN_DK)), rep, axis=2)
    v = v.reshape(bsz, L, GDN_V_HEADS, GDN_DV)
    beta = jax.nn.sigmoid(b_raw.astype(f32))
    g = -jnp.exp(a_log.astype(f32)) * jax.nn.softplus(a_raw.astype(f32) + dt_bias.astype(f32))
    o, s_fin = gated_delta_rule(q, k, v, g, beta, s0)
    o = rmsnorm(o, norm_w) * jax.nn.silu(z.astype(f32).reshape(bsz, L, GDN_V_HEADS, GDN_DV))
    return o.reshape(bsz, L, GDN_VW).astype(h.dtype) @ w_out, new_conv, s_fin


def swiglu(h, w_gate, w_up, w_down):
    return (jax.nn.silu(h @ w_gate) * (h @ w_up)) @ w_down


def setup_inputs(seed: int = 0) -> dict:
    key = jax.random.key(seed)
    ks = jax.random.split(key, 40)
    f32 = jnp.float32
    win_buf = min(A_WIN_MAX, PAST_LEN)

    def nrm(k, shape, scale):
        return jax.random.normal(k, shape, f32) * scale

    def gain(k, shape):
        return 1.0 + 0.02 * jax.random.normal(k, shape, f32)

    def dt_bias_init(k, shape):
        dt = jnp.exp(jax.random.uniform(k, shape, f32, math.log(1e-3), math.log(1e-1)))
        return dt + jnp.log(-jnp.expm1(-dt))

    def a_log_init(k, shape):
        return jnp.log(jax.random.uniform(k, shape, f32, 1.0, 16.0))

    return {
        'x_prompt': nrm(ks[0], (BATCH, SEQ, D_MODEL), 1.0),
        'x_sample': nrm(ks[1], (DEC_BATCH, DEC_SEQ, D_MODEL), 1.0),
        'cache_attn_k': nrm(ks[2], (N_HYB, DEC_BATCH, win_buf, A_HEADS, A_HEAD_DIM), 1.0),
        'cache_attn_v': nrm(ks[3], (N_HYB, DEC_BATCH, win_buf, A_HEADS, A_HEAD_DIM), 1.0),
        'state_ssm_conv': nrm(ks[4], (N_HYB, DEC_BATCH, SSM_CONV - 1, SSM_XBC), 1.0),
        'state_ssm': nrm(ks[5], (N_HYB, DEC_BATCH, SSM_HEADS, SSM_HEAD_DIM, SSM_STATE), 0.1),
        'state_gdn_conv': nrm(ks[6], (N_GDN, DEC_BATCH, GDN_CONV - 1, GDN_QKV), 1.0),
        'state_gdn': nrm(ks[7], (N_GDN, DEC_BATCH, GDN_V_HEADS, GDN_DK, GDN_DV), 0.1),
        'rel_bias': nrm(ks[8], (REL_BUCKETS, A_HEADS), 0.5),
        'norm_mix_pre': gain(ks[9], (DEPTH, D_MODEL)),
        'norm_mix_post': gain(ks[10], (DEPTH, D_MODEL)),
        'norm_ffn_pre': gain(ks[11], (DEPTH, D_MODEL)),
        'norm_ffn_post': gain(ks[12], (DEPTH, D_MODEL)),
        'w_hyb_in': nrm(ks[13], (N_HYB, D_MODEL, HYB_IN), D_MODEL ** -0.5),
        'ssm_conv_w': nrm(ks[14], (N_HYB, SSM_CONV, SSM_XBC), SSM_CONV ** -0.5),
        'ssm_conv_b': nrm(ks[15], (N_HYB, SSM_XBC), 0.02),
        'ssm_dt_bias': dt_bias_init(ks[16], (N_HYB, SSM_HEADS)),
        'ssm_a_log': a_log_init(ks[17], (N_HYB, SSM_HEADS)),
        'ssm_d': 1.0 + 0.1 * jax.random.normal(ks[18], (N_HYB, SSM_HEADS), f32),
        'ssm_norm_w': gain(ks[19], (N_HYB, SSM_D_INNER)),
        'w_hyb_out': nrm(ks[20], (N_HYB, HYB_MIX, D_MODEL), HYB_MIX ** -0.5),
        'w_gdn_in': nrm(ks[21], (N_GDN, D_MODEL, GDN_IN), D_MODEL ** -0.5),
        'gdn_conv_w': nrm(ks[22], (N_GDN, GDN_CONV, GDN_QKV), GDN_CONV ** -0.5),
        'gdn_dt_bias': dt_bias_init(ks[23], (N_GDN, GDN_V_HEADS)),
        'gdn_a_log': a_log_init(ks[24], (N_GDN, GDN_V_HEADS)),
        'gdn_norm_w': gain(ks[25], (N_GDN, GDN_DV)),
        'w_gdn_out': nrm(ks[26], (N_GDN, GDN_VW, D_MODEL), GDN_VW ** -0.5),
        'w_ffn_gate': nrm(ks[27], (DEPTH, D_MODEL, D_FF), D_MODEL ** -0.5),
        'w_ffn_up': nrm(ks[28], (DEPTH, D_MODEL, D_FF), D_MODEL ** -0.5),
        'w_ffn_down': nrm(ks[29], (DEPTH, D_FF, D_MODEL), D_FF ** -0.5),
    }


def reference(x_prompt, x_sample, cache_attn_k, cache_attn_v, state_ssm_conv, state_ssm,
              state_gdn_conv, state_gdn, rel_bias, norm_mix_pre, norm_mix_post, norm_ffn_pre,
              norm_ffn_post, w_hyb_in, ssm_conv_w, ssm_conv_b, ssm_dt_bias, ssm_a_log, ssm_d,
              ssm_norm_w, w_hyb_out, w_gdn_in, gdn_conv_w, gdn_dt_bias, gdn_a_log, gdn_norm_w,
              w_gdn_out, w_ffn_gate, w_ffn_up, w_ffn_down):

    def trunk(x, k_pre, v_pre, sconv, sssm, gconv, gstate, keep):
        nk, nv, nsc, nss, ngc, ngs = [], [], [], [], [], []
        for l in range(DEPTH):
            i = l // 2
            h = rmsnorm(x, norm_mix_pre[l])
            if l % 2 == 0:
                m, k_new, v_new, c_new, s_new = hybrid_mixer(
                    h, k_pre[i], v_pre[i], sconv[i], sssm[i], keep, rel_bias, w_hyb_in[i],
                    ssm_conv_w[i], ssm_conv_b[i], ssm_dt_bias[i], ssm_a_log[i], ssm_d[i],
                    ssm_norm_w[i], w_hyb_out[i])
                nk.append(k_new)
                nv.append(v_new)
                nsc.append(c_new)
                nss.append(s_new)
            else:
                m, c_new, s_new = gdn_mixer(h, gconv[i], gstate[i], w_gdn_in[i], gdn_conv_w[i],
                                            gdn_dt_bias[i], gdn_a_log[i], gdn_norm_w[i], w_gdn_out[i])
                ngc.append(c_new)
                ngs.append(s_new)
            x = x + rmsnorm(m, norm_mix_post[l])
            f = swiglu(rmsnorm(x, norm_ffn_pre[l]), w_ffn_gate[l], w_ffn_up[l], w_ffn_down[l])
            x = x + rmsnorm(f, norm_ffn_post[l])
        return (x, jnp.stack(nk), jnp.stack(nv), jnp.stack(nsc), jnp.stack(nss),
                jnp.stack(ngc), jnp.stack(ngs))

    bsz, s_len = x_prompt.shape[0], x_prompt.shape[1]
    dt_p = x_prompt.dtype
    p_k0 = jnp.zeros((N_HYB, bsz, 0, A_HEADS, A_HEAD_DIM), dt_p)
    p_sc0 = jnp.zeros((N_HYB, bsz, SSM_CONV - 1, SSM_XBC), dt_p)
    p_ss0 = jnp.zeros((N_HYB, bsz, SSM_HEADS, SSM_HEAD_DIM, SSM_STATE), jnp.float32)
    p_gc0 = jnp.zeros((N_GDN, bsz, GDN_CONV - 1, GDN_QKV), dt_p)
    p_gs0 = jnp.zeros((N_GDN, bsz, GDN_V_HEADS, GDN_DK, GDN_DV), jnp.float32)
    y_prompt, pk, pv, psc, pss, pgc, pgs = trunk(
        x_prompt, p_k0, p_k0, p_sc0, p_ss0, p_gc0, p_gs0, min(A_WIN_MAX, s_len))
    y_sample, sk, sv, ssc, sss, sgc, sgs = trunk(
        x_sample, cache_attn_k, cache_attn_v, state_ssm_conv, state_ssm, state_gdn_conv,
        state_gdn, cache_attn_k.shape[2])
    return (y_prompt, y_sample, pk, pv, psc, pss, pgc, pgs, sk, sv, ssc, sss, sgc, sgs)
```

```python
import math
from contextlib import ExitStack
import numpy as np
import concourse.bass as bass
import concourse.mybir as mybir
from concourse.bass_utils import run_bass_kernel_spmd

F32 = mybir.dt.float32
F32R = mybir.dt.float32r
AF = mybir.ActivationFunctionType
ALU = mybir.AluOpType
AX = mybir.AxisListType

D = 1024
EPS = 1e-6
A_W = 512
WIN = 2048
NKT = 17
SSM_DI = 1024
SSM_XBC = 1536
HYB_IN = 4112
HYB_MIX = 1536
G_VW = 2048
G_QKV = 4096
G_IN = 6176
D_FF = 2816
NEG = -30000.0
NCORES = 8
TABL = 2304


def rel_buckets(dist):
    max_exact = 16
    n = np.maximum(dist, 1).astype(np.float32)
    large = max_exact + (np.log(n / max_exact) / math.log(2048 / max_exact) * (32 - max_exact)).astype(np.int32)
    large = np.minimum(large, 31)
    return np.where(dist < max_exact, dist, large).astype(np.int32)


def attn_tables():
    dist = 2175 - np.arange(TABL)
    valid = (dist >= 0) & (dist <= 2048)
    dc = np.clip(dist, 0, 2048)
    cnt = ((dc <= 128).astype(np.float64) + ((dc % 4 == 0) & (dc <= 512)) + ((dc % 16 == 0) & (dc <= 2048)))
    cnt = np.where(valid, cnt, 0.0)
    logc = np.where(cnt > 0, np.log(np.maximum(cnt, 1e-9)), NEG).astype(np.float32)
    bk = rel_buckets(dc)
    oh = np.zeros((33, TABL), np.float32)
    oh[bk, np.arange(TABL)] = np.where(cnt > 0, 1.0, 0.0)
    oh[32, :] = logc
    return oh


class Sched:
    def __init__(self, nc, es):
        self.nc = nc
        self.es = es
        self.eng = {"pe": nc.tensor, "act": nc.scalar, "dve": nc.vector, "pool": nc.gpsimd, "sp": nc.sync}
        self.sem = {}
        self.cnt = {}
        self.nsem = 0
        for e in self.eng:
            self._new_sem(e)
        self.dma_sems = []
        for i in range(48):
            self.dma_sems.append([es.enter_context(nc.semaphore("dq%d" % i)), 0])
        self.dma_rr = 0
        self.waited = {e: {} for e in self.eng}
        self.lastw = {}
        self.readers = {}
        self.ninstr = 0
        self.nwait = 0

    def _new_sem(self, e):
        self.sem[e] = self.es.enter_context(self.nc.semaphore("s_%s_%d" % (e, self.nsem)))
        self.nsem += 1
        self.cnt[e] = 0

    def _wait(self, e, dep):
        sem, val = dep
        w = self.waited[e]
        k = id(sem)
        if w.get(k, 0) >= val:
            return
        w[k] = val
        self.eng[e].wait_ge(sem, val)
        self.nwait += 1

    def _deps(self, e, reads, writes):
        for k in reads:
            d = self.lastw.get(k)
            if d is not None:
                self._wait(e, d)
        for k in writes:
            d = self.lastw.get(k)
            if d is not None:
                self._wait(e, d)
            r = self.readers.get(k)
            if r:
                for d in r.values():
                    self._wait(e, d)

    def _commit(self, tok, reads, writes):
        for k in writes:
            self.lastw[k] = tok
            self.readers[k] = {}
        for k in reads:
            r = self.readers.setdefault(k, {})
            r[id(tok[0])] = tok

    def op(self, e, fn, reads=(), writes=()):
        self._deps(e, reads, writes)
        if self.cnt[e] >= 30000:
            self._new_sem(e)
        inst = fn(self.eng[e])
        inst.then_inc(self.sem[e], 1)
        self.cnt[e] += 1
        self.ninstr += 1
        tok = (self.sem[e], self.cnt[e])
        self._commit(tok, reads, writes)

    def dma(self, e, out, in_, reads=(), writes=(), slow=False):
        self._deps(e, reads, writes)
        slot = self.dma_sems[self.dma_rr]
        self.dma_rr = (self.dma_rr + 1) % len(self.dma_sems)
        if slot[1] > 0:
            self._wait(e, (slot[0], slot[1]))
        if slot[1] >= 30000:
            slot[0] = self.es.enter_context(self.nc.semaphore("dqx%d" % self.nsem))
            self.nsem += 1
            slot[1] = 0
        if slow:
            self.eng[e].dma_start(out=out, in_=in_, allow_slow_non_contiguous=True).then_inc(slot[0], 16)
        else:
            self.eng[e].dma_start(out=out, in_=in_).then_inc(slot[0], 16)
        slot[1] += 16
        self.ninstr += 1
        tok = (slot[0], slot[1])
        self._commit(tok, reads, writes)


def build_nc(n_ptiles, n_samp, depth=4, mm_r=True, dbg=None):
    nc = bass.Bass("TRN2", target_bir_lowering=False)
    SEQ = n_ptiles * 128
    NHYB = (depth + 1) // 2
    NGDN = depth // 2
    NG1 = max(NGDN, 1)
    NS1 = max(n_samp, 1)
    KEEP = min(WIN, SEQ)
    MMDT = F32R if mm_r else F32
    wq = "pool" if mm_r else "sp"

    def din(name, shape):
        return nc.dram_tensor(name, list(shape), F32, kind="ExternalInput")

    def dout(name, shape):
        return nc.dram_tensor(name, list(shape), F32, kind="ExternalOutput")

    def dscr(name, shape):
        return nc.dram_tensor(name, list(shape), F32, kind="Internal")

    x_prompt = din("x_prompt", [SEQ, D])
    x_sample = din("x_sample", [NS1, D])
    cache_k = din("cache_k", [NHYB, NS1, WIN, A_W])
    cache_v = din("cache_v", [NHYB, NS1, WIN, A_W])
    st_sconv = din("st_sconv", [NHYB, NS1, 3, SSM_XBC])
    st_ssm = din("st_ssm", [NHYB, NS1, 1024, 128])
    st_gconv = din("st_gconv", [NG1, NS1, 3, G_QKV])
    st_gdn = din("st_gdn", [NG1, NS1, 16, 128, 128])
    rel_bias = din("rel_bias", [32, 8])
    oh_tab = din("oh_tab", [33, TABL])
    norm_mix_pre = din("norm_mix_pre", [depth, D])
    norm_mix_post = din("norm_mix_post", [depth, D])
    norm_ffn_pre = din("norm_ffn_pre", [depth, D])
    norm_ffn_post = din("norm_ffn_post", [depth, D])
    w_hyb_in = din("w_hyb_in", [NHYB, D, HYB_IN])
    ssm_conv_w = din("ssm_conv_w", [NHYB, 4, SSM_XBC])
    ssm_conv_b = din("ssm_conv_b", [NHYB, SSM_XBC])
    ssm_dt_bias = din("ssm_dt_bias", [NHYB, 16])
    ssm_a_log = din("ssm_a_log", [NHYB, 16])
    ssm_d = din("ssm_d", [NHYB, 16])
    ssm_norm_w = din("ssm_norm_w", [NHYB, SSM_DI])
    w_hyb_out = din("w_hyb_out", [NHYB, HYB_MIX, D])
    w_gdn_in = din("w_gdn_in", [NG1, D, G_IN])
    gdn_conv_w = din("gdn_conv_w", [NG1, 4, G_QKV])
    gdn_dt_bias = din("gdn_dt_bias", [NG1, 16])
    gdn_a_log = din("gdn_a_log", [NG1, 16])
    gdn_norm_w = din("gdn_norm_w", [NG1, 128])
    w_gdn_out = din("w_gdn_out", [NG1, G_VW, D])
    w_ffn_gate = din("w_ffn_gate", [depth, D, D_FF])
    w_ffn_up = din("w_ffn_up", [depth, D, D_FF])
    w_ffn_down = din("w_ffn_down", [depth, D_FF, D])

    y_prompt = dout("y_prompt", [SEQ, D])
    y_sample = dout("y_sample", [NS1, D])
    o_pk = dout("o_pk", [NHYB, KEEP, A_W])
    o_pv = dout("o_pv", [NHYB, KEEP, A_W])
    o_psc = dout("o_psc", [NHYB, 3, SSM_XBC])
    o_pss = dout("o_pss", [NHYB, 1024, 128])
    o_pgc = dout("o_pgc", [NG1, 3, G_QKV])
    o_pgs = dout("o_pgs", [NG1, 16, 128, 128])
    o_sk = dout("o_sk", [NHYB, NS1, WIN, A_W])
    o_sv = dout("o_sv", [NHYB, NS1, WIN, A_W])
    o_ssc = dout("o_ssc", [NHYB, NS1, 3, SSM_XBC])
    o_sss = dout("o_sss", [NHYB, NS1, 1024, 128])
    o_sgc = dout("o_sgc", [NG1, NS1, 3, G_QKV])
    o_sgs = dout("o_sgs", [NG1, NS1, 16, 128, 128])
    dbg_out = {}
    if dbg:
        for nm, shp in dbg.items():
            dbg_out[nm] = dout("dbg_" + nm, shp)

    seqs = [("p", 0, 0, n_ptiles)] + [("s", s, 16, 1) for s in range(n_samp)]
    NSEQ = len(seqs)
    SCR_ROWS = max(SEQ, WIN + 128)
    kt_scr = dscr("kt_scr", [NHYB, NSEQ, 128, 4, SCR_ROWS])
    k_scr = dscr("k_scr", [NHYB, NSEQ, SCR_ROWS, A_W])
    v_scr = dscr("v_scr", [NHYB, NSEQ, SCR_ROWS, A_W])
    ftab = dscr("ftab", [8, TABL])

    es = ExitStack()
    with es:
        S = Sched(nc, es)

        def sb(name, shape, dt=F32):
            return es.enter_context(nc.sbuf_tensor(name, list(shape), dt))

        psum = [es.enter_context(nc.psum_tensor("ps%d" % i, [128, 512], F32)) for i in range(8)]
        ps_rr = [0]

        def ps():
            i = ps_rr[0]
            ps_rr[0] = (i + 1) % 6
            return psum[i], "ps%d" % i

        acc_rr = [0]

        def ps_acc():
            acc_rr[0] ^= 1
            i = 6 + acc_rr[0]
            return psum[i], "ps%d" % i

        ev_rr = [0]

        def evac_eng():
            ev_rr[0] ^= 1
            return "act" if ev_rr[0] else "dve"

        def copy(e, out, in_, reads, writes):
            if e == "act":
                S.op("act", lambda g: g.activation(out=out, in_=in_, func=AF.Copy), reads, writes)
            else:
                S.op(e, lambda g: g.tensor_copy(out, in_), reads, writes)

        def dump(name, ap, keys):
            if name in dbg_out:
                t = dbg_out[name]
                S.dma("sp", t.ap() if hasattr(t, "ap") else t[:], ap, keys, ["dbg_" + name])

        def mask_const(name, pattern, op, base, cm):
            t = sb(name, [128, 128])
            S.op("pool", lambda g: g.memset(t[:], 1.0), [], [name])
            S.op("pool", lambda g: g.affine_select(out=t[:], in_=t[:], pattern=pattern, compare_op=op, fill=0.0,
                                                   base=base, channel_multiplier=cm), [name], [name])
            return t

        ident = mask_const("ident", [[-1, 128]], ALU.is_equal, 0, 1)
        antiI = mask_const("antiI", [[1, 128]], ALU.is_equal, -127, 1)
        Umat = mask_const("Umat", [[1, 128]], ALU.is_ge, 0, -1)
        SUmat = mask_const("SUmat", [[1, 128]], ALU.is_gt, 0, -1)
        SLmat = mask_const("SLmat", [[-1, 128]], ALU.is_gt, 0, 1)
        ones = sb("ones", [128, 128])
        S.op("pool", lambda g: g.memset(ones[:], 1.0), [], ["ones"])
        ccol = sb("ccol", [128, 4])
        S.op("pool", lambda g: g.memset(ccol[:, 0:1], EPS), [], ["ccol"])
        S.op("pool", lambda g: g.memset(ccol[:, 1:2], 1.0), ["ccol"], ["ccol"])
        S.op("pool", lambda g: g.memset(ccol[:, 2:3], 0.0), ["ccol"], ["ccol"])
        c_eps = ccol[:, 0:1]
        c_one = ccol[:, 1:2]
        CONSTS = ["ident", "antiI", "Umat", "SUmat", "SLmat", "ones", "ccol"]

        xres = sb("xres", [128, D])
        hT = sb("hT", [128, 8, 128], MMDT)
        xn = sb("xn", [128, D])
        stat = sb("stat", [128, 8])
        gpost = sb("gpost", [128, D])
        WBUF = 4096
        NWB = 2
        wbuf = [sb("wbuf%d" % i, [128, WBUF], MMDT) for i in range(NWB)]
        wb_rr = [0]
        big = sb("big", [128, 6176])
        mixed = sb("mixed", [128, 2048])
        mixT = sb("mixT", [128, 22, 128], MMDT)
        hTr = mixT[:, 12:20, :]
        valid = sb("valid", [128, 1])
        BIGK = [("big", i) for i in range(13)]

        def bk(c0, c1):
            return [("big", i) for i in range(c0 // 512, (c1 - 1) // 512 + 1)]

        conv_h = [sb("conv_h%d" % i, [128, 12, 3]) for i in range(NHYB)]
        ssm_st = [sb("ssm_st%d" % i, [128, 1024]) for i in range(NHYB)]
        gconv_h = [sb("gconv_h%d" % i, [128, 32, 3]) for i in range(NGDN)]
        gdn_st = [sb("gdn_st%d" % i, [128, 16, 128]) for i in range(NGDN)]

        qT = sb("qT", [128, 4, 128])
        kTt = sb("kTt", [128, 4, 128])
        xbcT = sb("xbcT", [128, 12, 131])
        xbcA = sb("xbcA", [128, 12, 128])
        dtb = sb("dtb", [128, 16])
        alog = sb("alog", [128, 16])
        dsk = sb("dsk", [128, 16])
        dtt = sb("dtt", [128, 16])
        av = sb("av", [128, 16])
        csb = sb("csb", [128, 16])
        dec_b = sb("dec_b", [128, 16])
        dec_w = sb("dec_w", [128, 16])
        beta = sb("beta", [128, 16])
        rn = sb("rn", [128, 16])
        seg = sb("seg", [128, 16, 128])
        cbm = sb("cbm", [128, 2, 128])
        xtok = sb("xtok", [128, 1024])
        xw = sb("xw", [128, 1024])
        btok = sb("btok", [128, 256])
        ysb = sb("ysb", [128, 1024])
        ktw = sb("ktw", [128, NKT * 128])
        vw = sb("vw", [128, NKT, 128])
        Lb = sb("Lb", [128, NKT * 128])
        aU = Lb[:, 0:2048].rearrange("p (h i) -> p h i", i=128)
        Bb = sb("Bb", [128, NKT * 128])
        ETb = sb("ETb", [128, 4, 128])
        mx = sb("mx", [128, 4])
        stage = sb("stage", [128, 4, 128])
        SEGK = [("seg", q4) for q4 in range(4)]
        tm3 = big

        rb = big[0:33, 0:8]
        ohs = big[0:33, 512:512 + TABL]
        fsb = big[0:8, 3072:3072 + TABL]
        S.dma("sp", big[0:32, 0:8], rel_bias[:, :], [], [("big", 0)])
        S.op("pool", lambda g: g.memset(big[32:33, 0:8], 1.0), [], [("big", 0)])
        S.dma("sp", ohs, oh_tab[:, :], [], bk(512, 512 + TABL))
        for c0 in range(0, TABL, 512):
            cw = min(512, TABL - c0)
            p_, pk_ = ps()
            S.op("pe", lambda g: g.matmul(p_[0:8, 0:cw], rb, big[0:33, 512 + c0:512 + c0 + cw], start=True, stop=True),
                 BIGK, [pk_])
            copy("dve", big[0:8, 3072 + c0:3072 + c0 + cw], p_[0:8, 0:cw], [pk_], bk(3072 + c0, 3072 + c0 + cw))
        S.dma("sp", ftab[:, :], fsb, BIGK, ["ftab"])

        gv_all = sb("gv_all", [128, depth * 2, 8])
        for l_ in range(depth):
            S.dma("sp", gv_all[:, 2 * l_, :], bass.AP(norm_mix_pre, l_ * D, [[1, 128], [128, 8]]), [], ["gv_all"], slow=True)
            S.dma("sp", gv_all[:, 2 * l_ + 1, :], bass.AP(norm_ffn_pre, l_ * D, [[1, 128], [128, 8]]), [], ["gv_all"], slow=True)
        cwh_all = sb("cwh_all", [128, NHYB, 12, 4])
        cbh_all = sb("cbh_all", [128, NHYB, 12])
        for i_ in range(NHYB):
            for j_ in range(12):
                S.dma("sp", cwh_all[:, i_, j_, :], bass.AP(ssm_conv_w, i_ * 4 * SSM_XBC + j_ * 128, [[1, 128], [SSM_XBC, 4]]), [], ["cwh_all"], slow=True)
            S.dma("sp", cbh_all[:, i_, :], bass.AP(ssm_conv_b, i_ * SSM_XBC, [[1, 128], [128, 12]]), [], ["cbh_all"], slow=True)
        cwg_all = sb("cwg_all", [128, NG1, 32, 4])
        for i_ in range(NGDN):
            for j_ in range(32):
                S.dma("sp", cwg_all[:, i_, j_, :], bass.AP(gdn_conv_w, i_ * 4 * G_QKV + j_ * 128, [[1, 128], [G_QKV, 4]]), [], ["cwg_all"], slow=True)

        def load_bcast(dst, dkey, src_tensor, off, n):
            S.dma("sp", dst, bass.AP(src_tensor, off, [[0, 128], [1, n]]), [], [dkey])

        def rstd_from_ssq(col, n):
            S.op("act", lambda g: g.activation(out=stat[:, col:col + 1], in_=stat[:, col:col + 1], func=AF.Ln, bias=c_eps, scale=1.0 / n),
                 [("stat", col), "ccol"], [("stat", col)])
            S.op("act", lambda g: g.activation(out=stat[:, col:col + 1], in_=stat[:, col:col + 1], func=AF.Exp, scale=-0.5),
                 [("stat", col)], [("stat", col)])

        def rmsnorm_stats(src, skeys, n, col):
            S.op("act", lambda g: g.activation(out=Bb[:, 0:n], in_=src, func=AF.Square, accum_out=stat[:, col:col + 1]),
                 skeys, ["Bb", ("stat", col)])
            rstd_from_ssq(col, n)

        def prenorm(which, l, need_rev):
            gvec = gv_all[:, 2 * l + which, :]
            rmsnorm_stats(xres[:, :], ["xres"], D, 0)
            S.op("dve", lambda g: g.tensor_scalar(xn[:, :], xres[:, :], stat[:, 0:1], None, ALU.mult),
                 ["xres", ("stat", 0)], ["xn"])
            for rev in ([False, True] if need_rev else [False]):
                dst = hTr if rev else hT
                dk = "hT"
                ko = 12 if rev else 0
                dk = "mixT" if rev else "hT"
                for kc in range(8):
                    p_, pk_ = ps()
                    if rev:
                        S.op("pe", lambda g: g.matmul(p_[:, 0:128], xn[:, kc * 128:(kc + 1) * 128], antiI[:], start=True, stop=True),
                             ["xn", "antiI"], [pk_])
                    else:
                        S.op("pe", lambda g: g.transpose(p_[:, 0:128], xn[:, kc * 128:(kc + 1) * 128], ident[:]), ["xn", "ident"], [pk_])
                    if kc % 2:
                        S.op("dve", lambda g: g.tensor_scalar(dst[:, kc, :], p_[:, 0:128], gvec[:, kc:kc + 1], None, ALU.mult),
                             [pk_, "gv_all"], [(dk, ko + kc)])
                    else:
                        S.op("act", lambda g: g.activation(out=dst[:, kc, :], in_=p_[:, 0:128], func=AF.Copy, scale=gvec[:, kc:kc + 1]),
                             [pk_, "gv_all"], [(dk, ko + kc)])

        def load_w(W, l, K, N, c0, cw):
            KC = K // 128
            i = wb_rr[0]
            wb_rr[0] = (i + 1) % NWB
            wv = wbuf[i][:, 0:KC * cw].rearrange("p (k c) -> p k c", c=cw)
            src = bass.AP(W, l * K * N + c0, [[N, 128], [128 * N, KC], [1, cw]])
            S.dma(wq, wv, src, [], ["wbuf%d" % i])
            return wv, "wbuf%d" % i

        def proj_tok(W, l, K, N, c0, c1, src, skeys, evac):
            KC = K // 128
            cwmax = 512 if KC * 512 <= WBUF else (256 if KC * 256 <= WBUF else 128)
            c = c0
            while c < c1:
                cw = min(cwmax, c1 - c)
                wv, wk = load_w(W, l, K, N, c, cw)
                p_, pk_ = ps()
                for kc in range(KC):
                    S.op("pe", lambda g: g.matmul(p_[:, 0:cw], src[:, kc, :], wv[:, kc, :], start=(kc == 0), stop=(kc == KC - 1)),
                         [wk] + skeys, [pk_])
                evac(p_, pk_, c, cw)
                c += cw

        def proj_feat(W, l, K, N, c0, ncols, src, skeys, evac):
            KC = K // 128
            cwmax = 512 if KC * 512 <= WBUF else (256 if KC * 256 <= WBUF else 128)
            c = c0
            while c < c0 + ncols:
                cw = min(cwmax, c0 + ncols - c)
                wv, wk = load_w(W, l, K, N, c, cw)
                for j in range(cw // 128):
                    p_, pk_ = ps()
                    for kc in range(KC):
                        S.op("pe", lambda g: g.matmul(p_[:, 0:128], wv[:, kc, j * 128:(j + 1) * 128], src[:, kc, :],
                                                      start=(kc == 0), stop=(kc == KC - 1)), [wk] + skeys, [pk_])
                    evac(p_, pk_, (c - c0) // 128 + j)
                c += cw

        hkeys = [("hT", k) for k in range(8)]
        hrkeys = [("mixT", 12 + k) for k in range(8)]

        def post_norm_residual(gain_dram, l, src, skeys):
            load_bcast(gpost[:, :], "gpost", gain_dram, l * D, D)
            rmsnorm_stats(src, skeys, D, 1)
            S.op("dve", lambda g: g.scalar_tensor_tensor(out=xn[:, :], in0=src, scalar=stat[:, 1:2], in1=gpost[:, :],
                                                         op0=ALU.mult, op1=ALU.mult), skeys + [("stat", 1), "gpost"], ["xn"])
            S.op("pool", lambda g: g.tensor_tensor(out=xres[:, :], in0=xres[:, :], in1=xn[:, :], op=ALU.add),
                 ["xres", "xn"], ["xres"])

        def transposes_to(dst, dkey, j0, src, skeys, n, dtcast=True):
            for c in range(n):
                p_, pk_ = ps()
                S.op("pe", lambda g: g.transpose(p_[:, 0:128], src[:, c * 128:(c + 1) * 128], ident[:]), skeys + ["ident"], [pk_])
                copy(evac_eng(), dst[:, j0 + c, :], p_[:, 0:128], [pk_], [(dkey, j0 + c)])

        def ffn(l):
            prenorm(1, l, False)

            def ev_gate(p_, pk_, c, cw):
                S.op("act", lambda g: g.activation(out=big[:, c:c + cw], in_=p_[:, 0:cw], func=AF.Silu), [pk_], bk(c, c + cw))
            proj_tok(w_ffn_gate, l, D, D_FF, 0, D_FF, hT, hkeys, ev_gate)

            def ev_up(p_, pk_, c, cw):
                S.op("dve", lambda g: g.tensor_tensor(out=big[:, c:c + cw], in0=big[:, c:c + cw], in1=p_[:, 0:cw], op=ALU.mult),
                     [pk_] + bk(c, c + cw), bk(c, c + cw))
            proj_tok(w_ffn_up, l, D, D_FF, 0, D_FF, hT, hkeys, ev_up)
            transposes_to(mixT, "mixT", 0, big, bk(0, D_FF), 22)

            def ev_down(p_, pk_, c, cw):
                copy(evac_eng(), mixed[:, c:c + cw], p_[:, 0:cw], [pk_], [("mixed", c // 128)])
            proj_tok(w_ffn_down, l, D_FF, D, 0, D, mixT, [("mixT", c) for c in range(22)], ev_down)
            post_norm_residual(norm_ffn_post, l, mixed[:, 0:D], [("mixed", c) for c in range(8)])

        def decay_mats(gsrc, gkey):
            p_, pk_ = ps()
            S.op("pe", lambda g: g.matmul(p_[:, 0:16], Umat[:, :], gsrc, start=True, stop=True), ["Umat", gkey], [pk_])
            S.op("act", lambda g: g.activation(out=csb[:, :], in_=p_[:, 0:16], func=AF.Exp), [pk_], ["csb"])
            p_, pk_ = ps()
            S.op("pe", lambda g: g.matmul(p_[:, 0:16], ones[:, :], gsrc, start=True, stop=True), ["ones", gkey], [pk_])
            S.op("act", lambda g: g.activation(out=dec_b[:, :], in_=p_[:, 0:16], func=AF.Exp), [pk_], ["dec_b"])
            S.op("pool", lambda g: g.tensor_tensor(out=aU[:, :, :], in0=Umat[:, :].unsqueeze(1).to_broadcast([128, 16, 128]),
                                                   in1=gsrc.unsqueeze(2).to_broadcast([128, 16, 128]), op=ALU.mult),
                 ["Umat", gkey], ["Lb"])
            for q4 in range(4):
                p_, pk_ = ps()
                S.op("pe", lambda g: g.matmul(p_[:, 0:512], SLmat[:, :], aU[:, q4 * 4:(q4 + 1) * 4, :].rearrange("p h i -> p (h i)"),
                                              start=True, stop=True), ["SLmat", "Lb"], [pk_])
                S.op("act", lambda g: g.activation(out=seg[:, q4 * 4:(q4 + 1) * 4, :].rearrange("p h i -> p (h i)"), in_=p_[:, 0:512], func=AF.Exp),
                     [pk_], [("seg", q4)])
            S.op("dve", lambda g: g.tensor_copy(dec_w[:, :], seg[:, :, 127]), SEGK, ["dec_w"])

        def softplus_inplace(t, key):
            S.op("act", lambda g: g.activation(out=t, in_=t, func=AF.Exp), [key], [key])
            S.op("act", lambda g: g.activation(out=t, in_=t, func=AF.Ln, bias=c_one, scale=1.0), [key, "ccol"], [key])

        def conv_silu(nct, cwv, cwkey, hist, hkey, nvalid, bias_sb):
            xk = [("xbcT", j) for j in range(nct)]
            S.op("pool", lambda g: g.tensor_copy(xbcT[:, 0:nct, 0:3], hist), [hkey] + xk, xk)
            S.op("pool", lambda g: g.tensor_copy(hist, xbcT[:, 0:nct, nvalid:nvalid + 3]), xk, [hkey])
            for j in range(nct):
                S.op("dve", lambda g: g.tensor_scalar(xbcA[:, j, :], xbcT[:, j, 0:128], cwv[:, j, 0:1], None, ALU.mult),
                     [("xbcT", j), cwkey], [("xbcA", j)])
                for t in range(1, 4):
                    S.op("dve", lambda g: g.scalar_tensor_tensor(out=xbcA[:, j, :], in0=xbcT[:, j, t:t + 128], scalar=cwv[:, j, t:t + 1],
                                                                 in1=xbcA[:, j, :], op0=ALU.mult, op1=ALU.add),
                         [("xbcT", j), ("xbcA", j), cwkey], [("xbcA", j)])
                if bias_sb is not None:
                    S.op("act", lambda g: g.activation(out=xbcA[:, j, :], in_=xbcA[:, j, :], func=AF.Silu, bias=bias_sb[:, j:j + 1], scale=1.0),
                         [("xbcA", j), "cbh_all"], [("xbcA", j)])
                else:
                    S.op("act", lambda g: g.activation(out=xbcA[:, j, :], in_=xbcA[:, j, :], func=AF.Silu), [("xbcA", j)], [("xbcA", j)])

        def hybrid_load_params(i):
            load_bcast(dtb[:, :], "dtb", ssm_dt_bias, i * 16, 16)
            load_bcast(alog[:, :], "alog", ssm_a_log, i * 16, 16)
            S.op("act", lambda g: g.activation(out=alog[:, :], in_=alog[:, :], func=AF.Exp), ["alog"], ["alog"])
            S.op("dve", lambda g: g.tensor_scalar(alog[:, :], alog[:, :], -1.0, None, ALU.mult), ["alog"], ["alog"])
            load_bcast(dsk[:, :], "dsk", ssm_d, i * 16, 16)

        def hybrid_layer(l, i, sq, tpos, nvalid):
            full = (nvalid == 128)
            cst = conv_h[i]
            sst = ssm_st[i]
            ck = "conv_h%d" % i
            sk = "ssm_st%d" % i
            prenorm(0, l, True)

            def ev_q(p_, pk_, j):
                copy(evac_eng(), qT[:, j, :], p_[:, 0:128], [pk_], [("qT", j)])
            proj_feat(w_hyb_in, i, D, HYB_IN, 0, 512, hTr, hrkeys, ev_q)

            def ev_kT(p_, pk_, j):
                copy(evac_eng(), kTt[:, j, :], p_[:, 0:128], [pk_], [("kTt", j)])
            proj_feat(w_hyb_in, i, D, HYB_IN, 512, 512, hT, hkeys, ev_kT)
            scr_base = (i * NSEQ + sq) * 128 * 4 * SCR_ROWS
            S.dma("sp", bass.AP(kt_scr, scr_base + tpos * 128, [[4 * SCR_ROWS, 128], [SCR_ROWS, 4], [1, 128]]),
                  kTt[:, :, :], [("kTt", j) for j in range(4)], [("kt_scr", i, sq)])

            def ev_kvz(p_, pk_, c, cw):
                copy(evac_eng(), big[:, c - 512:c - 512 + cw], p_[:, 0:cw], [pk_], bk(c - 512, c - 512 + cw))
            proj_tok(w_hyb_in, i, D, HYB_IN, 512, 2560, hT, hkeys, ev_kvz)
            if not full:
                S.op("dve", lambda g: g.tensor_scalar(big[:, 0:1024], big[:, 0:1024], valid[:, 0:1], None, ALU.mult),
                     bk(0, 1024) + ["valid"], bk(0, 1024))
            S.dma("sp", k_scr[i, sq, tpos * 128:(tpos + 1) * 128, :], big[:, 0:512], bk(0, 512), [("k_scr", i, sq)])
            S.dma("sp", v_scr[i, sq, tpos * 128:(tpos + 1) * 128, :], big[:, 512:1024], bk(512, 1024), [("v_scr", i, sq)])

            def ev_dt(p_, pk_, c, cw):
                copy("dve", big[:, 2048:2064], p_[:, 0:16], [pk_], bk(2048, 2064))
            proj_tok(w_hyb_in, i, D, HYB_IN, 4096, 4112, hT, hkeys, ev_dt)

            def ev_xbc(p_, pk_, j):
                copy(evac_eng(), xbcT[:, j, 3:131], p_[:, 0:128], [pk_], [("xbcT", j)])
            proj_feat(w_hyb_in, i, D, HYB_IN, 2560, 1536, hT, hkeys, ev_xbc)

            t_lo = max(0, tpos - 16)
            nk = tpos - t_lo + 1
            W_ = nk * 128
            off_c = (NKT - nk) * 128
            for pr in range(4):
                S.dma("sp", ktw[:, 0:W_], bass.AP(kt_scr, scr_base + pr * SCR_ROWS + t_lo * 128, [[4 * SCR_ROWS, 128], [1, W_]]),
                      [("kt_scr", i, sq)], ["ktw"])
                S.dma("sp", vw[:, 0:nk, :], bass.AP(v_scr, ((i * NSEQ + sq) * SCR_ROWS + t_lo * 128) * A_W + pr * 128,
                                                    [[A_W, 128], [128 * A_W, nk], [1, 128]]), [("v_scr", i, sq)], ["vw"])
                for hh in range(2):
                    h = pr * 2 + hh
                    pb = hh * 64
                    S.dma("sp", Bb[:, 0:W_], bass.AP(ftab, h * TABL + off_c, [[1, 128], [1, W_]]), ["ftab"], ["Bb"])
                    for c0 in range(0, W_, 512):
                        cw = min(512, W_ - c0)
                        p_, pk_ = ps()
                        S.op("pe", lambda g: g.matmul(p_[:, 0:cw], qT[pb:pb + 64, pr, :], ktw[pb:pb + 64, c0:c0 + cw], start=True, stop=True),
                             [("qT", pr), "ktw"], [pk_])
                        S.op("dve", lambda g: g.scalar_tensor_tensor(out=Lb[:, c0:c0 + cw], in0=p_[:, 0:cw], scalar=0.125, in1=Bb[:, c0:c0 + cw],
                                                                     op0=ALU.mult, op1=ALU.add), [pk_, "Bb"], ["Lb"])
                    S.op("dve", lambda g: g.reduce_max(out=mx[:, 0:1], in_=Lb[:, 0:W_], axis=AX.X), ["Lb"], ["mx0"])
                    S.op("dve", lambda g: g.tensor_scalar(mx[:, 1:2], mx[:, 0:1], -1.0, None, ALU.mult), ["mx0"], ["mx1"])
                    S.op("act", lambda g: g.activation(out=Lb[:, 0:W_], in_=Lb[:, 0:W_], func=AF.Exp, bias=mx[:, 1:2], scale=1.0,
                                                       accum_out=mx[:, 2:3]), ["Lb", "mx1"], ["Lb", "mx2"])
                    S.op("dve", lambda g: g.reciprocal(mx[:, 3:4], mx[:, 2:3]), ["mx2"], ["mx3"])
                    po, pok = ps_acc()
                    for kt in range(nk):
                        p_, pk_ = ps()
                        S.op("pe", lambda g: g.transpose(p_[:, 0:128], Lb[:, kt * 128:(kt + 1) * 128], ident[:]), ["Lb", "ident"], [pk_])
                        copy(evac_eng(), ETb[:, kt % 4, :], p_[:, 0:128], [pk_], [("ETb", kt % 4)])
                        S.op("pe", lambda g: g.matmul(po[:, 0:64], ETb[:, kt % 4, :], vw[:, kt, pb:pb + 64], start=(kt == 0), stop=(kt == nk - 1)),
                             [("ETb", kt % 4), "vw"], [pok])
                    S.op("dve", lambda g: g.tensor_scalar(mixed[:, h * 64:(h + 1) * 64], po[:, 0:64], mx[:, 3:4], None, ALU.mult),
                         [pok, "mx3"], [("mixed", h // 2)])
            for c in range(4):
                p_, pk_ = ps()
                S.op("pe", lambda g: g.matmul(p_[:, 0:128], mixed[:, c * 128:(c + 1) * 128], antiI[:], start=True, stop=True),
                     [("mixed", c), "antiI"], [pk_])
                copy(evac_eng(), mixT[:, c, :], p_[:, 0:128], [pk_], [("mixT", c)])

            conv_silu(12, cwh_all[:, i, :, :], "cwh_all", cst[:, :, :], ck, nvalid, cbh_all[:, i, :])
            for c in range(8):
                p_, pk_ = ps()
                S.op("pe", lambda g: g.transpose(p_[:, 0:128], xbcA[:, c, :], ident[:]), [("xbcA", c), "ident"], [pk_])
                copy(evac_eng(), xtok[:, c * 128:(c + 1) * 128], p_[:, 0:128], [pk_], [("xtok", c)])
            for c in range(2):
                p_, pk_ = ps()
                S.op("pe", lambda g: g.transpose(p_[:, 0:128], xbcA[:, 8 + c, :], ident[:]), [("xbcA", 8 + c), "ident"], [pk_])
                copy(evac_eng(), btok[:, c * 128:(c + 1) * 128], p_[:, 0:128], [pk_], [("btok", c)])
            xtk = [("xtok", c) for c in range(8)]
            S.op("dve", lambda g: g.tensor_tensor(out=dtt[:, :], in0=big[:, 2048:2064], in1=dtb[:, :], op=ALU.add), bk(2048, 2064) + ["dtb"], ["dtt"])
            softplus_inplace(dtt[:, :], "dtt")
            if not full:
                S.op("dve", lambda g: g.tensor_scalar(dtt[:, :], dtt[:, :], valid[:, 0:1], None, ALU.mult), ["dtt", "valid"], ["dtt"])
                S.op("dve", lambda g: g.tensor_scalar(xtok[:, :], xtok[:, :], valid[:, 0:1], None, ALU.mult), xtk + ["valid"], xtk)
            S.op("dve", lambda g: g.tensor_tensor(out=av[:, :], in0=dtt[:, :], in1=alog[:, :], op=ALU.mult), ["dtt", "alog"], ["av"])
            decay_mats(av[:, :], "av")
            S.op("dve", lambda g: g.tensor_tensor(out=dec_w[:, :], in0=dec_w[:, :], in1=dtt[:, :], op=ALU.mult), ["dec_w", "dtt"], ["dec_w"])
            S.op("pool", lambda g: g.tensor_tensor(out=xw[:, :].rearrange("p (h d) -> p h d", d=64), in0=xtok[:, :].rearrange("p (h d) -> p h d", d=64),
                                                   in1=dec_w[:, :].unsqueeze(2).to_broadcast([128, 16, 64]), op=ALU.mult), xtk + ["dec_w"], ["xw"])
            for g_ in range(2):
                p_, pk_ = ps()
                S.op("pe", lambda g: g.matmul(p_[:, 0:128], xbcA[:, 8 + g_, :], xbcA[:, 10 + g_, :], start=True, stop=True),
                     [("xbcA", 8 + g_), ("xbcA", 10 + g_)], [pk_])
                S.op("dve", lambda g: g.tensor_tensor(out=cbm[:, g_, :], in0=p_[:, 0:128], in1=Umat[:, :], op=ALU.mult), [pk_, "Umat"], [("cbm", g_)])
            S.op("dve", lambda g: g.tensor_tensor(out=seg[:, :, :], in0=seg[:, :, :], in1=dtt[:, :].unsqueeze(2).to_broadcast([128, 16, 128]), op=ALU.mult),
                 SEGK + ["dtt", "dec_w"], SEGK)
            for g_ in range(2):
                S.op("pool", lambda g: g.tensor_tensor(out=seg[:, g_ * 8:(g_ + 1) * 8, :], in0=seg[:, g_ * 8:(g_ + 1) * 8, :],
                                                       in1=cbm[:, g_, :].unsqueeze(1).to_broadcast([128, 8, 128]), op=ALU.mult),
                     SEGK + [("cbm", g_)], SEGK)
            for g_ in range(2):
                p_, pk_ = ps()
                S.op("pe", lambda g: g.matmul(p_[:, 0:512], xbcA[:, 10 + g_, :], sst[:, g_ * 512:(g_ + 1) * 512], start=True, stop=True),
                     [("xbcA", 10 + g_), sk], [pk_])
                S.op("dve", lambda g: g.tensor_tensor(out=ysb[:, g_ * 512:(g_ + 1) * 512].rearrange("p (h d) -> p h d", d=64),
                                                      in0=p_[:, 0:512].rearrange("p (h d) -> p h d", d=64),
                                                      in1=csb[:, g_ * 8:(g_ + 1) * 8].unsqueeze(2).to_broadcast([128, 8, 64]), op=ALU.mult),
                     [pk_, "csb"], [("ysb", g_)])
            for g_ in range(2):
                p_, pk_ = ps()
                for hh in range(8):
                    h = g_ * 8 + hh
                    S.op("pe", lambda g: g.matmul(p_[:, hh * 64:(hh + 1) * 64], seg[:, h, :], xtok[:, h * 64:(h + 1) * 64], start=True, stop=True),
                         SEGK + xtk, [pk_])
                S.op("dve", lambda g: g.tensor_tensor(out=ysb[:, g_ * 512:(g_ + 1) * 512], in0=ysb[:, g_ * 512:(g_ + 1) * 512], in1=p_[:, 0:512], op=ALU.add),
                     [pk_, ("ysb", g_)], [("ysb", g_)])
            for g_ in range(2):
                p_, pk_ = ps()
                S.op("pe", lambda g: g.matmul(p_[:, 0:512], btok[:, g_ * 128:(g_ + 1) * 128], xw[:, g_ * 512:(g_ + 1) * 512], start=True, stop=True),
                     [("btok", g_), "xw"], [pk_])
                S.op("pool", lambda g: g.tensor_tensor(out=sst[:, g_ * 512:(g_ + 1) * 512].rearrange("p (h d) -> p h d", d=64),
                                                       in0=sst[:, g_ * 512:(g_ + 1) * 512].rearrange("p (h d) -> p h d", d=64),
                                                       in1=dec_b[:, g_ * 8:(g_ + 1) * 8].unsqueeze(2).to_broadcast([128, 8, 64]), op=ALU.mult),
                     [sk, "dec_b"], [sk])
                S.op("dve", lambda g: g.tensor_tensor(out=sst[:, g_ * 512:(g_ + 1) * 512], in0=sst[:, g_ * 512:(g_ + 1) * 512], in1=p_[:, 0:512], op=ALU.add),
                     [pk_, sk], [sk])
            yk = [("ysb", 0), ("ysb", 1)]
            S.op("pool", lambda g: g.tensor_tensor(out=xw[:, :].rearrange("p (h d) -> p h d", d=64), in0=xtok[:, :].rearrange("p (h d) -> p h d", d=64),
                                                   in1=dsk[:, :].unsqueeze(2).to_broadcast([128, 16, 64]), op=ALU.mult), xtk + ["dsk", "xw"], ["xw"])
            S.op("dve", lambda g: g.tensor_tensor(out=ysb[:, :], in0=ysb[:, :], in1=xw[:, :], op=ALU.add), yk + ["xw"], yk)
            S.op("act", lambda g: g.activation(out=big[:, 1024:2048], in_=big[:, 1024:2048], func=AF.Silu), bk(1024, 2048), bk(1024, 2048))
            S.op("dve", lambda g: g.tensor_tensor(out=ysb[:, :], in0=ysb[:, :], in1=big[:, 1024:2048], op=ALU.mult), yk + bk(1024, 2048), yk)
            load_bcast(gpost[:, :], "gpost", ssm_norm_w, i * SSM_DI, SSM_DI)
            snw = gpost
            for g_ in range(2):
                rmsnorm_stats(ysb[:, g_ * 512:(g_ + 1) * 512], [("ysb", g_)], 512, 2 + g_)
                S.op("dve", lambda g: g.scalar_tensor_tensor(out=mixed[:, 512 + g_ * 512:1024 + g_ * 512], in0=ysb[:, g_ * 512:(g_ + 1) * 512],
                                                             scalar=stat[:, 2 + g_:3 + g_], in1=snw[:, g_ * 512:(g_ + 1) * 512], op0=ALU.mult, op1=ALU.mult),
                     [("ysb", g_), ("stat", 2 + g_), "gpost"], [("mixed", 4 + 4 * g_ + c) for c in range(4)])
            transposes_to(mixT, "mixT", 4, mixed[:, 512:1536], [("mixed", 4 + c) for c in range(8)], 8)

            def ev_out(p_, pk_, c, cw):
                copy(evac_eng(), big[:, 3072 + c:3072 + c + cw], p_[:, 0:cw], [pk_], bk(3072 + c, 3072 + c + cw))
            proj_tok(w_hyb_out, i, HYB_MIX, D, 0, D, mixT, [("mixT", c) for c in range(12)], ev_out)
            post_norm_residual(norm_mix_post, l, big[:, 3072:4096], bk(3072, 4096))

        gdtb = dtb
        galog = alog
        gnw = sb("gnw", [128, 128])
        knT = xtok[:, :].rearrange("p (h d) -> p h d", d=128)
        QKT = xw[:, :].rearrange("p (h d) -> p h d", d=128)
        gs = []
        for par in range(2):
            base = ktw[:, :] if par == 0 else vw[:, :, :].rearrange("p a b -> p (a b)")
            dct = {}
            off = 0
            for nm, w in [("kb", 128), ("kbT", 128), ("MA", 128), ("MB", 128), ("MTA", 128), ("MTB", 128), ("XA", 256), ("XB", 256),
                          ("kcdT", 128), ("u", 128), ("qg", 128), ("qgT", 128), ("attnT", 128), ("kdec", 128)]:
                dct[nm] = base[:, off:off + w]
                off += w
            gs.append(dct)

        def gdn_load_params(i):
            load_bcast(gdtb[:, :], "dtb", gdn_dt_bias, i * 16, 16)
            load_bcast(galog[:, :], "alog", gdn_a_log, i * 16, 16)
            S.op("act", lambda g: g.activation(out=galog[:, :], in_=galog[:, :], func=AF.Exp), ["alog"], ["alog"])
            S.op("dve", lambda g: g.tensor_scalar(galog[:, :], galog[:, :], -1.0, None, ALU.mult), ["alog"], ["alog"])
            load_bcast(gnw[:, :], "gnw", gdn_norm_w, i * 128, 128)

        def gdn_layer(l, i, sq, tpos, nvalid):
            full = (nvalid == 128)
            cst = gconv_h[i]
            ck = "gconv_h%d" % i
            Sst = gdn_st[i]
            prenorm(0, l, False)
            S.op("pool", lambda g: g.memset(ktw[0:1, 0:1], 0.0), [], ["ktw"])
            S.op("pool", lambda g: g.memset(vw[0:1, 0:1, 0:1], 0.0), [], ["vw"])

            def ev_zba(p_, pk_, c, cw):
                copy(evac_eng(), big[:, c:c + cw], p_[:, 0:cw], [pk_], bk(c, c + cw))
            proj_tok(w_gdn_in, i, D, G_IN, 4096, G_IN, hT, hkeys, ev_zba)
            for grp in range(4):
                def ev_x(p_, pk_, j):
                    copy(evac_eng(), xbcT[:, j, 3:131], p_[:, 0:128], [pk_], [("xbcT", j)])
                proj_feat(w_gdn_in, i, D, G_IN, grp * 1024, 1024, hT, hkeys, ev_x)
                conv_silu(8, cwg_all[:, i, grp * 8:(grp + 1) * 8, :], "cwg_all", cst[:, grp * 8:(grp + 1) * 8, :], ck, nvalid, None)
                for j in range(8):
                    ct = grp * 8 + j
                    p_, pk_ = ps()
                    S.op("pe", lambda g: g.transpose(p_[:, 0:128], xbcA[:, j, :], ident[:]), [("xbcA", j), "ident"], [pk_])
                    copy(evac_eng(), big[:, ct * 128:(ct + 1) * 128], p_[:, 0:128], [pk_], bk(ct * 128, (ct + 1) * 128))
            S.op("pool", lambda g: g.tensor_tensor(out=Lb[:, 0:2048], in0=big[:, 0:2048], in1=big[:, 0:2048], op=ALU.mult), bk(0, 2048), ["Lb"])
            S.op("dve", lambda g: g.reduce_sum(out=rn[:, :], in_=Lb[:, 0:2048].rearrange("p (h d) -> p h d", d=128), axis=AX.X), ["Lb"], ["rn"])
            S.op("act", lambda g: g.activation(out=rn[:, :], in_=rn[:, :], func=AF.Ln, bias=c_eps, scale=1.0), ["rn", "ccol"], ["rn"])
            S.op("act", lambda g: g.activation(out=rn[:, :], in_=rn[:, :], func=AF.Exp, scale=-0.5), ["rn"], ["rn"])
            S.op("dve", lambda g: g.tensor_scalar(rn[:, 0:8], rn[:, 0:8], 128.0 ** -0.5, None, ALU.mult), ["rn"], ["rn"])
            if not full:
                S.op("dve", lambda g: g.tensor_scalar(rn[:, :], rn[:, :], valid[:, 0:1], None, ALU.mult), ["rn", "valid"], ["rn"])
                S.op("dve", lambda g: g.tensor_scalar(big[:, 2048:4096], big[:, 2048:4096], valid[:, 0:1], None, ALU.mult),
                     bk(2048, 4096) + ["valid"], bk(2048, 4096))
            S.op("dve", lambda g: g.tensor_tensor(out=big[:, 0:2048].rearrange("p (h d) -> p h d", d=128), in0=big[:, 0:2048].rearrange("p (h d) -> p h d", d=128),
                                                  in1=rn[:, :].unsqueeze(2).to_broadcast([128, 16, 128]), op=ALU.mult), bk(0, 2048) + ["rn"], bk(0, 2048))
            S.op("act", lambda g: g.activation(out=beta[:, :], in_=big[:, 6144:6160], func=AF.Exp, scale=-1.0), bk(6144, 6160), ["beta"])
            S.op("dve", lambda g: g.tensor_scalar(beta[:, :], beta[:, :], 1.0, None, ALU.add), ["beta"], ["beta"])
            S.op("dve", lambda g: g.reciprocal(beta[:, :], beta[:, :]), ["beta"], ["beta"])
            S.op("dve", lambda g: g.tensor_tensor(out=dtt[:, :], in0=big[:, 6160:6176], in1=gdtb[:, :], op=ALU.add), bk(6160, 6176) + ["dtb"], ["dtt"])
            softplus_inplace(dtt[:, :], "dtt")
            S.op("dve", lambda g: g.tensor_tensor(out=av[:, :], in0=dtt[:, :], in1=galog[:, :], op=ALU.mult), ["dtt", "alog"], ["av"])
            if not full:
                S.op("dve", lambda g: g.tensor_scalar(beta[:, :], beta[:, :], valid[:, 0:1], None, ALU.mult), ["beta", "valid"], ["beta"])
                S.op("dve", lambda g: g.tensor_scalar(av[:, :], av[:, :], valid[:, 0:1], None, ALU.mult), ["av", "valid"], ["av"])
            decay_mats(av[:, :], "av")
            S.op("pool", lambda g: g.tensor_tensor(out=aU[:, :, :], in0=seg[:, :, :], in1=SUmat[:, :].unsqueeze(1).to_broadcast([128, 16, 128]), op=ALU.mult),
                 SEGK + ["SUmat", "Lb"], ["Lb"])
            S.op("dve", lambda g: g.tensor_tensor(out=seg[:, :, :], in0=seg[:, :, :], in1=Umat[:, :].unsqueeze(1).to_broadcast([128, 16, 128]), op=ALU.mult),
                 SEGK + ["Umat", "dec_w", "Lb"], SEGK)
            for hq in range(8):
                p_, pk_ = ps()
                S.op("pe", lambda g: g.transpose(p_[:, 0:128], big[:, 1024 + hq * 128:1152 + hq * 128], ident[:]), bk(1024, 2048) + ["ident"], [pk_])
                copy(evac_eng(), knT[:, hq, :], p_[:, 0:128], [pk_], [("xtok", hq)])
                p_, pk_ = ps()
                S.op("pe", lambda g: g.transpose(p_[:, 0:128], big[:, hq * 128:(hq + 1) * 128], ident[:]), bk(0, 1024) + ["ident"], [pk_])
                copy(evac_eng(), stage[:, hq % 4, :], p_[:, 0:128], [pk_], [("stage", hq % 4)])
                p_, pk_ = ps()
                S.op("pe", lambda g: g.matmul(p_[:, 0:128], knT[:, hq, :], stage[:, hq % 4, :], start=True, stop=True), [("xtok", hq), ("stage", hq % 4)], [pk_])
                copy(evac_eng(), QKT[:, hq, :], p_[:, 0:128], [pk_], ["xw"])
            for h in range(16):
                hq = h // 2
                G = gs[h % 2]
                par = h % 2

                def K(nm):
                    return "g%s%d" % (nm, par)
                AL = "ktw" if par == 0 else "vw"

                def SO(e, fn, reads, writes, AL=AL):
                    S.op(e, fn, list(reads) + [AL], writes)

                def CP(e, out, in_, reads, writes, AL=AL):
                    copy(e, out, in_, list(reads) + [AL], writes)
                kcols = bk(1024 + hq * 128, 1152 + hq * 128)
                qcols = bk(hq * 128, (hq + 1) * 128)
                vcols = bk(2048 + h * 128, 2176 + h * 128)
                k_n = big[:, 1024 + hq * 128:1152 + hq * 128]
                q_n = big[:, hq * 128:(hq + 1) * 128]
                v_h = big[:, 2048 + h * 128:2176 + h * 128]
                SO("dve", lambda g: g.tensor_scalar(G["kb"][:, :], k_n, beta[:, h:h + 1], None, ALU.mult), kcols + ["beta"], [K("kb")])
                SO("dve", lambda g: g.tensor_scalar(G["XA"][:, 0:128], v_h, beta[:, h:h + 1], None, ALU.mult), vcols + ["beta"], [K("XA")])
                SO("dve", lambda g: g.tensor_scalar(G["XA"][:, 128:256], G["kb"][:, :], csb[:, h:h + 1], None, ALU.mult), [K("kb"), "csb", K("XA")], [K("XA")])
                p_, pk_ = ps()
                SO("pe", lambda g: g.transpose(p_[:, 0:128], G["kb"][:, :], ident[:]), [K("kb"), "ident"], [pk_])
                CP(evac_eng(), G["kbT"][:, :], p_[:, 0:128], [pk_], [K("kbT")])
                p_, pk_ = ps()
                SO("pe", lambda g: g.matmul(p_[:, 0:128], knT[:, hq, :], G["kbT"][:, :], start=True, stop=True), [("xtok", hq), K("kbT")], [pk_])
                SO("dve", lambda g: g.tensor_tensor(out=G["MTA"][:, :], in0=p_[:, 0:128], in1=aU[:, h, :], op=ALU.mult), [pk_, "Lb"], [K("MTA")])
                p_, pk_ = ps()
                SO("pe", lambda g: g.transpose(p_[:, 0:128], G["MTA"][:, :], ident[:]), [K("MTA"), "ident"], [pk_])
                CP(evac_eng(), G["MA"][:, :], p_[:, 0:128], [pk_], [K("MA")])
                p_, pk_ = ps()
                SO("pe", lambda g: g.matmul(p_[:, 0:256], G["MTA"][:, :], G["XA"][:, :], start=True, stop=True), [K("MTA"), K("XA")], [pk_])
                SO("dve", lambda g: g.tensor_tensor(out=G["XB"][:, :], in0=G["XA"][:, :], in1=p_[:, 0:256], op=ALU.subtract), [pk_, K("XA")], [K("XB")])
                Xc, Xn = "XB", "XA"
                Mc, Mn, MTc, MTn = "MA", "MB", "MTA", "MTB"
                for lvl in range(1, 7):
                    p_, pk_ = ps()
                    SO("pe", lambda g: g.matmul(p_[:, 0:128], G[Mc][:, :], G[MTc][:, :], start=True, stop=True), [K(Mc), K(MTc)], [pk_])
                    CP(evac_eng(), G[MTn][:, :], p_[:, 0:128], [pk_], [K(MTn)])
                    if lvl < 6:
                        p_, pk_ = ps()
                        SO("pe", lambda g: g.matmul(p_[:, 0:128], G[MTc][:, :], G[Mc][:, :], start=True, stop=True), [K(Mc), K(MTc)], [pk_])
                        CP(evac_eng(), G[Mn][:, :], p_[:, 0:128], [pk_], [K(Mn)])
                    p_, pk_ = ps()
                    SO("pe", lambda g: g.matmul(p_[:, 0:256], G[MTn][:, :], G[Xc][:, :], start=True, stop=True), [K(MTn), K(Xc)], [pk_])
                    SO("dve", lambda g: g.tensor_tensor(out=G[Xn][:, :], in0=G[Xc][:, :], in1=p_[:, 0:256], op=ALU.add), [pk_, K(Xc)], [K(Xn)])
                    Xc, Xn = Xn, Xc
                    Mc, Mn = Mn, Mc
                    MTc, MTn = MTn, MTc
                X = G[Xc]
                p_, pk_ = ps()
                SO("pe", lambda g: g.transpose(p_[:, 0:128], X[:, 128:256], ident[:]), [K(Xc), "ident"], [pk_])
                CP(evac_eng(), G["kcdT"][:, :], p_[:, 0:128], [pk_], [K("kcdT")])
                stk = ("gdn_st", i, h)
                p_, pk_ = ps()
                SO("pe", lambda g: g.matmul(p_[:, 0:128], G["kcdT"][:, :], Sst[:, h, :], start=True, stop=True), [K("kcdT"), stk], [pk_])
                SO("dve", lambda g: g.tensor_tensor(out=G["u"][:, :], in0=X[:, 0:128], in1=p_[:, 0:128], op=ALU.subtract), [pk_, K(Xc)], [K("u")])
                SO("dve", lambda g: g.tensor_scalar(G["qg"][:, :], q_n, csb[:, h:h + 1], None, ALU.mult), qcols + ["csb"], [K("qg")])
                p_, pk_ = ps()
                SO("pe", lambda g: g.transpose(p_[:, 0:128], G["qg"][:, :], ident[:]), [K("qg"), "ident"], [pk_])
                CP(evac_eng(), G["qgT"][:, :], p_[:, 0:128], [pk_], [K("qgT")])
                SO("pool", lambda g: g.tensor_tensor(out=G["attnT"][:, :], in0=QKT[:, hq, :], in1=seg[:, h, :], op=ALU.mult),
                     ["xw"] + SEGK, [K("attnT")])
                p_, pk_ = ps()
                SO("pe", lambda g: g.matmul(p_[:, 0:128], G["qgT"][:, :], Sst[:, h, :], start=True, stop=False), [K("qgT"), stk], [pk_])
                SO("pe", lambda g: g.matmul(p_[:, 0:128], G["attnT"][:, :], G["u"][:, :], start=False, stop=True), [K("attnT"), K("u")], [pk_])
                CP(evac_eng(), mixed[:, h * 128:(h + 1) * 128], p_[:, 0:128], [pk_], [("mixed", h)])
                SO("dve", lambda g: g.tensor_scalar(G["kdec"][:, :], k_n, dec_w[:, h:h + 1], None, ALU.mult), kcols + ["dec_w"], [K("kdec")])
                p_, pk_ = ps()
                SO("pe", lambda g: g.matmul(p_[:, 0:128], G["kdec"][:, :], G["u"][:, :], start=True, stop=True), [K("kdec"), K("u")], [pk_])
                SO("dve", lambda g: g.scalar_tensor_tensor(out=Sst[:, h, :], in0=Sst[:, h, :], scalar=dec_b[:, h:h + 1], in1=p_[:, 0:128],
                                                             op0=ALU.mult, op1=ALU.add), [pk_, stk, "dec_b"], [stk])
            mk = [("mixed", h) for h in range(16)]
            S.op("pool", lambda g: g.tensor_tensor(out=Lb[:, 0:2048], in0=mixed[:, :], in1=mixed[:, :], op=ALU.mult), mk, ["Lb"])
            S.op("dve", lambda g: g.reduce_sum(out=rn[:, :], in_=Lb[:, 0:2048].rearrange("p (h d) -> p h d", d=128), axis=AX.X), ["Lb"], ["rn"])
            S.op("act", lambda g: g.activation(out=rn[:, :], in_=rn[:, :], func=AF.Ln, bias=c_eps, scale=1.0 / 128), ["rn", "ccol"], ["rn"])
            S.op("act", lambda g: g.activation(out=rn[:, :], in_=rn[:, :], func=AF.Exp, scale=-0.5), ["rn"], ["rn"])
            S.op("dve", lambda g: g.tensor_tensor(out=mixed[:, :].rearrange("p (h d) -> p h d", d=128), in0=mixed[:, :].rearrange("p (h d) -> p h d", d=128),
                                                  in1=rn[:, :].unsqueeze(2).to_broadcast([128, 16, 128]), op=ALU.mult), mk + ["rn"], mk)
            S.op("pool", lambda g: g.tensor_tensor(out=mixed[:, :].rearrange("p (h d) -> p h d", d=128), in0=mixed[:, :].rearrange("p (h d) -> p h d", d=128),
                                                   in1=gnw[:, :].unsqueeze(1).to_broadcast([128, 16, 128]), op=ALU.mult), mk + ["gnw"], mk)
            S.op("act", lambda g: g.activation(out=big[:, 4096:6144], in_=big[:, 4096:6144], func=AF.Silu), bk(4096, 6144), bk(4096, 6144))
            S.op("dve", lambda g: g.tensor_tensor(out=mixed[:, :], in0=mixed[:, :], in1=big[:, 4096:6144], op=ALU.mult), mk + bk(4096, 6144), mk)
            transposes_to(mixT, "mixT", 0, mixed, mk, 16)

            def ev_out(p_, pk_, c, cw):
                copy(evac_eng(), big[:, c:c + cw], p_[:, 0:cw], [pk_], bk(c, c + cw))
            proj_tok(w_gdn_out, i, G_VW, D, 0, D, mixT, [("mixT", c) for c in range(16)], ev_out)
            post_norm_residual(norm_mix_post, l, big[:, 0:1024], bk(0, 1024))

        def conv_state_load(dst, dkey, nct, src_tensor, off, width):
            S.dma("sp", tm3[0:3, 0:width], bass.AP(src_tensor, off, [[width, 3], [1, width]]), [], BIGK)
            for j in range(nct):
                p_, pk_ = ps()
                S.op("pe", lambda g: g.transpose(p_[:, 0:3], tm3[0:3, j * 128:(j + 1) * 128], ident[0:3, 0:3]), BIGK + ["ident"], [pk_])
                copy("dve", dst[:, j, :], p_[:, 0:3], [pk_], [dkey])

        def conv_state_store(src, skey, nct, dst_tensor, off, width):
            for j in range(nct):
                p_, pk_ = ps()
                S.op("pe", lambda g: g.transpose(p_[0:3, 0:128], src[:, j, :], ident[:]), [skey, "ident"], [pk_])
                copy("dve", tm3[0:3, j * 128:(j + 1) * 128], p_[0:3, 0:128], [pk_], BIGK)
            S.dma("sp", bass.AP(dst_tensor, off, [[width, 3], [1, width]]), tm3[0:3, 0:width], BIGK, [("out", dst_tensor.name)])

        def ssm_state_load(i, src_tensor, off):
            S.dma("sp", Lb[:, 0:1024].rearrange("p (b n) -> p b n", n=128), bass.AP(src_tensor, off, [[128, 128], [128 * 128, 8], [1, 128]]), [], ["Lb"])
            for b in range(8):
                p_, pk_ = ps()
                S.op("pe", lambda g: g.transpose(p_[:, 0:128], Lb[:, b * 128:(b + 1) * 128], ident[:]), ["Lb", "ident"], [pk_])
                copy(evac_eng(), ssm_st[i][:, b * 128:(b + 1) * 128], p_[:, 0:128], [pk_], ["ssm_st%d" % i])

        def ssm_state_store(i, dst_tensor, off):
            for b in range(8):
                p_, pk_ = ps()
                S.op("pe", lambda g: g.transpose(p_[:, 0:128], ssm_st[i][:, b * 128:(b + 1) * 128], ident[:]), ["ssm_st%d" % i, "ident"], [pk_])
                copy(evac_eng(), Lb[:, b * 128:(b + 1) * 128], p_[:, 0:128], [pk_], ["Lb"])
            S.dma("sp", bass.AP(dst_tensor, off, [[128, 128], [128 * 128, 8], [1, 128]]), Lb[:, 0:1024].rearrange("p (b n) -> p b n", n=128),
                  ["Lb"], [("out", dst_tensor.name)])

        def flat_copy(dst_tensor, doff, src_tensor, soff, nelem, rkeys, wkeys):
            assert nelem % 128 == 0
            per = nelem // 128
            S.dma("sp", bass.AP(dst_tensor, doff, [[per, 128], [1, per]]), bass.AP(src_tensor, soff, [[per, 128], [1, per]]), rkeys, wkeys)

        def seq_begin(sq, kind, sidx):
            for i in range(NHYB):
                ck = "conv_h%d" % i
                if kind == "p":
                    S.op("pool", lambda g: g.memset(conv_h[i][:, :, :], 0.0), [], [ck])
                    S.op("pool", lambda g: g.memset(ssm_st[i][:, :], 0.0), [], ["ssm_st%d" % i])
                else:
                    conv_state_load(conv_h[i], ck, 12, st_sconv, (i * NS1 + sidx) * 3 * SSM_XBC, SSM_XBC)
                    ssm_state_load(i, st_ssm, (i * NS1 + sidx) * 1024 * 128)
                    flat_copy(k_scr, (i * NSEQ + sq) * SCR_ROWS * A_W, cache_k, (i * NS1 + sidx) * WIN * A_W, WIN * A_W, [], [("k_scr", i, sq)])
                    flat_copy(v_scr, (i * NSEQ + sq) * SCR_ROWS * A_W, cache_v, (i * NS1 + sidx) * WIN * A_W, WIN * A_W, [], [("v_scr", i, sq)])
                    scr_base = (i * NSEQ + sq) * 128 * 4 * SCR_ROWS
                    for t in range(16):
                        S.dma("sp", Lb[:, 0:512], bass.AP(cache_k, ((i * NS1 + sidx) * WIN + t * 128) * A_W, [[A_W, 128], [1, A_W]]), [], ["Lb"])
                        for pr in range(4):
                            p_, pk_ = ps()
                            S.op("pe", lambda g: g.transpose(p_[:, 0:128], Lb[:, pr * 128:(pr + 1) * 128], ident[:]), ["Lb", "ident"], [pk_])
                            copy(evac_eng(), stage[:, pr, :], p_[:, 0:128], [pk_], [("stage", pr)])
                        S.dma("sp", bass.AP(kt_scr, scr_base + t * 128, [[4 * SCR_ROWS, 128], [SCR_ROWS, 4], [1, 128]]), stage[:, :, :],
                              [("stage", pr) for pr in range(4)], [("kt_scr", i, sq)])
            for i in range(NGDN):
                ck = "gconv_h%d" % i
                if kind == "p":
                    S.op("pool", lambda g: g.memset(gconv_h[i][:, :, :], 0.0), [], [ck])
                    S.op("pool", lambda g: g.memset(gdn_st[i][:, :, :], 0.0), [], [("gdn_st", i, h) for h in range(16)])
                else:
                    conv_state_load(gconv_h[i], ck, 32, st_gconv, (i * NS1 + sidx) * 3 * G_QKV, G_QKV)
                    S.dma("sp", gdn_st[i][:, :, :], bass.AP(st_gdn, (i * NS1 + sidx) * 16 * 128 * 128, [[128, 128], [128 * 128, 16], [1, 128]]),
                          [], [("gdn_st", i, h) for h in range(16)])

        def seq_end(sq, kind, sidx):
            for i in range(NHYB):
                ck = "conv_h%d" % i
                if kind == "p":
                    conv_state_store(conv_h[i], ck, 12, o_psc, i * 3 * SSM_XBC, SSM_XBC)
                    ssm_state_store(i, o_pss, i * 1024 * 128)
                    flat_copy(o_pk, i * KEEP * A_W, k_scr, ((i * NSEQ + sq) * SCR_ROWS + SEQ - KEEP) * A_W, KEEP * A_W, [("k_scr", i, sq)], [("out", "pk", i)])
                    flat_copy(o_pv, i * KEEP * A_W, v_scr, ((i * NSEQ + sq) * SCR_ROWS + SEQ - KEEP) * A_W, KEEP * A_W, [("v_scr", i, sq)], [("out", "pv", i)])
                else:
                    conv_state_store(conv_h[i], ck, 12, o_ssc, (i * NS1 + sidx) * 3 * SSM_XBC, SSM_XBC)
                    ssm_state_store(i, o_sss, (i * NS1 + sidx) * 1024 * 128)
                    flat_copy(o_sk, (i * NS1 + sidx) * WIN * A_W, k_scr, ((i * NSEQ + sq) * SCR_ROWS + 1) * A_W, WIN * A_W, [("k_scr", i, sq)], [("out", "sk", i, sidx)])
                    flat_copy(o_sv, (i * NS1 + sidx) * WIN * A_W, v_scr, ((i * NSEQ + sq) * SCR_ROWS + 1) * A_W, WIN * A_W, [("v_scr", i, sq)], [("out", "sv", i, sidx)])
            for i in range(NGDN):
                ck = "gconv_h%d" % i
                stk = [("gdn_st", i, h) for h in range(16)]
                if kind == "p":
                    conv_state_store(gconv_h[i], ck, 32, o_pgc, i * 3 * G_QKV, G_QKV)
                    S.dma("sp", bass.AP(o_pgs, i * 16 * 128 * 128, [[128, 128], [128 * 128, 16], [1, 128]]), gdn_st[i][:, :, :], stk, [("out", "pgs", i)])
                else:
                    conv_state_store(gconv_h[i], ck, 32, o_sgc, (i * NS1 + sidx) * 3 * G_QKV, G_QKV)
                    S.dma("sp", bass.AP(o_sgs, (i * NS1 + sidx) * 16 * 128 * 128, [[128, 128], [128 * 128, 16], [1, 128]]), gdn_st[i][:, :, :], stk,
                          [("out", "sgs", i, sidx)])

        for sq, (kind, sidx, t0, ntl) in enumerate(seqs):
            nvalid = 128 if kind == "p" else 1
            if kind == "s":
                S.op("pool", lambda g: g.memset(valid[:, :], 0.0), [], ["valid"])
                S.op("pool", lambda g: g.memset(valid[0:1, :], 1.0), ["valid"], ["valid"])
            seq_begin(sq, kind, sidx)
            for tl in range(ntl):
                tpos = t0 + tl
                if kind == "p":
                    S.dma("sp", xres[:, :], x_prompt[tl * 128:(tl + 1) * 128, :], [], ["xres"])
                else:
                    S.op("pool", lambda g: g.memset(xres[:, :], 0.0), [], ["xres"])
                    S.dma("sp", xres[0:1, :], x_sample[sidx:sidx + 1, :], ["xres"], ["xres"])
                for l in range(depth):
                    i = l // 2
                    if l % 2 == 0:
                        hybrid_load_params(i)
                        hybrid_layer(l, i, sq, tpos, nvalid)
                    else:
                        gdn_load_params(i)
                        gdn_layer(l, i, sq, tpos, nvalid)
                    ffn(l)
                if kind == "p":
                    S.dma("sp", y_prompt[tl * 128:(tl + 1) * 128, :], xres[:, :], ["xres"], ["y_prompt"])
                else:
                    S.dma("sp", y_sample[sidx:sidx + 1, :], xres[0:1, :], ["xres"], ["y_sample"])
            seq_end(sq, kind, sidx)

        for slot in S.dma_sems:
            if slot[1] > 0:
                S._wait("sp", (slot[0], slot[1]))
        for e in S.eng:
            if e != "sp" and S.cnt[e] > 0:
                S._wait("sp", (S.sem[e], S.cnt[e]))
        print("instructions:", S.ninstr, "waits:", S.nwait, "sems:", S.nsem)
    return nc


_NC_CACHE = {}


def kernel(x_prompt, x_sample, cache_attn_k, cache_attn_v, state_ssm_conv, state_ssm, state_gdn_conv, state_gdn,
           rel_bias, norm_mix_pre, norm_mix_post, norm_ffn_pre, norm_ffn_post, w_hyb_in, ssm_conv_w, ssm_conv_b,
           ssm_dt_bias, ssm_a_log, ssm_d, ssm_norm_w, w_hyb_out, w_gdn_in, gdn_conv_w, gdn_dt_bias, gdn_a_log,
           gdn_norm_w, w_gdn_out, w_ffn_gate, w_ffn_up, w_ffn_down):
    f = lambda a: np.ascontiguousarray(np.asarray(a, dtype=np.float32))
    x_prompt = f(x_prompt)
    B, SEQ, _ = x_prompt.shape
    x_sample = f(x_sample)
    DB = x_sample.shape[0]
    depth = np.asarray(norm_mix_pre).shape[0]
    n_ptiles = SEQ // 128
    assert DB % NCORES == 0
    n_samp = DB // NCORES
    key = (n_ptiles, n_samp, depth)
    if key not in _NC_CACHE:
        _NC_CACHE[key] = build_nc(n_ptiles, n_samp, depth=depth)
    nc = _NC_CACHE[key]
    NHYB = (depth + 1) // 2
    NGDN = depth // 2
    shared = dict(
        rel_bias=f(rel_bias), oh_tab=attn_tables(), norm_mix_pre=f(norm_mix_pre), norm_mix_post=f(norm_mix_post),
        norm_ffn_pre=f(norm_ffn_pre), norm_ffn_post=f(norm_ffn_post), w_hyb_in=f(w_hyb_in), ssm_conv_w=f(ssm_conv_w),
        ssm_conv_b=f(ssm_conv_b), ssm_dt_bias=f(ssm_dt_bias), ssm_a_log=f(ssm_a_log), ssm_d=f(ssm_d), ssm_norm_w=f(ssm_norm_w),
        w_hyb_out=f(w_hyb_out), w_gdn_in=f(w_gdn_in), gdn_conv_w=f(gdn_conv_w), gdn_dt_bias=f(gdn_dt_bias),
        gdn_a_log=f(gdn_a_log), gdn_norm_w=f(gdn_norm_w), w_gdn_out=f(w_gdn_out), w_ffn_gate=f(w_ffn_gate),
        w_ffn_up=f(w_ffn_up), w_ffn_down=f(w_ffn_down))
    ck = f(cache_attn_k).reshape(NHYB, DB, WIN, A_W)
    cv = f(cache_attn_v).reshape(NHYB, DB, WIN, A_W)
    sc = f(state_ssm_conv)
    ss = f(state_ssm).reshape(NHYB, DB, 1024, 128)
    gc = f(state_gdn_conv)
    gst = f(state_gdn)
    in_maps = []
    for c in range(NCORES):
        sl = slice(c * n_samp, (c + 1) * n_samp)
        m = dict(shared)
        m["x_prompt"] = np.ascontiguousarray(x_prompt[c % B])
        m["x_sample"] = np.ascontiguousarray(x_sample[sl, 0, :])
        m["cache_k"] = np.ascontiguousarray(ck[:, sl])
        m["cache_v"] = np.ascontiguousarray(cv[:, sl])
        m["st_sconv"] = np.ascontiguousarray(sc[:, sl])
        m["st_ssm"] = np.ascontiguousarray(ss[:, sl])
        m["st_gconv"] = np.ascontiguousarray(gc[:, sl])
        m["st_gdn"] = np.ascontiguousarray(gst[:, sl])
        in_maps.append(m)
    res = run_bass_kernel_spmd(nc, in_maps, core_ids=list(range(NCORES)))
    R = res.results
    KEEP = min(WIN, SEQ)
    pc = list(range(B))
    y_prompt = np.stack([R[c]["y_prompt"] for c in pc], 0)
    y_sample = np.concatenate([R[c]["y_sample"] for c in range(NCORES)], 0)[:, None, :]
    pk = np.stack([R[c]["o_pk"] for c in pc], 1).reshape(NHYB, B, KEEP, 8, 64)
    pv = np.stack([R[c]["o_pv"] for c in pc], 1).reshape(NHYB, B, KEEP, 8, 64)
    psc = np.stack([R[c]["o_psc"] for c in pc], 1)
    pss = np.stack([R[c]["o_pss"] for c in pc], 1).reshape(NHYB, B, 16, 64, 128)
    pgc = np.stack([R[c]["o_pgc"] for c in pc], 1)
    pgs = np.stack([R[c]["o_pgs"] for c in pc], 1)
    cat = lambda nm: np.concatenate([R[c][nm] for c in range(NCORES)], 1)
    sk = cat("o_sk").reshape(NHYB, DB, WIN, 8, 64)
    sv = cat("o_sv").reshape(NHYB, DB, WIN, 8, 64)
    ssc = cat("o_ssc")
    sss = cat("o_sss").reshape(NHYB, DB, 16, 64, 128)
    sgc = cat("o_sgc")
    sgs = cat("o_sgs")
    return (y_prompt, y_sample, pk, pv, psc, pss, pgc, pgs, sk, sv, ssc, sss, sgc, sgs)
```

```python
import math
from contextlib import ExitStack
import numpy as np
import concourse.bass as bass
import concourse.mybir as mybir
from concourse.bass_utils import run_bass_kernel_spmd

F32 = mybir.dt.float32
F32R = mybir.dt.float32r
AF = mybir.ActivationFunctionType
ALU = mybir.AluOpType
AX = mybir.AxisListType

D = 1024
EPS = 1e-6
A_W = 512
WIN = 2048
NKT = 17
SSM_DI = 1024
SSM_XBC = 1536
HYB_IN = 4112
HYB_MIX = 1536
G_VW = 2048
G_QKV = 4096
G_IN = 6176
D_FF = 2816
NEG = -30000.0
NCORES = 8
TABL = 2304


def rel_buckets(dist):
    max_exact = 16
    n = np.maximum(dist, 1).astype(np.float32)
    large = max_exact + (np.log(n / max_exact) / math.log(2048 / max_exact) * (32 - max_exact)).astype(np.int32)
    large = np.minimum(large, 31)
    return np.where(dist < max_exact, dist, large).astype(np.int32)


def attn_tables():
    dist = 2175 - np.arange(TABL)
    valid = (dist >= 0) & (dist <= 2048)
    dc = np.clip(dist, 0, 2048)
    cnt = ((dc <= 128).astype(np.float64) + ((dc % 4 == 0) & (dc <= 512)) + ((dc % 16 == 0) & (dc <= 2048)))
    cnt = np.where(valid, cnt, 0.0)
    logc = np.where(cnt > 0, np.log(np.maximum(cnt, 1e-9)), NEG).astype(np.float32)
    bk = rel_buckets(dc)
    oh = np.zeros((33, TABL), np.float32)
    oh[bk, np.arange(TABL)] = np.where(cnt > 0, 1.0, 0.0)
    oh[32, :] = logc
    return oh


class Sched:
    def __init__(self, nc, es):
        self.nc = nc
        self.es = es
        self.eng = {"pe": nc.tensor, "act": nc.scalar, "dve": nc.vector, "pool": nc.gpsimd, "sp": nc.sync}
        self.sem = {}
        self.cnt = {}
        self.nsem = 0
        self.pe_sems = set()
        for e in self.eng:
            self._new_sem(e)
        self.dma_sems = []
        for i in range(48):
            self.dma_sems.append([es.enter_context(nc.semaphore("dq%d" % i)), 0])
        self.dma_rr = 0
        self.waited = {e: {} for e in self.eng}
        self.lastw = {}
        self.readers = {}
        self.ninstr = 0
        self.nwait = 0

    def _new_sem(self, e):
        self.sem[e] = self.es.enter_context(self.nc.semaphore("s_%s_%d" % (e, self.nsem)))
        if e == "pe":
            self.pe_sems.add(id(self.sem[e]))
        self.nsem += 1
        self.cnt[e] = 0

    def _wait(self, e, dep):
        sem, val = dep
        if e == "pe" and id(sem) in self.pe_sems:
            return
        w = self.waited[e]
        k = id(sem)
        if w.get(k, 0) >= val:
            return
        w[k] = val
        self.eng[e].wait_ge(sem, val)
        self.nwait += 1

    def _deps(self, e, reads, writes):
        for k in reads:
            d = self.lastw.get(k)
            if d is not None:
                self._wait(e, d)
        for k in writes:
            d = self.lastw.get(k)
            if d is not None:
                self._wait(e, d)
            r = self.readers.get(k)
            if r:
                for d in r.values():
                    self._wait(e, d)

    def _commit(self, tok, reads, writes):
        for k in writes:
            self.lastw[k] = tok
            self.readers[k] = {}
        for k in reads:
            r = self.readers.setdefault(k, {})
            r[id(tok[0])] = tok

    def op(self, e, fn, reads=(), writes=()):
        self._deps(e, reads, writes)
        if self.cnt[e] >= 30000:
            self._new_sem(e)
        inst = fn(self.eng[e])
        inst.then_inc(self.sem[e], 1)
        self.cnt[e] += 1
        self.ninstr += 1
        tok = (self.sem[e], self.cnt[e])
        self._commit(tok, reads, writes)

    def dma(self, e, out, in_, reads=(), writes=(), slow=False):
        self._deps(e, reads, writes)
        slot = self.dma_sems[self.dma_rr]
        self.dma_rr = (self.dma_rr + 1) % len(self.dma_sems)
        if slot[1] > 0:
            self._wait(e, (slot[0], slot[1]))
        if slot[1] >= 30000:
            slot[0] = self.es.enter_context(self.nc.semaphore("dqx%d" % self.nsem))
            self.nsem += 1
            slot[1] = 0
        if slow:
            self.eng[e].dma_start(out=out, in_=in_, allow_slow_non_contiguous=True).then_inc(slot[0], 16)
        else:
            self.eng[e].dma_start(out=out, in_=in_).then_inc(slot[0], 16)
        slot[1] += 16
        self.ninstr += 1
        tok = (slot[0], slot[1])
        self._commit(tok, reads, writes)


def build_nc(n_ptiles, n_samp, depth=4, mm_r=True, dbg=None):
    nc = bass.Bass("TRN2", target_bir_lowering=False)
    SEQ = n_ptiles * 128
    NHYB = (depth + 1) // 2
    NGDN = depth // 2
    NG1 = max(NGDN, 1)
    NS1 = max(n_samp, 1)
    KEEP = min(WIN, SEQ)
    MMDT = F32R if mm_r else F32
    wq = "pool" if mm_r else "sp"

    def din(name, shape):
        return nc.dram_tensor(name, list(shape), F32, kind="ExternalInput")

    def dout(name, shape):
        return nc.dram_tensor(name, list(shape), F32, kind="ExternalOutput")

    def dscr(name, shape):
        return nc.dram_tensor(name, list(shape), F32, kind="Internal")

    x_prompt = din("x_prompt", [SEQ, D])
    x_sample = din("x_sample", [NS1, D])
    cache_k = din("cache_k", [NHYB, NS1, WIN, A_W])
    cache_v = din("cache_v", [NHYB, NS1, WIN, A_W])
    st_sconv = din("st_sconv", [NHYB, NS1, 3, SSM_XBC])
    st_ssm = din("st_ssm", [NHYB, NS1, 1024, 128])
    st_gconv = din("st_gconv", [NG1, NS1, 3, G_QKV])
    st_gdn = din("st_gdn", [NG1, NS1, 16, 128, 128])
    rel_bias = din("rel_bias", [32, 8])
    oh_tab = din("oh_tab", [33, TABL])
    norm_mix_pre = din("norm_mix_pre", [depth, D])
    norm_mix_post = din("norm_mix_post", [depth, D])
    norm_ffn_pre = din("norm_ffn_pre", [depth, D])
    norm_ffn_post = din("norm_ffn_post", [depth, D])
    w_hyb_in = din("w_hyb_in", [NHYB, D, HYB_IN])
    ssm_conv_w = din("ssm_conv_w", [NHYB, 4, SSM_XBC])
    ssm_conv_b = din("ssm_conv_b", [NHYB, SSM_XBC])
    ssm_dt_bias = din("ssm_dt_bias", [NHYB, 16])
    ssm_a_log = din("ssm_a_log", [NHYB, 16])
    ssm_d = din("ssm_d", [NHYB, 16])
    ssm_norm_w = din("ssm_norm_w", [NHYB, SSM_DI])
    w_hyb_out = din("w_hyb_out", [NHYB, HYB_MIX, D])
    w_gdn_in = din("w_gdn_in", [NG1, D, G_IN])
    gdn_conv_w = din("gdn_conv_w", [NG1, 4, G_QKV])
    gdn_dt_bias = din("gdn_dt_bias", [NG1, 16])
    gdn_a_log = din("gdn_a_log", [NG1, 16])
    gdn_norm_w = din("gdn_norm_w", [NG1, 128])
    w_gdn_out = din("w_gdn_out", [NG1, G_VW, D])
    w_ffn_gate = din("w_ffn_gate", [depth, D, D_FF])
    w_ffn_up = din("w_ffn_up", [depth, D, D_FF])
    w_ffn_down = din("w_ffn_down", [depth, D_FF, D])

    y_prompt = dout("y_prompt", [SEQ, D])
    y_sample = dout("y_sample", [NS1, D])
    o_pk = dout("o_pk", [NHYB, KEEP, A_W])
    o_pv = dout("o_pv", [NHYB, KEEP, A_W])
    o_psc = dout("o_psc", [NHYB, 3, SSM_XBC])
    o_pss = dout("o_pss", [NHYB, 1024, 128])
    o_pgc = dout("o_pgc", [NG1, 3, G_QKV])
    o_pgs = dout("o_pgs", [NG1, 16, 128, 128])
    o_sk = dout("o_sk", [NHYB, NS1, WIN, A_W])
    o_sv = dout("o_sv", [NHYB, NS1, WIN, A_W])
    o_ssc = dout("o_ssc", [NHYB, NS1, 3, SSM_XBC])
    o_sss = dout("o_sss", [NHYB, NS1, 1024, 128])
    o_sgc = dout("o_sgc", [NG1, NS1, 3, G_QKV])
    o_sgs = dout("o_sgs", [NG1, NS1, 16, 128, 128])
    dbg_out = {}
    if dbg:
        for nm, shp in dbg.items():
            dbg_out[nm] = dout("dbg_" + nm, shp)

    seqs = [("p", 0, 0, n_ptiles)] + [("s", s, 16, 1) for s in range(n_samp)]
    NSEQ = len(seqs)
    SCR_ROWS = max(SEQ, WIN + 128)
    kt_scr = dscr("kt_scr", [NHYB, NSEQ, 128, 4, SCR_ROWS])
    k_scr = dscr("k_scr", [NHYB, NSEQ, SCR_ROWS, A_W])
    v_scr = dscr("v_scr", [NHYB, NSEQ, SCR_ROWS, A_W])
    ftab = dscr("ftab", [8, TABL])

    es = ExitStack()
    with es:
        S = Sched(nc, es)

        def sb(name, shape, dt=F32):
            return es.enter_context(nc.sbuf_tensor(name, list(shape), dt))

        psum = [es.enter_context(nc.psum_tensor("ps%d" % i, [128, 512], F32)) for i in range(8)]
        ps_rr = [0]

        def ps():
            i = ps_rr[0]
            ps_rr[0] = (i + 1) % 6
            return psum[i], "ps%d" % i

        acc_rr = [0]

        def ps_acc():
            acc_rr[0] ^= 1
            i = 6 + acc_rr[0]
            return psum[i], "ps%d" % i

        ev_rr = [0]

        def evac_eng():
            ev_rr[0] ^= 1
            return "act" if ev_rr[0] else "dve"

        def copy(e, out, in_, reads, writes):
            if e == "act":
                S.op("act", lambda g: g.activation(out=out, in_=in_, func=AF.Copy), reads, writes)
            else:
                S.op(e, lambda g: g.tensor_copy(out, in_), reads, writes)

        def dump(name, ap, keys):
            if name in dbg_out:
                t = dbg_out[name]
                S.dma("sp", t.ap() if hasattr(t, "ap") else t[:], ap, keys, ["dbg_" + name])

        def mask_const(name, pattern, op, base, cm):
            t = sb(name, [128, 128])
            S.op("pool", lambda g: g.memset(t[:], 1.0), [], [name])
            S.op("pool", lambda g: g.affine_select(out=t[:], in_=t[:], pattern=pattern, compare_op=op, fill=0.0,
                                                   base=base, channel_multiplier=cm), [name], [name])
            return t

        ident = mask_const("ident", [[-1, 128]], ALU.is_equal, 0, 1)
        antiI = mask_const("antiI", [[1, 128]], ALU.is_equal, -127, 1)
        Umat = mask_const("Umat", [[1, 128]], ALU.is_ge, 0, -1)
        SUmat = mask_const("SUmat", [[1, 128]], ALU.is_gt, 0, -1)
        SLmat = mask_const("SLmat", [[-1, 128]], ALU.is_gt, 0, 1)
        ones = sb("ones", [128, 128])
        S.op("pool", lambda g: g.memset(ones[:], 1.0), [], ["ones"])
        ccol = sb("ccol", [128, 4])
        S.op("pool", lambda g: g.memset(ccol[:, 0:1], EPS), [], ["ccol"])
        S.op("pool", lambda g: g.memset(ccol[:, 1:2], 1.0), ["ccol"], ["ccol"])
        S.op("pool", lambda g: g.memset(ccol[:, 2:3], 0.0), ["ccol"], ["ccol"])
        c_eps = ccol[:, 0:1]
        c_one = ccol[:, 1:2]
        CONSTS = ["ident", "antiI", "Umat", "SUmat", "SLmat", "ones", "ccol"]

        xres = sb("xres", [128, D])
        hT = sb("hT", [128, 8, 128], MMDT)
        xn = sb("xn", [128, D])
        stat = sb("stat", [128, 8])
        gpost = sb("gpost", [128, D])
        WBUF = 4096
        NWB = 2
        wbuf = [sb("wbuf%d" % i, [128, WBUF], MMDT) for i in range(NWB)]
        wb_rr = [0]
        big = sb("big", [128, 6176])
        mixed = sb("mixed", [128, 2048])
        mixT = sb("mixT", [128, 22, 128], MMDT)
        hTr = mixT[:, 12:20, :]
        valid = sb("valid", [128, 1])
        BIGK = [("big", i) for i in range(13)]

        def bk(c0, c1):
            return [("big", i) for i in range(c0 // 512, (c1 - 1) // 512 + 1)]

        conv_h = [sb("conv_h%d" % i, [128, 12, 3]) for i in range(NHYB)]
        ssm_st = [sb("ssm_st%d" % i, [128, 1024]) for i in range(NHYB)]
        gconv_h = [sb("gconv_h%d" % i, [128, 32, 3]) for i in range(NGDN)]
        gdn_st = [sb("gdn_st%d" % i, [128, 16, 128]) for i in range(NGDN)]

        qT = sb("qT", [128, 4, 128])
        kTt = sb("kTt", [128, 4, 128])
        xbcT = sb("xbcT", [128, 12, 131])
        xbcA = sb("xbcA", [128, 12, 128])
        dtb = sb("dtb", [128, 16])
        alog = sb("alog", [128, 16])
        dsk = sb("dsk", [128, 16])
        dtt = sb("dtt", [128, 16])
        av = sb("av", [128, 16])
        csb = sb("csb", [128, 16])
        dec_b = sb("dec_b", [128, 16])
        dec_w = sb("dec_w", [128, 16])
        beta = sb("beta", [128, 16])
        rn = sb("rn", [128, 16])
        seg = sb("seg", [128, 16, 128])
        cbm = sb("cbm", [128, 2, 128])
        xtok = sb("xtok", [128, 1024])
        xw = sb("xw", [128, 1024])
        btok = sb("btok", [128, 256])
        ysb = sb("ysb", [128, 1024])
        ktw = sb("ktw", [128, NKT * 128])
        vw = sb("vw", [128, NKT, 128])
        Lb = sb("Lb", [128, NKT * 128])
        aU = Lb[:, 0:2048].rearrange("p (h i) -> p h i", i=128)
        Bb = sb("Bb", [128, NKT * 128])
        ETb = sb("ETb", [128, 4, 128])
        mx = sb("mx", [128, 4])
        stage = sb("stage", [128, 4, 128])
        SEGK = [("seg", q4) for q4 in range(4)]
        tm3 = big

        rb = big[0:33, 0:8]
        ohs = big[0:33, 512:512 + TABL]
        fsb = big[0:8, 3072:3072 + TABL]
        S.dma("sp", big[0:32, 0:8], rel_bias[:, :], [], [("big", 0)])
        S.op("pool", lambda g: g.memset(big[32:33, 0:8], 1.0), [], [("big", 0)])
        S.dma("sp", ohs, oh_tab[:, :], [], bk(512, 512 + TABL))
        for c0 in range(0, TABL, 512):
            cw = min(512, TABL - c0)
            p_, pk_ = ps()
            S.op("pe", lambda g: g.matmul(p_[0:8, 0:cw], rb, big[0:33, 512 + c0:512 + c0 + cw], start=True, stop=True),
                 BIGK, [pk_])
            copy("dve", big[0:8, 3072 + c0:3072 + c0 + cw], p_[0:8, 0:cw], [pk_], bk(3072 + c0, 3072 + c0 + cw))
        S.dma("sp", ftab[:, :], fsb, BIGK, ["ftab"])

        gv_all = sb("gv_all", [128, depth * 2, 8])
        for l_ in range(depth):
            S.dma("sp", gv_all[:, 2 * l_, :], bass.AP(norm_mix_pre, l_ * D, [[1, 128], [128, 8]]), [], ["gv_all"], slow=True)
            S.dma("sp", gv_all[:, 2 * l_ + 1, :], bass.AP(norm_ffn_pre, l_ * D, [[1, 128], [128, 8]]), [], ["gv_all"], slow=True)
        cwh_all = sb("cwh_all", [128, NHYB, 12, 4])
        cbh_all = sb("cbh_all", [128, NHYB, 12])
        for i_ in range(NHYB):
            for j_ in range(12):
                S.dma("sp", cwh_all[:, i_, j_, :], bass.AP(ssm_conv_w, i_ * 4 * SSM_XBC + j_ * 128, [[1, 128], [SSM_XBC, 4]]), [], ["cwh_all"], slow=True)
            S.dma("sp", cbh_all[:, i_, :], bass.AP(ssm_conv_b, i_ * SSM_XBC, [[1, 128], [128, 12]]), [], ["cbh_all"], slow=True)
        cwg_all = sb("cwg_all", [128, NG1, 32, 4])
        for i_ in range(NGDN):
            for j_ in range(32):
                S.dma("sp", cwg_all[:, i_, j_, :], bass.AP(gdn_conv_w, i_ * 4 * G_QKV + j_ * 128, [[1, 128], [G_QKV, 4]]), [], ["cwg_all"], slow=True)

        def load_bcast(dst, dkey, src_tensor, off, n):
            S.dma("sp", dst, bass.AP(src_tensor, off, [[0, 128], [1, n]]), [], [dkey])

        def rstd_from_ssq(col, n):
            S.op("act", lambda g: g.activation(out=stat[:, col:col + 1], in_=stat[:, col:col + 1], func=AF.Ln, bias=c_eps, scale=1.0 / n),
                 [("stat", col), "ccol"], [("stat", col)])
            S.op("act", lambda g: g.activation(out=stat[:, col:col + 1], in_=stat[:, col:col + 1], func=AF.Exp, scale=-0.5),
                 [("stat", col)], [("stat", col)])

        def rmsnorm_stats(src, skeys, n, col):
            S.op("act", lambda g: g.activation(out=Bb[:, 0:n], in_=src, func=AF.Square, accum_out=stat[:, col:col + 1]),
                 skeys, ["Bb", ("stat", col)])
            rstd_from_ssq(col, n)

        def prenorm(which, l, need_rev):
            gvec = gv_all[:, 2 * l + which, :]
            rmsnorm_stats(xres[:, :], ["xres"], D, 0)
            S.op("dve", lambda g: g.tensor_scalar(xn[:, :], xres[:, :], stat[:, 0:1], None, ALU.mult),
                 ["xres", ("stat", 0)], ["xn"])
            for rev in ([False, True] if need_rev else [False]):
                dst = hTr if rev else hT
                dk = "hT"
                ko = 12 if rev else 0
                dk = "mixT" if rev else "hT"
                for kc in range(8):
                    p_, pk_ = ps()
                    if rev:
                        S.op("pe", lambda g: g.matmul(p_[:, 0:128], xn[:, kc * 128:(kc + 1) * 128], antiI[:], start=True, stop=True),
                             ["xn", "antiI"], [pk_])
                    else:
                        S.op("pe", lambda g: g.transpose(p_[:, 0:128], xn[:, kc * 128:(kc + 1) * 128], ident[:]), ["xn", "ident"], [pk_])
                    if kc % 2:
                        S.op("dve", lambda g: g.tensor_scalar(dst[:, kc, :], p_[:, 0:128], gvec[:, kc:kc + 1], None, ALU.mult),
                             [pk_, "gv_all"], [(dk, ko + kc)])
                    else:
                        S.op("act", lambda g: g.activation(out=dst[:, kc, :], in_=p_[:, 0:128], func=AF.Copy, scale=gvec[:, kc:kc + 1]),
                             [pk_, "gv_all"], [(dk, ko + kc)])

        def load_w(W, l, K, N, c0, cw):
            KC = K // 128
            i = wb_rr[0]
            wb_rr[0] = (i + 1) % NWB
            wv = wbuf[i][:, 0:KC * cw].rearrange("p (k c) -> p k c", c=cw)
            src = bass.AP(W, l * K * N + c0, [[N, 128], [128 * N, KC], [1, cw]])
            S.dma(wq, wv, src, [], ["wbuf%d" % i])
            return wv, "wbuf%d" % i

        def proj_tok(W, l, K, N, c0, c1, src, skeys, evac):
            KC = K // 128
            cwmax = 512 if KC * 512 <= WBUF else (256 if KC * 256 <= WBUF else 128)
            c = c0
            while c < c1:
                cw = min(cwmax, c1 - c)
                wv, wk = load_w(W, l, K, N, c, cw)
                p_, pk_ = ps()
                for kc in range(KC):
                    S.op("pe", lambda g: g.matmul(p_[:, 0:cw], src[:, kc, :], wv[:, kc, :], start=(kc == 0), stop=(kc == KC - 1)),
                         [wk] + skeys, [pk_])
                evac(p_, pk_, c, cw)
                c += cw

        def proj_feat(W, l, K, N, c0, ncols, src, skeys, evac):
            KC = K // 128
            cwmax = 512 if KC * 512 <= WBUF else (256 if KC * 256 <= WBUF else 128)
            c = c0
            while c < c0 + ncols:
                cw = min(cwmax, c0 + ncols - c)
                wv, wk = load_w(W, l, K, N, c, cw)
                for j in range(cw // 128):
                    p_, pk_ = ps()
                    for kc in range(KC):
                        S.op("pe", lambda g: g.matmul(p_[:, 0:128], wv[:, kc, j * 128:(j + 1) * 128], src[:, kc, :],
                                                      start=(kc == 0), stop=(kc == KC - 1)), [wk] + skeys, [pk_])
                    evac(p_, pk_, (c - c0) // 128 + j)
                c += cw

        hkeys = [("hT", k) for k in range(8)]
        hrkeys = [("mixT", 12 + k) for k in range(8)]

        def post_norm_residual(gain_dram, l, src, skeys):
            load_bcast(gpost[:, :], "gpost", gain_dram, l * D, D)
            rmsnorm_stats(src, skeys, D, 1)
            S.op("dve", lambda g: g.scalar_tensor_tensor(out=xn[:, :], in0=src, scalar=stat[:, 1:2], in1=gpost[:, :],
                                                         op0=ALU.mult, op1=ALU.mult), skeys + [("stat", 1), "gpost"], ["xn"])
            S.op("pool", lambda g: g.tensor_tensor(out=xres[:, :], in0=xres[:, :], in1=xn[:, :], op=ALU.add),
                 ["xres", "xn"], ["xres"])

        def transposes_to(dst, dkey, j0, src, skeys, n, dtcast=True):
            for c in range(n):
                p_, pk_ = ps()
                S.op("pe", lambda g: g.transpose(p_[:, 0:128], src[:, c * 128:(c + 1) * 128], ident[:]), skeys + ["ident"], [pk_])
                copy(evac_eng(), dst[:, j0 + c, :], p_[:, 0:128], [pk_], [(dkey, j0 + c)])

        def ffn(l):
            prenorm(1, l, False)

            def ev_gate(p_, pk_, c, cw):
                S.op("act", lambda g: g.activation(out=big[:, c:c + cw], in_=p_[:, 0:cw], func=AF.Silu), [pk_], bk(c, c + cw))
            proj_tok(w_ffn_gate, l, D, D_FF, 0, D_FF, hT, hkeys, ev_gate)

            def ev_up(p_, pk_, c, cw):
                S.op("dve", lambda g: g.tensor_tensor(out=big[:, c:c + cw], in0=big[:, c:c + cw], in1=p_[:, 0:cw], op=ALU.mult),
                     [pk_] + bk(c, c + cw), bk(c, c + cw))
            proj_tok(w_ffn_up, l, D, D_FF, 0, D_FF, hT, hkeys, ev_up)
            transposes_to(mixT, "mixT", 0, big, bk(0, D_FF), 22)

            def ev_down(p_, pk_, c, cw):
                copy(evac_eng(), mixed[:, c:c + cw], p_[:, 0:cw], [pk_], [("mixed", c // 128)])
            proj_tok(w_ffn_down, l, D_FF, D, 0, D, mixT, [("mixT", c) for c in range(22)], ev_down)
            post_norm_residual(norm_ffn_post, l, mixed[:, 0:D], [("mixed", c) for c in range(8)])

        def decay_mats(gsrc, gkey):
            p_, pk_ = ps()
            S.op("pe", lambda g: g.matmul(p_[:, 0:16], Umat[:, :], gsrc, start=True, stop=True), ["Umat", gkey], [pk_])
            S.op("act", lambda g: g.activation(out=csb[:, :], in_=p_[:, 0:16], func=AF.Exp), [pk_], ["csb"])
            p_, pk_ = ps()
            S.op("pe", lambda g: g.matmul(p_[:, 0:16], ones[:, :], gsrc, start=True, stop=True), ["ones", gkey], [pk_])
            S.op("act", lambda g: g.activation(out=dec_b[:, :], in_=p_[:, 0:16], func=AF.Exp), [pk_], ["dec_b"])
            S.op("pool", lambda g: g.tensor_tensor(out=aU[:, :, :], in0=Umat[:, :].unsqueeze(1).to_broadcast([128, 16, 128]),
                                                   in1=gsrc.unsqueeze(2).to_broadcast([128, 16, 128]), op=ALU.mult),
                 ["Umat", gkey], ["Lb"])
            for q4 in range(4):
                p_, pk_ = ps()
                S.op("pe", lambda g: g.matmul(p_[:, 0:512], SLmat[:, :], aU[:, q4 * 4:(q4 + 1) * 4, :].rearrange("p h i -> p (h i)"),
                                              start=True, stop=True), ["SLmat", "Lb"], [pk_])
                S.op("act", lambda g: g.activation(out=seg[:, q4 * 4:(q4 + 1) * 4, :].rearrange("p h i -> p (h i)"), in_=p_[:, 0:512], func=AF.Exp),
                     [pk_], [("seg", q4)])
            S.op("dve", lambda g: g.tensor_copy(dec_w[:, :], seg[:, :, 127]), SEGK, ["dec_w"])

        def softplus_inplace(t, key):
            S.op("act", lambda g: g.activation(out=t, in_=t, func=AF.Exp), [key], [key])
            S.op("act", lambda g: g.activation(out=t, in_=t, func=AF.Ln, bias=c_one, scale=1.0), [key, "ccol"], [key])

        def conv_silu(nct, cwv, cwkey, hist, hkey, nvalid, bias_sb):
            xk = [("xbcT", j) for j in range(nct)]
            S.op("pool", lambda g: g.tensor_copy(xbcT[:, 0:nct, 0:3], hist), [hkey] + xk, xk)
            S.op("pool", lambda g: g.tensor_copy(hist, xbcT[:, 0:nct, nvalid:nvalid + 3]), xk, [hkey])
            for j in range(nct):
                S.op("dve", lambda g: g.tensor_scalar(xbcA[:, j, :], xbcT[:, j, 0:128], cwv[:, j, 0:1], None, ALU.mult),
                     [("xbcT", j), cwkey], [("xbcA", j)])
                for t in range(1, 4):
                    S.op("dve", lambda g: g.scalar_tensor_tensor(out=xbcA[:, j, :], in0=xbcT[:, j, t:t + 128], scalar=cwv[:, j, t:t + 1],
                                                                 in1=xbcA[:, j, :], op0=ALU.mult, op1=ALU.add),
                         [("xbcT", j), ("xbcA", j), cwkey], [("xbcA", j)])
                if bias_sb is not None:
                    S.op("act", lambda g: g.activation(out=xbcA[:, j, :], in_=xbcA[:, j, :], func=AF.Silu, bias=bias_sb[:, j:j + 1], scale=1.0),
                         [("xbcA", j), "cbh_all"], [("xbcA", j)])
                else:
                    S.op("act", lambda g: g.activation(out=xbcA[:, j, :], in_=xbcA[:, j, :], func=AF.Silu), [("xbcA", j)], [("xbcA", j)])

        def hybrid_load_params(i):
            load_bcast(dtb[:, :], "dtb", ssm_dt_bias, i * 16, 16)
            load_bcast(alog[:, :], "alog", ssm_a_log, i * 16, 16)
            S.op("act", lambda g: g.activation(out=alog[:, :], in_=alog[:, :], func=AF.Exp), ["alog"], ["alog"])
            S.op("dve", lambda g: g.tensor_scalar(alog[:, :], alog[:, :], -1.0, None, ALU.mult), ["alog"], ["alog"])
            load_bcast(dsk[:, :], "dsk", ssm_d, i * 16, 16)

        def hybrid_layer(l, i, sq, tpos, nvalid):
            full = (nvalid == 128)
            cst = conv_h[i]
            sst = ssm_st[i]
            ck = "conv_h%d" % i
            sk = "ssm_st%d" % i
            prenorm(0, l, True)

            def ev_q(p_, pk_, j):
                copy(evac_eng(), qT[:, j, :], p_[:, 0:128], [pk_], [("qT", j)])
            proj_feat(w_hyb_in, i, D, HYB_IN, 0, 512, hTr, hrkeys, ev_q)

            def ev_kT(p_, pk_, j):
                copy(evac_eng(), kTt[:, j, :], p_[:, 0:128], [pk_], [("kTt", j)])
            proj_feat(w_hyb_in, i, D, HYB_IN, 512, 512, hT, hkeys, ev_kT)
            scr_base = (i * NSEQ + sq) * 128 * 4 * SCR_ROWS
            S.dma("sp", bass.AP(kt_scr, scr_base + tpos * 128, [[4 * SCR_ROWS, 128], [SCR_ROWS, 4], [1, 128]]),
                  kTt[:, :, :], [("kTt", j) for j in range(4)], [("kt_scr", i, sq)])

            def ev_kvz(p_, pk_, c, cw):
                copy(evac_eng(), big[:, c - 512:c - 512 + cw], p_[:, 0:cw], [pk_], bk(c - 512, c - 512 + cw))
            proj_tok(w_hyb_in, i, D, HYB_IN, 512, 2560, hT, hkeys, ev_kvz)
            if not full:
                S.op("dve", lambda g: g.tensor_scalar(big[:, 0:1024], big[:, 0:1024], valid[:, 0:1], None, ALU.mult),
                     bk(0, 1024) + ["valid"], bk(0, 1024))
            S.dma("sp", k_scr[i, sq, tpos * 128:(tpos + 1) * 128, :], big[:, 0:512], bk(0, 512), [("k_scr", i, sq)])
            S.dma("sp", v_scr[i, sq, tpos * 128:(tpos + 1) * 128, :], big[:, 512:1024], bk(512, 1024), [("v_scr", i, sq)])

            def ev_dt(p_, pk_, c, cw):
                copy("dve", big[:, 2048:2064], p_[:, 0:16], [pk_], bk(2048, 2064))
            proj_tok(w_hyb_in, i, D, HYB_IN, 4096, 4112, hT, hkeys, ev_dt)

            def ev_xbc(p_, pk_, j):
                copy(evac_eng(), xbcT[:, j, 3:131], p_[:, 0:128], [pk_], [("xbcT", j)])
            proj_feat(w_hyb_in, i, D, HYB_IN, 2560, 1536, hT, hkeys, ev_xbc)

            t_lo = max(0, tpos - 16)
            nk = tpos - t_lo + 1
            W_ = nk * 128
            off_c = (NKT - nk) * 128
            for pr in range(4):
                S.dma("sp", ktw[:, 0:W_], bass.AP(kt_scr, scr_base + pr * SCR_ROWS + t_lo * 128, [[4 * SCR_ROWS, 128], [1, W_]]),
                      [("kt_scr", i, sq)], ["ktw"])
                S.dma("sp", vw[:, 0:nk, :], bass.AP(v_scr, ((i * NSEQ + sq) * SCR_ROWS + t_lo * 128) * A_W + pr * 128,
                                                    [[A_W, 128], [128 * A_W, nk], [1, 128]]), [("v_scr", i, sq)], ["vw"])
                for hh in range(2):
                    h = pr * 2 + hh
                    pb = hh * 64
                    S.dma("sp", Bb[:, 0:W_], bass.AP(ftab, h * TABL + off_c, [[1, 128], [1, W_]]), ["ftab"], ["Bb"])
                    for c0 in range(0, W_, 512):
                        cw = min(512, W_ - c0)
                        p_, pk_ = ps()
                        S.op("pe", lambda g: g.matmul(p_[:, 0:cw], qT[pb:pb + 64, pr, :], ktw[pb:pb + 64, c0:c0 + cw], start=True, stop=True),
                             [("qT", pr), "ktw"], [pk_])
                        S.op("dve", lambda g: g.scalar_tensor_tensor(out=Lb[:, c0:c0 + cw], in0=p_[:, 0:cw], scalar=0.125, in1=Bb[:, c0:c0 + cw],
                                                                     op0=ALU.mult, op1=ALU.add), [pk_, "Bb"], ["Lb"])
                    S.op("dve", lambda g: g.reduce_max(out=mx[:, 0:1], in_=Lb[:, 0:W_], axis=AX.X), ["Lb"], ["mx0"])
                    S.op("dve", lambda g: g.tensor_scalar(mx[:, 1:2], mx[:, 0:1], -1.0, None, ALU.mult), ["mx0"], ["mx1"])
                    S.op("act", lambda g: g.activation(out=Lb[:, 0:W_], in_=Lb[:, 0:W_], func=AF.Exp, bias=mx[:, 1:2], scale=1.0,
                                                       accum_out=mx[:, 2:3]), ["Lb", "mx1"], ["Lb", "mx2"])
                    S.op("dve", lambda g: g.reciprocal(mx[:, 3:4], mx[:, 2:3]), ["mx2"], ["mx3"])
                    po, pok = ps_acc()
                    for kt in range(nk):
                        p_, pk_ = ps()
                        S.op("pe", lambda g: g.transpose(p_[:, 0:128], Lb[:, kt * 128:(kt + 1) * 128], ident[:]), ["Lb", "ident"], [pk_])
                        copy(evac_eng(), ETb[:, kt % 4, :], p_[:, 0:128], [pk_], [("ETb", kt % 4)])
                        S.op("pe", lambda g: g.matmul(po[:, 0:64], ETb[:, kt % 4, :], vw[:, kt, pb:pb + 64], start=(kt == 0), stop=(kt == nk - 1)),
                             [("ETb", kt % 4), "vw"], [pok])
                    S.op("dve", lambda g: g.tensor_scalar(mixed[:, h * 64:(h + 1) * 64], po[:, 0:64], mx[:, 3:4], None, ALU.mult),
                         [pok, "mx3"], [("mixed", h // 2)])
            for c in range(4):
                p_, pk_ = ps()
                S.op("pe", lambda g: g.matmul(p_[:, 0:128], mixed[:, c * 128:(c + 1) * 128], antiI[:], start=True, stop=True),
                     [("mixed", c), "antiI"], [pk_])
                copy(evac_eng(), mixT[:, c, :], p_[:, 0:128], [pk_], [("mixT", c)])

            conv_silu(12, cwh_all[:, i, :, :], "cwh_all", cst[:, :, :], ck, nvalid, cbh_all[:, i, :])
            for c in range(8):
                p_, pk_ = ps()
                S.op("pe", lambda g: g.transpose(p_[:, 0:128], xbcA[:, c, :], ident[:]), [("xbcA", c), "ident"], [pk_])
                copy(evac_eng(), xtok[:, c * 128:(c + 1) * 128], p_[:, 0:128], [pk_], [("xtok", c)])
            for c in range(2):
                p_, pk_ = ps()
                S.op("pe", lambda g: g.transpose(p_[:, 0:128], xbcA[:, 8 + c, :], ident[:]), [("xbcA", 8 + c), "ident"], [pk_])
                copy(evac_eng(), btok[:, c * 128:(c + 1) * 128], p_[:, 0:128], [pk_], [("btok", c)])
            xtk = [("xtok", c) for c in range(8)]
            S.op("dve", lambda g: g.tensor_tensor(out=dtt[:, :], in0=big[:, 2048:2064], in1=dtb[:, :], op=ALU.add), bk(2048, 2064) + ["dtb"], ["dtt"])
            softplus_inplace(dtt[:, :], "dtt")
            if not full:
                S.op("dve", lambda g: g.tensor_scalar(dtt[:, :], dtt[:, :], valid[:, 0:1], None, ALU.mult), ["dtt", "valid"], ["dtt"])
                S.op("dve", lambda g: g.tensor_scalar(xtok[:, :], xtok[:, :], valid[:, 0:1], None, ALU.mult), xtk + ["valid"], xtk)
            S.op("dve", lambda g: g.tensor_tensor(out=av[:, :], in0=dtt[:, :], in1=alog[:, :], op=ALU.mult), ["dtt", "alog"], ["av"])
            decay_mats(av[:, :], "av")
            S.op("dve", lambda g: g.tensor_tensor(out=dec_w[:, :], in0=dec_w[:, :], in1=dtt[:, :], op=ALU.mult), ["dec_w", "dtt"], ["dec_w"])
            S.op("pool", lambda g: g.tensor_tensor(out=xw[:, :].rearrange("p (h d) -> p h d", d=64), in0=xtok[:, :].rearrange("p (h d) -> p h d", d=64),
                                                   in1=dec_w[:, :].unsqueeze(2).to_broadcast([128, 16, 64]), op=ALU.mult), xtk + ["dec_w"], ["xw"])
            for g_ in range(2):
                p_, pk_ = ps()
                S.op("pe", lambda g: g.matmul(p_[:, 0:128], xbcA[:, 8 + g_, :], xbcA[:, 10 + g_, :], start=True, stop=True),
                     [("xbcA", 8 + g_), ("xbcA", 10 + g_)], [pk_])
                S.op("dve", lambda g: g.tensor_tensor(out=cbm[:, g_, :], in0=p_[:, 0:128], in1=Umat[:, :], op=ALU.mult), [pk_, "Umat"], [("cbm", g_)])
            S.op("dve", lambda g: g.tensor_tensor(out=seg[:, :, :], in0=seg[:, :, :], in1=dtt[:, :].unsqueeze(2).to_broadcast([128, 16, 128]), op=ALU.mult),
                 SEGK + ["dtt", "dec_w"], SEGK)
            for g_ in range(2):
                S.op("pool", lambda g: g.tensor_tensor(out=seg[:, g_ * 8:(g_ + 1) * 8, :], in0=seg[:, g_ * 8:(g_ + 1) * 8, :],
                                                       in1=cbm[:, g_, :].unsqueeze(1).to_broadcast([128, 8, 128]), op=ALU.mult),
                     SEGK + [("cbm", g_)], SEGK)
            for g_ in range(2):
                p_, pk_ = ps()
                S.op("pe", lambda g: g.matmul(p_[:, 0:512], xbcA[:, 10 + g_, :], sst[:, g_ * 512:(g_ + 1) * 512], start=True, stop=True),
                     [("xbcA", 10 + g_), sk], [pk_])
                S.op("dve", lambda g: g.tensor_tensor(out=ysb[:, g_ * 512:(g_ + 1) * 512].rearrange("p (h d) -> p h d", d=64),
                                                      in0=p_[:, 0:512].rearrange("p (h d) -> p h d", d=64),
                                                      in1=csb[:, g_ * 8:(g_ + 1) * 8].unsqueeze(2).to_broadcast([128, 8, 64]), op=ALU.mult),
                     [pk_, "csb"], [("ysb", g_)])
            for g_ in range(2):
                p_, pk_ = ps()
                for hh in range(8):
                    h = g_ * 8 + hh
                    S.op("pe", lambda g: g.matmul(p_[:, hh * 64:(hh + 1) * 64], seg[:, h, :], xtok[:, h * 64:(h + 1) * 64], start=True, stop=True),
                         SEGK + xtk, [pk_])
                S.op("dve", lambda g: g.tensor_tensor(out=ysb[:, g_ * 512:(g_ + 1) * 512], in0=ysb[:, g_ * 512:(g_ + 1) * 512], in1=p_[:, 0:512], op=ALU.add),
                     [pk_, ("ysb", g_)], [("ysb", g_)])
            for g_ in range(2):
                p_, pk_ = ps()
                S.op("pe", lambda g: g.matmul(p_[:, 0:512], btok[:, g_ * 128:(g_ + 1) * 128], xw[:, g_ * 512:(g_ + 1) * 512], start=True, stop=True),
                     [("btok", g_), "xw"], [pk_])
                S.op("pool", lambda g: g.tensor_tensor(out=sst[:, g_ * 512:(g_ + 1) * 512].rearrange("p (h d) -> p h d", d=64),
                                                       in0=sst[:, g_ * 512:(g_ + 1) * 512].rearrange("p (h d) -> p h d", d=64),
                                                       in1=dec_b[:, g_ * 8:(g_ + 1) * 8].unsqueeze(2).to_broadcast([128, 8, 64]), op=ALU.mult),
                     [sk, "dec_b"], [sk])
                S.op("dve", lambda g: g.tensor_tensor(out=sst[:, g_ * 512:(g_ + 1) * 512], in0=sst[:, g_ * 512:(g_ + 1) * 512], in1=p_[:, 0:512], op=ALU.add),
                     [pk_, sk], [sk])
            yk = [("ysb", 0), ("ysb", 1)]
            S.op("pool", lambda g: g.tensor_tensor(out=xw[:, :].rearrange("p (h d) -> p h d", d=64), in0=xtok[:, :].rearrange("p (h d) -> p h d", d=64),
                                                   in1=dsk[:, :].unsqueeze(2).to_broadcast([128, 16, 64]), op=ALU.mult), xtk + ["dsk", "xw"], ["xw"])
            S.op("dve", lambda g: g.tensor_tensor(out=ysb[:, :], in0=ysb[:, :], in1=xw[:, :], op=ALU.add), yk + ["xw"], yk)
            S.op("act", lambda g: g.activation(out=big[:, 1024:2048], in_=big[:, 1024:2048], func=AF.Silu), bk(1024, 2048), bk(1024, 2048))
            S.op("dve", lambda g: g.tensor_tensor(out=ysb[:, :], in0=ysb[:, :], in1=big[:, 1024:2048], op=ALU.mult), yk + bk(1024, 2048), yk)
            load_bcast(gpost[:, :], "gpost", ssm_norm_w, i * SSM_DI, SSM_DI)
            snw = gpost
            for g_ in range(2):
                rmsnorm_stats(ysb[:, g_ * 512:(g_ + 1) * 512], [("ysb", g_)], 512, 2 + g_)
                S.op("dve", lambda g: g.scalar_tensor_tensor(out=mixed[:, 512 + g_ * 512:1024 + g_ * 512], in0=ysb[:, g_ * 512:(g_ + 1) * 512],
                                                             scalar=stat[:, 2 + g_:3 + g_], in1=snw[:, g_ * 512:(g_ + 1) * 512], op0=ALU.mult, op1=ALU.mult),
                     [("ysb", g_), ("stat", 2 + g_), "gpost"], [("mixed", 4 + 4 * g_ + c) for c in range(4)])
            transposes_to(mixT, "mixT", 4, mixed[:, 512:1536], [("mixed", 4 + c) for c in range(8)], 8)

            def ev_out(p_, pk_, c, cw):
                copy(evac_eng(), big[:, 3072 + c:3072 + c + cw], p_[:, 0:cw], [pk_], bk(3072 + c, 3072 + c + cw))
            proj_tok(w_hyb_out, i, HYB_MIX, D, 0, D, mixT, [("mixT", c) for c in range(12)], ev_out)
            post_norm_residual(norm_mix_post, l, big[:, 3072:4096], bk(3072, 4096))

        gdtb = dtb
        galog = alog
        gnw = sb("gnw", [128, 128])
        knT = xtok[:, :].rearrange("p (h d) -> p h d", d=128)
        QKT = xw[:, :].rearrange("p (h d) -> p h d", d=128)
        def mk_set(parts):
            return parts
        gs = []
        gs_alias = []
        vwf = vw[:, :, :].rearrange("p a b -> p (a b)")
        for base, al in [(ktw, ["ktw"]), (vwf, ["vw"]), (Bb, ["Bb"])]:
            dct = {}
            for n_, nm in enumerate(["s0", "s1", "s2", "s3", "s4", "s5"]):
                dct[nm] = base[:, n_ * 128:(n_ + 1) * 128]
            dct["XA"] = base[:, 768:1024]
            dct["XB"] = base[:, 1024:1280]
            gs.append(dct)
            gs_alias.append(al)
        dct = {}
        for n_, nm in enumerate(["s0", "s1", "s2", "s3", "s4", "s5"]):
            dct[nm] = ktw[:, 1280 + n_ * 128:1280 + (n_ + 1) * 128]
        dct["XA"] = vwf[:, 1280:1536]
        dct["XB"] = vwf[:, 1536:1792]
        gs.append(dct)
        gs_alias.append(["ktw", "vw"])

        def gdn_load_params(i):
            load_bcast(gdtb[:, :], "dtb", gdn_dt_bias, i * 16, 16)
            load_bcast(galog[:, :], "alog", gdn_a_log, i * 16, 16)
            S.op("act", lambda g: g.activation(out=galog[:, :], in_=galog[:, :], func=AF.Exp), ["alog"], ["alog"])
            S.op("dve", lambda g: g.tensor_scalar(galog[:, :], galog[:, :], -1.0, None, ALU.mult), ["alog"], ["alog"])
            load_bcast(gnw[:, :], "gnw", gdn_norm_w, i * 128, 128)

        def gdn_layer(l, i, sq, tpos, nvalid):
            full = (nvalid == 128)
            cst = gconv_h[i]
            ck = "gconv_h%d" % i
            Sst = gdn_st[i]
            prenorm(0, l, False)
            S.op("pool", lambda g: g.memset(ktw[0:1, 0:1], 0.0), [], ["ktw"])
            S.op("pool", lambda g: g.memset(vw[0:1, 0:1, 0:1], 0.0), [], ["vw"])
            S.op("pool", lambda g: g.memset(Bb[0:1, 0:1], 0.0), [], ["Bb"])

            def ev_zba(p_, pk_, c, cw):
                copy(evac_eng(), big[:, c:c + cw], p_[:, 0:cw], [pk_], bk(c, c + cw))
            proj_tok(w_gdn_in, i, D, G_IN, 4096, G_IN, hT, hkeys, ev_zba)
            for grp in range(4):
                def ev_x(p_, pk_, j):
                    copy(evac_eng(), xbcT[:, j, 3:131], p_[:, 0:128], [pk_], [("xbcT", j)])
                proj_feat(w_gdn_in, i, D, G_IN, grp * 1024, 1024, hT, hkeys, ev_x)
                conv_silu(8, cwg_all[:, i, grp * 8:(grp + 1) * 8, :], "cwg_all", cst[:, grp * 8:(grp + 1) * 8, :], ck, nvalid, None)
                for j in range(8):
                    ct = grp * 8 + j
                    p_, pk_ = ps()
                    S.op("pe", lambda g: g.transpose(p_[:, 0:128], xbcA[:, j, :], ident[:]), [("xbcA", j), "ident"], [pk_])
                    copy(evac_eng(), big[:, ct * 128:(ct + 1) * 128], p_[:, 0:128], [pk_], bk(ct * 128, (ct + 1) * 128))
            S.op("pool", lambda g: g.tensor_tensor(out=Lb[:, 0:2048], in0=big[:, 0:2048], in1=big[:, 0:2048], op=ALU.mult), bk(0, 2048), ["Lb"])
            S.op("dve", lambda g: g.reduce_sum(out=rn[:, :], in_=Lb[:, 0:2048].rearrange("p (h d) -> p h d", d=128), axis=AX.X), ["Lb"], ["rn"])
            S.op("act", lambda g: g.activation(out=rn[:, :], in_=rn[:, :], func=AF.Ln, bias=c_eps, scale=1.0), ["rn", "ccol"], ["rn"])
            S.op("act", lambda g: g.activation(out=rn[:, :], in_=rn[:, :], func=AF.Exp, scale=-0.5), ["rn"], ["rn"])
            S.op("dve", lambda g: g.tensor_scalar(rn[:, 0:8], rn[:, 0:8], 128.0 ** -0.5, None, ALU.mult), ["rn"], ["rn"])
            if not full:
                S.op("dve", lambda g: g.tensor_scalar(rn[:, :], rn[:, :], valid[:, 0:1], None, ALU.mult), ["rn", "valid"], ["rn"])
                S.op("dve", lambda g: g.tensor_scalar(big[:, 2048:4096], big[:, 2048:4096], valid[:, 0:1], None, ALU.mult),
                     bk(2048, 4096) + ["valid"], bk(2048, 4096))
            S.op("dve", lambda g: g.tensor_tensor(out=big[:, 0:2048].rearrange("p (h d) -> p h d", d=128), in0=big[:, 0:2048].rearrange("p (h d) -> p h d", d=128),
                                                  in1=rn[:, :].unsqueeze(2).to_broadcast([128, 16, 128]), op=ALU.mult), bk(0, 2048) + ["rn"], bk(0, 2048))
            S.op("act", lambda g: g.activation(out=beta[:, :], in_=big[:, 6144:6160], func=AF.Exp, scale=-1.0), bk(6144, 6160), ["beta"])
            S.op("dve", lambda g: g.tensor_scalar(beta[:, :], beta[:, :], 1.0, None, ALU.add), ["beta"], ["beta"])
            S.op("dve", lambda g: g.reciprocal(beta[:, :], beta[:, :]), ["beta"], ["beta"])
            S.op("dve", lambda g: g.tensor_tensor(out=dtt[:, :], in0=big[:, 6160:6176], in1=gdtb[:, :], op=ALU.add), bk(6160, 6176) + ["dtb"], ["dtt"])
            softplus_inplace(dtt[:, :], "dtt")
            S.op("dve", lambda g: g.tensor_tensor(out=av[:, :], in0=dtt[:, :], in1=galog[:, :], op=ALU.mult), ["dtt", "alog"], ["av"])
            if not full:
                S.op("dve", lambda g: g.tensor_scalar(beta[:, :], beta[:, :], valid[:, 0:1], None, ALU.mult), ["beta", "valid"], ["beta"])
                S.op("dve", lambda g: g.tensor_scalar(av[:, :], av[:, :], valid[:, 0:1], None, ALU.mult), ["av", "valid"], ["av"])
            decay_mats(av[:, :], "av")
            S.op("pool", lambda g: g.tensor_tensor(out=aU[:, :, :], in0=seg[:, :, :], in1=SUmat[:, :].unsqueeze(1).to_broadcast([128, 16, 128]), op=ALU.mult),
                 SEGK + ["SUmat", "Lb"], ["Lb"])
            S.op("dve", lambda g: g.tensor_tensor(out=seg[:, :, :], in0=seg[:, :, :], in1=Umat[:, :].unsqueeze(1).to_broadcast([128, 16, 128]), op=ALU.mult),
                 SEGK + ["Umat", "dec_w", "Lb"], SEGK)
            for hq in range(8):
                p_, pk_ = ps()
                S.op("pe", lambda g: g.transpose(p_[:, 0:128], big[:, 1024 + hq * 128:1152 + hq * 128], ident[:]), bk(1024, 2048) + ["ident"], [pk_])
                copy(evac_eng(), knT[:, hq, :], p_[:, 0:128], [pk_], [("xtok", hq)])
                p_, pk_ = ps()
                S.op("pe", lambda g: g.transpose(p_[:, 0:128], big[:, hq * 128:(hq + 1) * 128], ident[:]), bk(0, 1024) + ["ident"], [pk_])
                copy(evac_eng(), stage[:, hq % 4, :], p_[:, 0:128], [pk_], [("stage", hq % 4)])
                p_, pk_ = ps()
                S.op("pe", lambda g: g.matmul(p_[:, 0:128], knT[:, hq, :], stage[:, hq % 4, :], start=True, stop=True), [("xtok", hq), ("stage", hq % 4)], [pk_])
                copy(evac_eng(), QKT[:, hq, :], p_[:, 0:128], [pk_], ["xw"])
            def head_gen(h, par):
                hq = h // 2
                G = gs[par]
                ALS = gs_alias[par]

                def K(nm):
                    return "g%s%d" % (nm, par)

                def SO(e, fn, reads, writes):
                    S.op(e, fn, list(reads) + ALS, writes)

                def CP(e, out, in_, reads, writes):
                    copy(e, out, in_, list(reads) + ALS, writes)
                kcols = bk(1024 + hq * 128, 1152 + hq * 128)
                qcols = bk(hq * 128, (hq + 1) * 128)
                vcols = bk(2048 + h * 128, 2176 + h * 128)
                k_n = big[:, 1024 + hq * 128:1152 + hq * 128]
                q_n = big[:, hq * 128:(hq + 1) * 128]
                v_h = big[:, 2048 + h * 128:2176 + h * 128]
                SO("dve", lambda g: g.tensor_scalar(G["s0"], k_n, beta[:, h:h + 1], None, ALU.mult), kcols + ["beta"], [K("s0")])
                SO("dve", lambda g: g.tensor_scalar(G["XA"][:, 0:128], v_h, beta[:, h:h + 1], None, ALU.mult), vcols + ["beta"], [K("XA")])
                SO("dve", lambda g: g.tensor_scalar(G["XA"][:, 128:256], G["s0"], csb[:, h:h + 1], None, ALU.mult), [K("s0"), "csb", K("XA")], [K("XA")])
                p_, pk_ = ps()
                SO("pe", lambda g: g.transpose(p_[:, 0:128], G["s0"], ident[:]), [K("s0"), "ident"], [pk_])
                CP(evac_eng(), G["s1"], p_[:, 0:128], [pk_], [K("s1")])
                yield
                p_, pk_ = ps()
                SO("pe", lambda g: g.matmul(p_[:, 0:128], knT[:, hq, :], G["s1"], start=True, stop=True), [("xtok", hq), K("s1")], [pk_])
                SO("dve", lambda g: g.tensor_tensor(out=G["s2"], in0=p_[:, 0:128], in1=aU[:, h, :], op=ALU.mult), [pk_, "Lb"], [K("s2")])
                yield
                p_, pk_ = ps()
                SO("pe", lambda g: g.transpose(p_[:, 0:128], G["s2"], ident[:]), [K("s2"), "ident"], [pk_])
                CP(evac_eng(), G["s3"], p_[:, 0:128], [pk_], [K("s3")])
                p_, pk_ = ps()
                SO("pe", lambda g: g.matmul(p_[:, 0:256], G["s2"], G["XA"], start=True, stop=True), [K("s2"), K("XA")], [pk_])
                SO("dve", lambda g: g.tensor_tensor(out=G["XB"], in0=G["XA"], in1=p_[:, 0:256], op=ALU.subtract), [pk_, K("XA")], [K("XB")])
                yield
                Xc, Xn = "XB", "XA"
                Mc, Mn, MTc, MTn = "s3", "s5", "s2", "s4"
                for lvl in range(1, 7):
                    p_, pk_ = ps()
                    SO("pe", lambda g: g.matmul(p_[:, 0:128], G[Mc], G[MTc], start=True, stop=True), [K(Mc), K(MTc)], [pk_])
                    CP(evac_eng(), G[MTn], p_[:, 0:128], [pk_], [K(MTn)])
                    if lvl < 6:
                        p_, pk_ = ps()
                        SO("pe", lambda g: g.matmul(p_[:, 0:128], G[MTc], G[Mc], start=True, stop=True), [K(Mc), K(MTc)], [pk_])
                        CP(evac_eng(), G[Mn], p_[:, 0:128], [pk_], [K(Mn)])
                    yield
                    p_, pk_ = ps()
                    SO("pe", lambda g: g.matmul(p_[:, 0:256], G[MTn], G[Xc], start=True, stop=True), [K(MTn), K(Xc)], [pk_])
                    SO("dve", lambda g: g.tensor_tensor(out=G[Xn], in0=G[Xc], in1=p_[:, 0:256], op=ALU.add), [pk_, K(Xc)], [K(Xn)])
                    yield
                    Xc, Xn = Xn, Xc
                    Mc, Mn = Mn, Mc
                    MTc, MTn = MTn, MTc
                X = G[Xc]
                p_, pk_ = ps()
                SO("pe", lambda g: g.transpose(p_[:, 0:128], X[:, 128:256], ident[:]), [K(Xc), "ident"], [pk_])
                CP(evac_eng(), G["s0"], p_[:, 0:128], [pk_], [K("s0")])
                SO("dve", lambda g: g.tensor_scalar(G["s3"], q_n, csb[:, h:h + 1], None, ALU.mult), qcols + ["csb"], [K("s3")])
                yield
                stk = ("gdn_st", i, h)
                p_, pk_ = ps()
                SO("pe", lambda g: g.matmul(p_[:, 0:128], G["s0"], Sst[:, h, :], start=True, stop=True), [K("s0"), stk], [pk_])
                SO("dve", lambda g: g.tensor_tensor(out=G["s1"], in0=X[:, 0:128], in1=p_[:, 0:128], op=ALU.subtract), [pk_, K(Xc)], [K("s1")])
                p_, pk_ = ps()
                SO("pe", lambda g: g.transpose(p_[:, 0:128], G["s3"], ident[:]), [K("s3"), "ident"], [pk_])
                CP(evac_eng(), G["s5"], p_[:, 0:128], [pk_], [K("s5")])
                SO("pool", lambda g: g.tensor_tensor(out=G["s2"], in0=QKT[:, hq, :], in1=seg[:, h, :], op=ALU.mult),
                   ["xw"] + SEGK, [K("s2")])
                SO("dve", lambda g: g.tensor_scalar(G["s4"], k_n, dec_w[:, h:h + 1], None, ALU.mult), kcols + ["dec_w"], [K("s4")])
                yield
                p_, pk_ = ps()
                SO("pe", lambda g: g.matmul(p_[:, 0:128], G["s5"], Sst[:, h, :], start=True, stop=False), [K("s5"), stk], [pk_])
                SO("pe", lambda g: g.matmul(p_[:, 0:128], G["s2"], G["s1"], start=False, stop=True), [K("s2"), K("s1")], [pk_])
                CP(evac_eng(), mixed[:, h * 128:(h + 1) * 128], p_[:, 0:128], [pk_], [("mixed", h)])
                p_, pk_ = ps()
                SO("pe", lambda g: g.matmul(p_[:, 0:128], G["s4"], G["s1"], start=True, stop=True), [K("s4"), K("s1")], [pk_])
                SO("dve", lambda g: g.scalar_tensor_tensor(out=Sst[:, h, :], in0=Sst[:, h, :], scalar=dec_b[:, h:h + 1], in1=p_[:, 0:128],
                                                           op0=ALU.mult, op1=ALU.add), [pk_, stk, "dec_b"], [stk])
                yield

            NGRP = len(gs)
            for h0 in range(0, 16, NGRP):
                gens = [head_gen(h0 + j, j) for j in range(min(NGRP, 16 - h0))]
                alive = list(gens)
                while alive:
                    nxt = []
                    for g_ in alive:
                        try:
                            next(g_)
                            nxt.append(g_)
                        except StopIteration:
                            pass
                    alive = nxt
            mk = [("mixed", h) for h in range(16)]
            S.op("pool", lambda g: g.tensor_tensor(out=Lb[:, 0:2048], in0=mixed[:, :], in1=mixed[:, :], op=ALU.mult), mk, ["Lb"])
            S.op("dve", lambda g: g.reduce_sum(out=rn[:, :], in_=Lb[:, 0:2048].rearrange("p (h d) -> p h d", d=128), axis=AX.X), ["Lb"], ["rn"])
            S.op("act", lambda g: g.activation(out=rn[:, :], in_=rn[:, :], func=AF.Ln, bias=c_eps, scale=1.0 / 128), ["rn", "ccol"], ["rn"])
            S.op("act", lambda g: g.activation(out=rn[:, :], in_=rn[:, :], func=AF.Exp, scale=-0.5), ["rn"], ["rn"])
            S.op("dve", lambda g: g.tensor_tensor(out=mixed[:, :].rearrange("p (h d) -> p h d", d=128), in0=mixed[:, :].rearrange("p (h d) -> p h d", d=128),
                                                  in1=rn[:, :].unsqueeze(2).to_broadcast([128, 16, 128]), op=ALU.mult), mk + ["rn"], mk)
            S.op("pool", lambda g: g.tensor_tensor(out=mixed[:, :].rearrange("p (h d) -> p h d", d=128), in0=mixed[:, :].rearrange("p (h d) -> p h d", d=128),
                                                   in1=gnw[:, :].unsqueeze(1).to_broadcast([128, 16, 128]), op=ALU.mult), mk + ["gnw"], mk)
            S.op("act", lambda g: g.activation(out=big[:, 4096:6144], in_=big[:, 4096:6144], func=AF.Silu), bk(4096, 6144), bk(4096, 6144))
            S.op("dve", lambda g: g.tensor_tensor(out=mixed[:, :], in0=mixed[:, :], in1=big[:, 4096:6144], op=ALU.mult), mk + bk(4096, 6144), mk)
            transposes_to(mixT, "mixT", 0, mixed, mk, 16)

            def ev_out(p_, pk_, c, cw):
                copy(evac_eng(), big[:, c:c + cw], p_[:, 0:cw], [pk_], bk(c, c + cw))
            proj_tok(w_gdn_out, i, G_VW, D, 0, D, mixT, [("mixT", c) for c in range(16)], ev_out)
            post_norm_residual(norm_mix_post, l, big[:, 0:1024], bk(0, 1024))

        def conv_state_load(dst, dkey, nct, src_tensor, off, width):
            S.dma("sp", tm3[0:3, 0:width], bass.AP(src_tensor, off, [[width, 3], [1, width]]), [], BIGK)
            for j in range(nct):
                p_, pk_ = ps()
                S.op("pe", lambda g: g.transpose(p_[:, 0:3], tm3[0:3, j * 128:(j + 1) * 128], ident[0:3, 0:3]), BIGK + ["ident"], [pk_])
                copy("dve", dst[:, j, :], p_[:, 0:3], [pk_], [dkey])

        def conv_state_store(src, skey, nct, dst_tensor, off, width):
            for j in range(nct):
                p_, pk_ = ps()
                S.op("pe", lambda g: g.transpose(p_[0:3, 0:128], src[:, j, :], ident[:]), [skey, "ident"], [pk_])
                copy("dve", tm3[0:3, j * 128:(j + 1) * 128], p_[0:3, 0:128], [pk_], BIGK)
            S.dma("sp", bass.AP(dst_tensor, off, [[width, 3], [1, width]]), tm3[0:3, 0:width], BIGK, [("out", dst_tensor.name)])

        def ssm_state_load(i, src_tensor, off):
            S.dma("sp", Lb[:, 0:1024].rearrange("p (b n) -> p b n", n=128), bass.AP(src_tensor, off, [[128, 128], [128 * 128, 8], [1, 128]]), [], ["Lb"])
            for b in range(8):
                p_, pk_ = ps()
                S.op("pe", lambda g: g.transpose(p_[:, 0:128], Lb[:, b * 128:(b + 1) * 128], ident[:]), ["Lb", "ident"], [pk_])
                copy(evac_eng(), ssm_st[i][:, b * 128:(b + 1) * 128], p_[:, 0:128], [pk_], ["ssm_st%d" % i])

        def ssm_state_store(i, dst_tensor, off):
            for b in range(8):
                p_, pk_ = ps()
                S.op("pe", lambda g: g.transpose(p_[:, 0:128], ssm_st[i][:, b * 128:(b + 1) * 128], ident[:]), ["ssm_st%d" % i, "ident"], [pk_])
                copy(evac_eng(), Lb[:, b * 128:(b + 1) * 128], p_[:, 0:128], [pk_], ["Lb"])
            S.dma("sp", bass.AP(dst_tensor, off, [[128, 128], [128 * 128, 8], [1, 128]]), Lb[:, 0:1024].rearrange("p (b n) -> p b n", n=128),
                  ["Lb"], [("out", dst_tensor.name)])

        def flat_copy(dst_tensor, doff, src_tensor, soff, nelem, rkeys, wkeys):
            assert nelem % 128 == 0
            per = nelem // 128
            S.dma("sp", bass.AP(dst_tensor, doff, [[per, 128], [1, per]]), bass.AP(src_tensor, soff, [[per, 128], [1, per]]), rkeys, wkeys)

        def seq_begin(sq, kind, sidx):
            for i in range(NHYB):
                ck = "conv_h%d" % i
                if kind == "p":
                    S.op("pool", lambda g: g.memset(conv_h[i][:, :, :], 0.0), [], [ck])
                    S.op("pool", lambda g: g.memset(ssm_st[i][:, :], 0.0), [], ["ssm_st%d" % i])
                else:
                    conv_state_load(conv_h[i], ck, 12, st_sconv, (i * NS1 + sidx) * 3 * SSM_XBC, SSM_XBC)
                    ssm_state_load(i, st_ssm, (i * NS1 + sidx) * 1024 * 128)
                    flat_copy(k_scr, (i * NSEQ + sq) * SCR_ROWS * A_W, cache_k, (i * NS1 + sidx) * WIN * A_W, WIN * A_W, [], [("k_scr", i, sq)])
                    flat_copy(v_scr, (i * NSEQ + sq) * SCR_ROWS * A_W, cache_v, (i * NS1 + sidx) * WIN * A_W, WIN * A_W, [], [("v_scr", i, sq)])
                    scr_base = (i * NSEQ + sq) * 128 * 4 * SCR_ROWS
                    for t in range(16):
                        S.dma("sp", Lb[:, 0:512], bass.AP(cache_k, ((i * NS1 + sidx) * WIN + t * 128) * A_W, [[A_W, 128], [1, A_W]]), [], ["Lb"])
                        for pr in range(4):
                            p_, pk_ = ps()
                            S.op("pe", lambda g: g.transpose(p_[:, 0:128], Lb[:, pr * 128:(pr + 1) * 128], ident[:]), ["Lb", "ident"], [pk_])
                            copy(evac_eng(), stage[:, pr, :], p_[:, 0:128], [pk_], [("stage", pr)])
                        S.dma("sp", bass.AP(kt_scr, scr_base + t * 128, [[4 * SCR_ROWS, 128], [SCR_ROWS, 4], [1, 128]]), stage[:, :, :],
                              [("stage", pr) for pr in range(4)], [("kt_scr", i, sq)])
            for i in range(NGDN):
                ck = "gconv_h%d" % i
                if kind == "p":
                    S.op("pool", lambda g: g.memset(gconv_h[i][:, :, :], 0.0), [], [ck])
                    S.op("pool", lambda g: g.memset(gdn_st[i][:, :, :], 0.0), [], [("gdn_st", i, h) for h in range(16)])
                else:
                    conv_state_load(gconv_h[i], ck, 32, st_gconv, (i * NS1 + sidx) * 3 * G_QKV, G_QKV)
                    S.dma("sp", gdn_st[i][:, :, :], bass.AP(st_gdn, (i * NS1 + sidx) * 16 * 128 * 128, [[128, 128], [128 * 128, 16], [1, 128]]),
                          [], [("gdn_st", i, h) for h in range(16)])

        def seq_end(sq, kind, sidx):
            for i in range(NHYB):
                ck = "conv_h%d" % i
                if kind == "p":
                    conv_state_store(conv_h[i], ck, 12, o_psc, i * 3 * SSM_XBC, SSM_XBC)
                    ssm_state_store(i, o_pss, i * 1024 * 128)
                    flat_copy(o_pk, i * KEEP * A_W, k_scr, ((i * NSEQ + sq) * SCR_ROWS + SEQ - KEEP) * A_W, KEEP * A_W, [("k_scr", i, sq)], [("out", "pk", i)])
                    flat_copy(o_pv, i * KEEP * A_W, v_scr, ((i * NSEQ + sq) * SCR_ROWS + SEQ - KEEP) * A_W, KEEP * A_W, [("v_scr", i, sq)], [("out", "pv", i)])
                else:
                    conv_state_store(conv_h[i], ck, 12, o_ssc, (i * NS1 + sidx) * 3 * SSM_XBC, SSM_XBC)
                    ssm_state_store(i, o_sss, (i * NS1 + sidx) * 1024 * 128)
                    flat_copy(o_sk, (i * NS1 + sidx) * WIN * A_W, k_scr, ((i * NSEQ + sq) * SCR_ROWS + 1) * A_W, WIN * A_W, [("k_scr", i, sq)], [("out", "sk", i, sidx)])
                    flat_copy(o_sv, (i * NS1 + sidx) * WIN * A_W, v_scr, ((i * NSEQ + sq) * SCR_ROWS + 1) * A_W, WIN * A_W, [("v_scr", i, sq)], [("out", "sv", i, sidx)])
            for i in range(NGDN):
                ck = "gconv_h%d" % i
                stk = [("gdn_st", i, h) for h in range(16)]
                if kind == "p":
                    conv_state_store(gconv_h[i], ck, 32, o_pgc, i * 3 * G_QKV, G_QKV)
                    S.dma("sp", bass.AP(o_pgs, i * 16 * 128 * 128, [[128, 128], [128 * 128, 16], [1, 128]]), gdn_st[i][:, :, :], stk, [("out", "pgs", i)])
                else:
                    conv_state_store(gconv_h[i], ck, 32, o_sgc, (i * NS1 + sidx) * 3 * G_QKV, G_QKV)
                    S.dma("sp", bass.AP(o_sgs, (i * NS1 + sidx) * 16 * 128 * 128, [[128, 128], [128 * 128, 16], [1, 128]]), gdn_st[i][:, :, :], stk,
                          [("out", "sgs", i, sidx)])

        for sq, (kind, sidx, t0, ntl) in enumerate(seqs):
            nvalid = 128 if kind == "p" else 1
            if kind == "s":
                S.op("pool", lambda g: g.memset(valid[:, :], 0.0), [], ["valid"])
                S.op("pool", lambda g: g.memset(valid[0:1, :], 1.0), ["valid"], ["valid"])
            seq_begin(sq, kind, sidx)
            for tl in range(ntl):
                tpos = t0 + tl
                if kind == "p":
                    S.dma("sp", xres[:, :], x_prompt[tl * 128:(tl + 1) * 128, :], [], ["xres"])
                else:
                    S.op("pool", lambda g: g.memset(xres[:, :], 0.0), [], ["xres"])
                    S.dma("sp", xres[0:1, :], x_sample[sidx:sidx + 1, :], ["xres"], ["xres"])
                for l in range(depth):
                    i = l // 2
                    if l % 2 == 0:
                        hybrid_load_params(i)
                        hybrid_layer(l, i, sq, tpos, nvalid)
                    else:
                        gdn_load_params(i)
                        gdn_layer(l, i, sq, tpos, nvalid)
                    ffn(l)
                if kind == "p":
                    S.dma("sp", y_prompt[tl * 128:(tl + 1) * 128, :], xres[:, :], ["xres"], ["y_prompt"])
                else:
                    S.dma("sp", y_sample[sidx:sidx + 1, :], xres[0:1, :], ["xres"], ["y_sample"])
            seq_end(sq, kind, sidx)

        for slot in S.dma_sems:
            if slot[1] > 0:
                S._wait("sp", (slot[0], slot[1]))
        for e in S.eng:
            if e != "sp" and S.cnt[e] > 0:
                S._wait("sp", (S.sem[e], S.cnt[e]))
        print("instructions:", S.ninstr, "waits:", S.nwait, "sems:", S.nsem)
    return nc


_NC_CACHE = {}


def kernel(x_prompt, x_sample, cache_attn_k, cache_attn_v, state_ssm_conv, state_ssm, state_gdn_conv, state_gdn,
           rel_bias, norm_mix_pre, norm_mix_post, norm_ffn_pre, norm_ffn_post, w_hyb_in, ssm_conv_w, ssm_conv_b,
           ssm_dt_bias, ssm_a_log, ssm_d, ssm_norm_w, w_hyb_out, w_gdn_in, gdn_conv_w, gdn_dt_bias, gdn_a_log,
           gdn_norm_w, w_gdn_out, w_ffn_gate, w_ffn_up, w_ffn_down):
    f = lambda a: np.ascontiguousarray(np.asarray(a, dtype=np.float32))
    x_prompt = f(x_prompt)
    B, SEQ, _ = x_prompt.shape
    x_sample = f(x_sample)
    DB = x_sample.shape[0]
    depth = np.asarray(norm_mix_pre).shape[0]
    n_ptiles = SEQ // 128
    assert DB % NCORES == 0
    n_samp = DB // NCORES
    key = (n_ptiles, n_samp, depth)
    if key not in _NC_CACHE:
        _NC_CACHE[key] = build_nc(n_ptiles, n_samp, depth=depth)
    nc = _NC_CACHE[key]
    NHYB = (depth + 1) // 2
    NGDN = depth // 2
    shared = dict(
        rel_bias=f(rel_bias), oh_tab=attn_tables(), norm_mix_pre=f(norm_mix_pre), norm_mix_post=f(norm_mix_post),
        norm_ffn_pre=f(norm_ffn_pre), norm_ffn_post=f(norm_ffn_post), w_hyb_in=f(w_hyb_in), ssm_conv_w=f(ssm_conv_w),
        ssm_conv_b=f(ssm_conv_b), ssm_dt_bias=f(ssm_dt_bias), ssm_a_log=f(ssm_a_log), ssm_d=f(ssm_d), ssm_norm_w=f(ssm_norm_w),
        w_hyb_out=f(w_hyb_out), w_gdn_in=f(w_gdn_in), gdn_conv_w=f(gdn_conv_w), gdn_dt_bias=f(gdn_dt_bias),
        gdn_a_log=f(gdn_a_log), gdn_norm_w=f(gdn_norm_w), w_gdn_out=f(w_gdn_out), w_ffn_gate=f(w_ffn_gate),
        w_ffn_up=f(w_ffn_up), w_ffn_down=f(w_ffn_down))
    ck = f(cache_attn_k).reshape(NHYB, DB, WIN, A_W)
    cv = f(cache_attn_v).reshape(NHYB, DB, WIN, A_W)
    sc = f(state_ssm_conv)
    ss = f(state_ssm).reshape(NHYB, DB, 1024, 128)
    gc = f(state_gdn_conv)
    gst = f(state_gdn)
    in_maps = []
    for c in range(NCORES):
        sl = slice(c * n_samp, (c + 1) * n_samp)
        m = dict(shared)
        m["x_prompt"] = np.ascontiguousarray(x_prompt[c % B])
        m["x_sample"] = np.ascontiguousarray(x_sample[sl, 0, :])
        m["cache_k"] = np.ascontiguousarray(ck[:, sl])
        m["cache_v"] = np.ascontiguousarray(cv[:, sl])
        m["st_sconv"] = np.ascontiguousarray(sc[:, sl])
        m["st_ssm"] = np.ascontiguousarray(ss[:, sl])
        m["st_gconv"] = np.ascontiguousarray(gc[:, sl])
        m["st_gdn"] = np.ascontiguousarray(gst[:, sl])
        in_maps.append(m)
    res = run_bass_kernel_spmd(nc, in_maps, core_ids=list(range(NCORES)))
    R = res.results
    KEEP = min(WIN, SEQ)
    pc = list(range(B))
    y_prompt = np.stack([R[c]["y_prompt"] for c in pc], 0)
    y_sample = np.concatenate([R[c]["y_sample"] for c in range(NCORES)], 0)[:, None, :]
    pk = np.stack([R[c]["o_pk"] for c in pc], 1).reshape(NHYB, B, KEEP, 8, 64)
    pv = np.stack([R[c]["o_pv"] for c in pc], 1).reshape(NHYB, B, KEEP, 8, 64)
    psc = np.stack([R[c]["o_psc"] for c in pc], 1)
    pss = np.stack([R[c]["o_pss"] for c in pc], 1).reshape(NHYB, B, 16, 64, 128)
    pgc = np.stack([R[c]["o_pgc"] for c in pc], 1)
    pgs = np.stack([R[c]["o_pgs"] for c in pc], 1)
    cat = lambda nm: np.concatenate([R[c][nm] for c in range(NCORES)], 1)
    sk = cat("o_sk").reshape(NHYB, DB, WIN, 8, 64)
    sv = cat("o_sv").reshape(NHYB, DB, WIN, 8, 64)
    ssc = cat("o_ssc")
    sss = cat("o_sss").reshape(NHYB, DB, 16, 64, 128)
    sgc = cat("o_sgc")
    sgs = cat("o_sgs")
    return (y_prompt, y_sample, pk, pv, psc, pss, pgc, pgs, sk, sv, ssc, sss, sgc, sgs)
```

```python
import math
from contextlib import ExitStack
import numpy as np
import concourse.bass as bass
import concourse.mybir as mybir
from concourse.bass_utils import run_bass_kernel_spmd

F32 = mybir.dt.float32
F32R = mybir.dt.float32r
AF = mybir.ActivationFunctionType
ALU = mybir.AluOpType
AX = mybir.AxisListType

D = 1024
EPS = 1e-6
A_W = 512
WIN = 2048
NKT = 17
SSM_DI = 1024
SSM_XBC = 1536
HYB_IN = 4112
HYB_MIX = 1536
G_VW = 2048
G_QKV = 4096
G_IN = 6176
D_FF = 2816
NEG = -30000.0
NCORES = 8
TABL = 2304


def rel_buckets(dist):
    max_exact = 16
    n = np.maximum(dist, 1).astype(np.float32)
    large = max_exact + (np.log(n / max_exact) / math.log(2048 / max_exact) * (32 - max_exact)).astype(np.int32)
    large = np.minimum(large, 31)
    return np.where(dist < max_exact, dist, large).astype(np.int32)


def attn_tables():
    dist = 2175 - np.arange(TABL)
    valid = (dist >= 0) & (dist <= 2048)
    dc = np.clip(dist, 0, 2048)
    cnt = ((dc <= 128).astype(np.float64) + ((dc % 4 == 0) & (dc <= 512)) + ((dc % 16 == 0) & (dc <= 2048)))
    cnt = np.where(valid, cnt, 0.0)
    logc = np.where(cnt > 0, np.log(np.maximum(cnt, 1e-9)), NEG).astype(np.float32)
    bk = rel_buckets(dc)
    oh = np.zeros((33, TABL), np.float32)
    oh[bk, np.arange(TABL)] = np.where(cnt > 0, 1.0, 0.0)
    oh[32, :] = logc
    return oh


class Sched:
    def __init__(self, nc, es):
        self.nc = nc
        self.es = es
        self.eng = {"pe": nc.tensor, "act": nc.scalar, "dve": nc.vector, "pool": nc.gpsimd, "sp": nc.sync}
        self.sem = {}
        self.cnt = {}
        self.nsem = 0
        self.pe_sems = set()
        for e in self.eng:
            self._new_sem(e)
        self.dma_sems = []
        for i in range(48):
            self.dma_sems.append([es.enter_context(nc.semaphore("dq%d" % i)), 0])
        self.dma_rr = 0
        self.waited = {e: {} for e in self.eng}
        self.lastw = {}
        self.readers = {}
        self.ninstr = 0
        self.nwait = 0

    def _new_sem(self, e):
        self.sem[e] = self.es.enter_context(self.nc.semaphore("s_%s_%d" % (e, self.nsem)))
        if e == "pe":
            self.pe_sems.add(id(self.sem[e]))
        self.nsem += 1
        self.cnt[e] = 0

    def _wait(self, e, dep):
        sem, val = dep
        if e == "pe" and id(sem) in self.pe_sems:
            return
        w = self.waited[e]
        k = id(sem)
        if w.get(k, 0) >= val:
            return
        w[k] = val
        self.eng[e].wait_ge(sem, val)
        self.nwait += 1

    def _deps(self, e, reads, writes):
        for k in reads:
            d = self.lastw.get(k)
            if d is not None:
                self._wait(e, d)
        for k in writes:
            d = self.lastw.get(k)
            if d is not None:
                self._wait(e, d)
            r = self.readers.get(k)
            if r:
                for d in r.values():
                    self._wait(e, d)

    def _commit(self, tok, reads, writes):
        for k in writes:
            self.lastw[k] = tok
            self.readers[k] = {}
        for k in reads:
            r = self.readers.setdefault(k, {})
            r[id(tok[0])] = tok

    def op(self, e, fn, reads=(), writes=()):
        self._deps(e, reads, writes)
        if self.cnt[e] >= 30000:
            self._new_sem(e)
        inst = fn(self.eng[e])
        inst.then_inc(self.sem[e], 1)
        self.cnt[e] += 1
        self.ninstr += 1
        tok = (self.sem[e], self.cnt[e])
        self._commit(tok, reads, writes)

    def dma(self, e, out, in_, reads=(), writes=(), slow=False):
        self._deps(e, reads, writes)
        slot = self.dma_sems[self.dma_rr]
        self.dma_rr = (self.dma_rr + 1) % len(self.dma_sems)
        if slot[1] > 0:
            self._wait(e, (slot[0], slot[1]))
        if slot[1] >= 30000:
            slot[0] = self.es.enter_context(self.nc.semaphore("dqx%d" % self.nsem))
            self.nsem += 1
            slot[1] = 0
        if slow:
            self.eng[e].dma_start(out=out, in_=in_, allow_slow_non_contiguous=True).then_inc(slot[0], 16)
        else:
            self.eng[e].dma_start(out=out, in_=in_).then_inc(slot[0], 16)
        slot[1] += 16
        self.ninstr += 1
        tok = (slot[0], slot[1])
        self._commit(tok, reads, writes)


def build_nc(n_ptiles, n_samp, depth=4, mm_r=True, dbg=None):
    nc = bass.Bass("TRN2", target_bir_lowering=False)
    SEQ = n_ptiles * 128
    NHYB = (depth + 1) // 2
    NGDN = depth // 2
    NG1 = max(NGDN, 1)
    NS1 = max(n_samp, 1)
    KEEP = min(WIN, SEQ)
    MMDT = F32R if mm_r else F32
    wq = "pool" if mm_r else "sp"

    def din(name, shape):
        return nc.dram_tensor(name, list(shape), F32, kind="ExternalInput")

    def dout(name, shape):
        return nc.dram_tensor(name, list(shape), F32, kind="ExternalOutput")

    def dscr(name, shape):
        return nc.dram_tensor(name, list(shape), F32, kind="Internal")

    x_prompt = din("x_prompt", [SEQ, D])
    x_sample = din("x_sample", [NS1, D])
    cache_k = din("cache_k", [NHYB, NS1, WIN, A_W])
    cache_v = din("cache_v", [NHYB, NS1, WIN, A_W])
    st_sconv = din("st_sconv", [NHYB, NS1, 3, SSM_XBC])
    st_ssm = din("st_ssm", [NHYB, NS1, 1024, 128])
    st_gconv = din("st_gconv", [NG1, NS1, 3, G_QKV])
    st_gdn = din("st_gdn", [NG1, NS1, 16, 128, 128])
    rel_bias = din("rel_bias", [32, 8])
    oh_tab = din("oh_tab", [33, TABL])
    norm_mix_pre = din("norm_mix_pre", [depth, D])
    norm_mix_post = din("norm_mix_post", [depth, D])
    norm_ffn_pre = din("norm_ffn_pre", [depth, D])
    norm_ffn_post = din("norm_ffn_post", [depth, D])
    w_hyb_in = din("w_hyb_in", [NHYB, D, HYB_IN])
    ssm_conv_w = din("ssm_conv_w", [NHYB, 4, SSM_XBC])
    ssm_conv_b = din("ssm_conv_b", [NHYB, SSM_XBC])
    ssm_dt_bias = din("ssm_dt_bias", [NHYB, 16])
    ssm_a_log = din("ssm_a_log", [NHYB, 16])
    ssm_d = din("ssm_d", [NHYB, 16])
    ssm_norm_w = din("ssm_norm_w", [NHYB, SSM_DI])
    w_hyb_out = din("w_hyb_out", [NHYB, HYB_MIX, D])
    w_gdn_in = din("w_gdn_in", [NG1, D, G_IN])
    gdn_conv_w = din("gdn_conv_w", [NG1, 4, G_QKV])
    gdn_dt_bias = din("gdn_dt_bias", [NG1, 16])
    gdn_a_log = din("gdn_a_log", [NG1, 16])
    gdn_norm_w = din("gdn_norm_w", [NG1, 128])
    w_gdn_out = din("w_gdn_out", [NG1, G_VW, D])
    w_ffn_gate = din("w_ffn_gate", [depth, D, D_FF])
    w_ffn_up = din("w_ffn_up", [depth, D, D_FF])
    w_ffn_down = din("w_ffn_down", [depth, D_FF, D])

    y_prompt = dout("y_prompt", [SEQ, D])
    y_sample = dout("y_sample", [NS1, D])
    o_pk = dout("o_pk", [NHYB, KEEP, A_W])
    o_pv = dout("o_pv", [NHYB, KEEP, A_W])
    o_psc = dout("o_psc", [NHYB, 3, SSM_XBC])
    o_pss = dout("o_pss", [NHYB, 1024, 128])
    o_pgc = dout("o_pgc", [NG1, 3, G_QKV])
    o_pgs = dout("o_pgs", [NG1, 16, 128, 128])
    o_sk = dout("o_sk", [NHYB, NS1, WIN, A_W])
    o_sv = dout("o_sv", [NHYB, NS1, WIN, A_W])
    o_ssc = dout("o_ssc", [NHYB, NS1, 3, SSM_XBC])
    o_sss = dout("o_sss", [NHYB, NS1, 1024, 128])
    o_sgc = dout("o_sgc", [NG1, NS1, 3, G_QKV])
    o_sgs = dout("o_sgs", [NG1, NS1, 16, 128, 128])
    dbg_out = {}
    if dbg:
        for nm, shp in dbg.items():
            dbg_out[nm] = dout("dbg_" + nm, shp)

    seqs = [("p", 0, 0, n_ptiles)] + [("s", s, 16, 1) for s in range(n_samp)]
    NSEQ = len(seqs)
    SCR_ROWS = max(SEQ, WIN + 128)
    kt_scr = dscr("kt_scr", [NHYB, NSEQ, 128, 4, SCR_ROWS])
    k_scr = dscr("k_scr", [NHYB, NSEQ, SCR_ROWS, A_W])
    v_scr = dscr("v_scr", [NHYB, NSEQ, SCR_ROWS, A_W])
    ftab = dscr("ftab", [8, TABL])

    es = ExitStack()
    with es:
        S = Sched(nc, es)

        sb_total = [0]

        def sb(name, shape, dt=F32):
            sb_total[0] += int(np.prod(shape[1:])) * 4
            return es.enter_context(nc.sbuf_tensor(name, list(shape), dt))

        psum = [es.enter_context(nc.psum_tensor("ps%d" % i, [128, 512], F32)) for i in range(8)]
        ps_rr = [0]

        def ps():
            i = ps_rr[0]
            ps_rr[0] = (i + 1) % 6
            return psum[i], "ps%d" % i

        acc_rr = [0]

        def ps_acc():
            acc_rr[0] ^= 1
            i = 6 + acc_rr[0]
            return psum[i], "ps%d" % i

        ev_rr = [0]

        def evac_eng():
            ev_rr[0] ^= 1
            return "act" if ev_rr[0] else "dve"

        def copy(e, out, in_, reads, writes):
            if e == "act":
                S.op("act", lambda g: g.activation(out=out, in_=in_, func=AF.Copy), reads, writes)
            else:
                S.op(e, lambda g: g.tensor_copy(out, in_), reads, writes)

        def dump(name, ap, keys):
            if name in dbg_out:
                t = dbg_out[name]
                S.dma("sp", t.ap() if hasattr(t, "ap") else t[:], ap, keys, ["dbg_" + name])

        def mask_const(name, pattern, op, base, cm):
            t = sb(name, [128, 128])
            S.op("pool", lambda g: g.memset(t[:], 1.0), [], [name])
            S.op("pool", lambda g: g.affine_select(out=t[:], in_=t[:], pattern=pattern, compare_op=op, fill=0.0,
                                                   base=base, channel_multiplier=cm), [name], [name])
            return t

        ident = mask_const("ident", [[-1, 128]], ALU.is_equal, 0, 1)
        antiI = mask_const("antiI", [[1, 128]], ALU.is_equal, -127, 1)
        Umat = mask_const("Umat", [[1, 128]], ALU.is_ge, 0, -1)
        SUmat = mask_const("SUmat", [[1, 128]], ALU.is_gt, 0, -1)
        SLmat = mask_const("SLmat", [[-1, 128]], ALU.is_gt, 0, 1)
        ones = sb("ones", [128, 128])
        S.op("pool", lambda g: g.memset(ones[:], 1.0), [], ["ones"])
        ccol = sb("ccol", [128, 4])
        S.op("pool", lambda g: g.memset(ccol[:, 0:1], EPS), [], ["ccol"])
        S.op("pool", lambda g: g.memset(ccol[:, 1:2], 1.0), ["ccol"], ["ccol"])
        S.op("pool", lambda g: g.memset(ccol[:, 2:3], 0.0), ["ccol"], ["ccol"])
        c_eps = ccol[:, 0:1]
        c_one = ccol[:, 1:2]
        CONSTS = ["ident", "antiI", "Umat", "SUmat", "SLmat", "ones", "ccol"]

        xres = sb("xres", [128, D])
        hT = sb("hT", [128, 8, 128], MMDT)
        stat = sb("stat", [128, 8])
        WBUF = 4096
        NWB = 3
        wbuf = [sb("wbuf%d" % i, [128, WBUF], MMDT) for i in range(NWB)]
        wb_rr = [0]
        big = sb("big", [128, 6176])
        mixed = sb("mixed", [128, 2048])
        mixT = sb("mixT", [128, 22, 128], MMDT)
        hTr = mixT[:, 12:20, :]
        valid = sb("valid", [128, 1])
        BIGK = [("big", i) for i in range(13)]

        def bk(c0, c1):
            return [("big", i) for i in range(c0 // 512, (c1 - 1) // 512 + 1)]

        conv_h = [sb("conv_h%d" % i, [128, 12, 3]) for i in range(NHYB)]
        ssm_st = [sb("ssm_st%d" % i, [128, 1024]) for i in range(NHYB)]
        gconv_h = [sb("gconv_h%d" % i, [128, 32, 3]) for i in range(NGDN)]
        gdn_st = [sb("gdn_st%d" % i, [128, 16, 128]) for i in range(NGDN)]

        qT = sb("qT", [128, 4, 128])
        kTt = sb("kTt", [128, 4, 128])
        xbcT = sb("xbcT", [128, 12, 131])
        xbcA = sb("xbcA", [128, 12, 128])
        dtb = sb("dtb", [128, 16])
        alog = sb("alog", [128, 16])
        dsk = sb("dsk", [128, 16])
        dtt = sb("dtt", [128, 16])
        av = sb("av", [128, 16])
        csb = sb("csb", [128, 16])
        dec_b = sb("dec_b", [128, 16])
        dec_w = sb("dec_w", [128, 16])
        beta = sb("beta", [128, 16])
        rn = sb("rn", [128, 16])
        seg = sb("seg", [128, 16, 128])
        cbm = sb("cbm", [128, 2, 128])
        xtok = sb("xtok", [128, 1024])
        xw = sb("xw", [128, 1024])
        btok = sb("btok", [128, 256])
        ysb = sb("ysb", [128, 1024])
        ktw = sb("ktw", [128, NKT * 128])
        vw = sb("vw", [128, NKT, 128])
        Lb = sb("Lb", [128, NKT * 128])
        aU = Lb[:, 0:2048].rearrange("p (h i) -> p h i", i=128)
        Bb = sb("Bb", [128, NKT * 128])
        ETb = sb("ETb", [128, 4, 128])
        mx = sb("mx", [128, 4])
        xn = Lb[:, 0:1024]
        gpost = Bb[:, 0:1024]
        stage = sb("stage", [128, 4, 128])
        SEGK = [("seg", q4) for q4 in range(4)]
        tm3 = big

        rb = big[0:33, 0:8]
        ohs = big[0:33, 512:512 + TABL]
        fsb = big[0:8, 3072:3072 + TABL]
        S.dma("sp", big[0:32, 0:8], rel_bias[:, :], [], [("big", 0)])
        S.op("pool", lambda g: g.memset(big[32:33, 0:8], 1.0), [], [("big", 0)])
        S.dma("sp", ohs, oh_tab[:, :], [], bk(512, 512 + TABL))
        for c0 in range(0, TABL, 512):
            cw = min(512, TABL - c0)
            p_, pk_ = ps()
            S.op("pe", lambda g: g.matmul(p_[0:8, 0:cw], rb, big[0:33, 512 + c0:512 + c0 + cw], start=True, stop=True),
                 BIGK, [pk_])
            copy("dve", big[0:8, 3072 + c0:3072 + c0 + cw], p_[0:8, 0:cw], [pk_], bk(3072 + c0, 3072 + c0 + cw))
        S.dma("sp", ftab[:, :], fsb, BIGK, ["ftab"])

        gv_all = sb("gv_all", [128, depth * 2, 8])
        for l_ in range(depth):
            S.dma("sp", gv_all[:, 2 * l_, :], bass.AP(norm_mix_pre, l_ * D, [[1, 128], [128, 8]]), [], ["gv_all"], slow=True)
            S.dma("sp", gv_all[:, 2 * l_ + 1, :], bass.AP(norm_ffn_pre, l_ * D, [[1, 128], [128, 8]]), [], ["gv_all"], slow=True)
        cwh_all = sb("cwh_all", [128, NHYB, 12, 4])
        cbh_all = sb("cbh_all", [128, NHYB, 12])
        for i_ in range(NHYB):
            for j_ in range(12):
                S.dma("sp", cwh_all[:, i_, j_, :], bass.AP(ssm_conv_w, i_ * 4 * SSM_XBC + j_ * 128, [[1, 128], [SSM_XBC, 4]]), [], ["cwh_all"], slow=True)
            S.dma("sp", cbh_all[:, i_, :], bass.AP(ssm_conv_b, i_ * SSM_XBC, [[1, 128], [128, 12]]), [], ["cbh_all"], slow=True)
        cwg_all = sb("cwg_all", [128, NG1, 32, 4])
        for i_ in range(NGDN):
            for j_ in range(32):
                S.dma("sp", cwg_all[:, i_, j_, :], bass.AP(gdn_conv_w, i_ * 4 * G_QKV + j_ * 128, [[1, 128], [G_QKV, 4]]), [], ["cwg_all"], slow=True)

        def load_bcast(dst, dkey, src_tensor, off, n):
            S.dma("sp", dst, bass.AP(src_tensor, off, [[0, 128], [1, n]]), [], [dkey])

        def rstd_from_ssq(col, n):
            S.op("act", lambda g: g.activation(out=stat[:, col:col + 1], in_=stat[:, col:col + 1], func=AF.Ln, bias=c_eps, scale=1.0 / n),
                 [("stat", col), "ccol"], [("stat", col)])
            S.op("act", lambda g: g.activation(out=stat[:, col:col + 1], in_=stat[:, col:col + 1], func=AF.Exp, scale=-0.5),
                 [("stat", col)], [("stat", col)])

        def rmsnorm_stats(src, skeys, n, col):
            S.op("act", lambda g: g.activation(out=Lb[:, 1024:1024 + n], in_=src, func=AF.Square, accum_out=stat[:, col:col + 1]),
                 skeys, ["Lb", ("stat", col)])
            rstd_from_ssq(col, n)

        def prenorm(which, l, need_rev):
            gvec = gv_all[:, 2 * l + which, :]
            rmsnorm_stats(xres[:, :], ["xres"], D, 0)
            S.op("dve", lambda g: g.tensor_scalar(xn[:, :], xres[:, :], stat[:, 0:1], None, ALU.mult),
                 ["xres", ("stat", 0)], ["Lb"])
            for rev in ([False, True] if need_rev else [False]):
                dst = hTr if rev else hT
                dk = "hT"
                ko = 12 if rev else 0
                dk = "mixT" if rev else "hT"
                for kc in range(8):
                    p_, pk_ = ps()
                    if rev:
                        S.op("pe", lambda g: g.matmul(p_[:, 0:128], xn[:, kc * 128:(kc + 1) * 128], antiI[:], start=True, stop=True),
                             ["Lb", "antiI"], [pk_])
                    else:
                        S.op("pe", lambda g: g.transpose(p_[:, 0:128], xn[:, kc * 128:(kc + 1) * 128], ident[:]), ["Lb", "ident"], [pk_])
                    if kc % 2:
                        S.op("dve", lambda g: g.tensor_scalar(dst[:, kc, :], p_[:, 0:128], gvec[:, kc:kc + 1], None, ALU.mult),
                             [pk_, "gv_all"], [(dk, ko + kc)])
                    else:
                        S.op("act", lambda g: g.activation(out=dst[:, kc, :], in_=p_[:, 0:128], func=AF.Copy, scale=gvec[:, kc:kc + 1]),
                             [pk_, "gv_all"], [(dk, ko + kc)])

        def load_w(W, l, K, N, c0, cw):
            KC = K // 128
            i = wb_rr[0]
            wb_rr[0] = (i + 1) % NWB
            wv = wbuf[i][:, 0:KC * cw].rearrange("p (k c) -> p k c", c=cw)
            src = bass.AP(W, l * K * N + c0, [[N, 128], [128 * N, KC], [1, cw]])
            S.dma(wq, wv, src, [], ["wbuf%d" % i])
            return wv, "wbuf%d" % i

        def proj_tok(W, l, K, N, c0, c1, src, skeys, evac):
            KC = K // 128
            cwmax = 512 if KC * 512 <= WBUF else (256 if KC * 256 <= WBUF else 128)
            c = c0
            while c < c1:
                cw = min(cwmax, c1 - c)
                wv, wk = load_w(W, l, K, N, c, cw)
                p_, pk_ = ps()
                for kc in range(KC):
                    S.op("pe", lambda g: g.matmul(p_[:, 0:cw], src[:, kc, :], wv[:, kc, :], start=(kc == 0), stop=(kc == KC - 1)),
                         [wk] + skeys, [pk_])
                evac(p_, pk_, c, cw)
                c += cw

        def proj_feat(W, l, K, N, c0, ncols, src, skeys, evac):
            KC = K // 128
            cwmax = 512 if KC * 512 <= WBUF else (256 if KC * 256 <= WBUF else 128)
            c = c0
            while c < c0 + ncols:
                cw = min(cwmax, c0 + ncols - c)
                wv, wk = load_w(W, l, K, N, c, cw)
                for j in range(cw // 128):
                    p_, pk_ = ps()
                    for kc in range(KC):
                        S.op("pe", lambda g: g.matmul(p_[:, 0:128], wv[:, kc, j * 128:(j + 1) * 128], src[:, kc, :],
                                                      start=(kc == 0), stop=(kc == KC - 1)), [wk] + skeys, [pk_])
                    evac(p_, pk_, (c - c0) // 128 + j)
                c += cw

        hkeys = [("hT", k) for k in range(8)]
        hrkeys = [("mixT", 12 + k) for k in range(8)]

        def post_norm_residual(gain_dram, l, src, skeys):
            load_bcast(gpost[:, :], "Bb", gain_dram, l * D, D)
            rmsnorm_stats(src, skeys, D, 1)
            S.op("dve", lambda g: g.scalar_tensor_tensor(out=xn[:, :], in0=src, scalar=stat[:, 1:2], in1=gpost[:, :],
                                                         op0=ALU.mult, op1=ALU.mult), skeys + [("stat", 1), "Bb"], ["Lb"])
            S.op("pool", lambda g: g.tensor_tensor(out=xres[:, :], in0=xres[:, :], in1=xn[:, :], op=ALU.add),
                 ["xres", "Lb"], ["xres"])

        def transposes_to(dst, dkey, j0, src, skeys, n, dtcast=True):
            for c in range(n):
                p_, pk_ = ps()
                S.op("pe", lambda g: g.transpose(p_[:, 0:128], src[:, c * 128:(c + 1) * 128], ident[:]), skeys + ["ident"], [pk_])
                copy(evac_eng(), dst[:, j0 + c, :], p_[:, 0:128], [pk_], [(dkey, j0 + c)])

        def ffn(l):
            prenorm(1, l, False)

            def ev_gate(p_, pk_, c, cw):
                S.op("act", lambda g: g.activation(out=big[:, c:c + cw], in_=p_[:, 0:cw], func=AF.Silu), [pk_], bk(c, c + cw))
            proj_tok(w_ffn_gate, l, D, D_FF, 0, D_FF, hT, hkeys, ev_gate)

            def ev_up(p_, pk_, c, cw):
                S.op("dve", lambda g: g.tensor_tensor(out=big[:, c:c + cw], in0=big[:, c:c + cw], in1=p_[:, 0:cw], op=ALU.mult),
                     [pk_] + bk(c, c + cw), bk(c, c + cw))
            proj_tok(w_ffn_up, l, D, D_FF, 0, D_FF, hT, hkeys, ev_up)
            transposes_to(mixT, "mixT", 0, big, bk(0, D_FF), 22)

            def ev_down(p_, pk_, c, cw):
                copy(evac_eng(), mixed[:, c:c + cw], p_[:, 0:cw], [pk_], [("mixed", c // 128)])
            proj_tok(w_ffn_down, l, D_FF, D, 0, D, mixT, [("mixT", c) for c in range(22)], ev_down)
            post_norm_residual(norm_ffn_post, l, mixed[:, 0:D], [("mixed", c) for c in range(8)])

        def decay_mats(gsrc, gkey):
            p_, pk_ = ps()
            S.op("pe", lambda g: g.matmul(p_[:, 0:16], Umat[:, :], gsrc, start=True, stop=True), ["Umat", gkey], [pk_])
            S.op("act", lambda g: g.activation(out=csb[:, :], in_=p_[:, 0:16], func=AF.Exp), [pk_], ["csb"])
            p_, pk_ = ps()
            S.op("pe", lambda g: g.matmul(p_[:, 0:16], ones[:, :], gsrc, start=True, stop=True), ["ones", gkey], [pk_])
            S.op("act", lambda g: g.activation(out=dec_b[:, :], in_=p_[:, 0:16], func=AF.Exp), [pk_], ["dec_b"])
            S.op("pool", lambda g: g.tensor_tensor(out=aU[:, :, :], in0=Umat[:, :].unsqueeze(1).to_broadcast([128, 16, 128]),
                                                   in1=gsrc.unsqueeze(2).to_broadcast([128, 16, 128]), op=ALU.mult),
                 ["Umat", gkey], ["Lb"])
            for q4 in range(4):
                p_, pk_ = ps()
                S.op("pe", lambda g: g.matmul(p_[:, 0:512], SLmat[:, :], aU[:, q4 * 4:(q4 + 1) * 4, :].rearrange("p h i -> p (h i)"),
                                              start=True, stop=True), ["SLmat", "Lb"], [pk_])
                S.op("act", lambda g: g.activation(out=seg[:, q4 * 4:(q4 + 1) * 4, :].rearrange("p h i -> p (h i)"), in_=p_[:, 0:512], func=AF.Exp),
                     [pk_], [("seg", q4)])
            S.op("dve", lambda g: g.tensor_copy(dec_w[:, :], seg[:, :, 127]), SEGK, ["dec_w"])

        def softplus_inplace(t, key):
            S.op("act", lambda g: g.activation(out=t, in_=t, func=AF.Exp), [key], [key])
            S.op("act", lambda g: g.activation(out=t, in_=t, func=AF.Ln, bias=c_one, scale=1.0), [key, "ccol"], [key])

        def conv_silu(nct, cwv, cwkey, hist, hkey, nvalid, bias_sb):
            xk = [("xbcT", j) for j in range(nct)]
            S.op("pool", lambda g: g.tensor_copy(xbcT[:, 0:nct, 0:3], hist), [hkey] + xk, xk)
            S.op("pool", lambda g: g.tensor_copy(hist, xbcT[:, 0:nct, nvalid:nvalid + 3]), xk, [hkey])
            for j in range(nct):
                S.op("dve", lambda g: g.tensor_scalar(xbcA[:, j, :], xbcT[:, j, 0:128], cwv[:, j, 0:1], None, ALU.mult),
                     [("xbcT", j), cwkey], [("xbcA", j)])
                for t in range(1, 4):
                    S.op("dve", lambda g: g.scalar_tensor_tensor(out=xbcA[:, j, :], in0=xbcT[:, j, t:t + 128], scalar=cwv[:, j, t:t + 1],
                                                                 in1=xbcA[:, j, :], op0=ALU.mult, op1=ALU.add),
                         [("xbcT", j), ("xbcA", j), cwkey], [("xbcA", j)])
                if bias_sb is not None:
                    S.op("act", lambda g: g.activation(out=xbcA[:, j, :], in_=xbcA[:, j, :], func=AF.Silu, bias=bias_sb[:, j:j + 1], scale=1.0),
                         [("xbcA", j), "cbh_all"], [("xbcA", j)])
                else:
                    S.op("act", lambda g: g.activation(out=xbcA[:, j, :], in_=xbcA[:, j, :], func=AF.Silu), [("xbcA", j)], [("xbcA", j)])

        def hybrid_load_params(i):
            load_bcast(dtb[:, :], "dtb", ssm_dt_bias, i * 16, 16)
            load_bcast(alog[:, :], "alog", ssm_a_log, i * 16, 16)
            S.op("act", lambda g: g.activation(out=alog[:, :], in_=alog[:, :], func=AF.Exp), ["alog"], ["alog"])
            S.op("dve", lambda g: g.tensor_scalar(alog[:, :], alog[:, :], -1.0, None, ALU.mult), ["alog"], ["alog"])
            load_bcast(dsk[:, :], "dsk", ssm_d, i * 16, 16)

        def hybrid_layer(l, i, sq, tpos, nvalid):
            full = (nvalid == 128)
            cst = conv_h[i]
            sst = ssm_st[i]
            ck = "conv_h%d" % i
            sk = "ssm_st%d" % i
            prenorm(0, l, True)

            def ev_q(p_, pk_, j):
                copy(evac_eng(), qT[:, j, :], p_[:, 0:128], [pk_], [("qT", j)])
            proj_feat(w_hyb_in, i, D, HYB_IN, 0, 512, hTr, hrkeys, ev_q)

            def ev_kT(p_, pk_, j):
                copy(evac_eng(), kTt[:, j, :], p_[:, 0:128], [pk_], [("kTt", j)])
            proj_feat(w_hyb_in, i, D, HYB_IN, 512, 512, hT, hkeys, ev_kT)
            scr_base = (i * NSEQ + sq) * 128 * 4 * SCR_ROWS
            S.dma("sp", bass.AP(kt_scr, scr_base + tpos * 128, [[4 * SCR_ROWS, 128], [SCR_ROWS, 4], [1, 128]]),
                  kTt[:, :, :], [("kTt", j) for j in range(4)], [("kt_scr", i, sq)])

            def ev_kvz(p_, pk_, c, cw):
                copy(evac_eng(), big[:, c - 512:c - 512 + cw], p_[:, 0:cw], [pk_], bk(c - 512, c - 512 + cw))
            proj_tok(w_hyb_in, i, D, HYB_IN, 512, 2560, hT, hkeys, ev_kvz)
            if not full:
                S.op("dve", lambda g: g.tensor_scalar(big[:, 0:1024], big[:, 0:1024], valid[:, 0:1], None, ALU.mult),
                     bk(0, 1024) + ["valid"], bk(0, 1024))
            S.dma("sp", k_scr[i, sq, tpos * 128:(tpos + 1) * 128, :], big[:, 0:512], bk(0, 512), [("k_scr", i, sq)])
            S.dma("sp", v_scr[i, sq, tpos * 128:(tpos + 1) * 128, :], big[:, 512:1024], bk(512, 1024), [("v_scr", i, sq)])

            def ev_dt(p_, pk_, c, cw):
                copy("dve", big[:, 2048:2064], p_[:, 0:16], [pk_], bk(2048, 2064))
            proj_tok(w_hyb_in, i, D, HYB_IN, 4096, 4112, hT, hkeys, ev_dt)

            def ev_xbc(p_, pk_, j):
                copy(evac_eng(), xbcT[:, j, 3:131], p_[:, 0:128], [pk_], [("xbcT", j)])
            proj_feat(w_hyb_in, i, D, HYB_IN, 2560, 1536, hT, hkeys, ev_xbc)

            t_lo = max(0, tpos - 16)
            nk = tpos - t_lo + 1
            W_ = nk * 128
            off_c = (NKT - nk) * 128
            for pr in range(4):
                S.dma("sp", ktw[:, 0:W_], bass.AP(kt_scr, scr_base + pr * SCR_ROWS + t_lo * 128, [[4 * SCR_ROWS, 128], [1, W_]]),
                      [("kt_scr", i, sq)], ["ktw"])
                S.dma("sp", vw[:, 0:nk, :], bass.AP(v_scr, ((i * NSEQ + sq) * SCR_ROWS + t_lo * 128) * A_W + pr * 128,
                                                    [[A_W, 128], [128 * A_W, nk], [1, 128]]), [("v_scr", i, sq)], ["vw"])
                for hh in range(2):
                    h = pr * 2 + hh
                    pb = hh * 64
                    S.dma("sp", Bb[:, 0:W_], bass.AP(ftab, h * TABL + off_c, [[1, 128], [1, W_]]), ["ftab"], ["Bb"])
                    for c0 in range(0, W_, 512):
                        cw = min(512, W_ - c0)
                        p_, pk_ = ps()
                        S.op("pe", lambda g: g.matmul(p_[:, 0:cw], qT[pb:pb + 64, pr, :], ktw[pb:pb + 64, c0:c0 + cw], start=True, stop=True),
                             [("qT", pr), "ktw"], [pk_])
                        S.op("dve", lambda g: g.scalar_tensor_tensor(out=Lb[:, c0:c0 + cw], in0=p_[:, 0:cw], scalar=0.125, in1=Bb[:, c0:c0 + cw],
                                                                     op0=ALU.mult, op1=ALU.add), [pk_, "Bb"], ["Lb"])
                    S.op("dve", lambda g: g.reduce_max(out=mx[:, 0:1], in_=Lb[:, 0:W_], axis=AX.X), ["Lb"], ["mx0"])
                    S.op("dve", lambda g: g.tensor_scalar(mx[:, 1:2], mx[:, 0:1], -1.0, None, ALU.mult), ["mx0"], ["mx1"])
                    S.op("act", lambda g: g.activation(out=Lb[:, 0:W_], in_=Lb[:, 0:W_], func=AF.Exp, bias=mx[:, 1:2], scale=1.0,
                                                       accum_out=mx[:, 2:3]), ["Lb", "mx1"], ["Lb", "mx2"])
                    S.op("dve", lambda g: g.reciprocal(mx[:, 3:4], mx[:, 2:3]), ["mx2"], ["mx3"])
                    po, pok = ps_acc()
                    for kt in range(nk):
                        p_, pk_ = ps()
                        S.op("pe", lambda g: g.transpose(p_[:, 0:128], Lb[:, kt * 128:(kt + 1) * 128], ident[:]), ["Lb", "ident"], [pk_])
                        copy(evac_eng(), ETb[:, kt % 4, :], p_[:, 0:128], [pk_], [("ETb", kt % 4)])
                        S.op("pe", lambda g: g.matmul(po[:, 0:64], ETb[:, kt % 4, :], vw[:, kt, pb:pb + 64], start=(kt == 0), stop=(kt == nk - 1)),
                             [("ETb", kt % 4), "vw"], [pok])
                    S.op("dve", lambda g: g.tensor_scalar(mixed[:, h * 64:(h + 1) * 64], po[:, 0:64], mx[:, 3:4], None, ALU.mult),
                         [pok, "mx3"], [("mixed", h // 2)])
            for c in range(4):
                p_, pk_ = ps()
                S.op("pe", lambda g: g.matmul(p_[:, 0:128], mixed[:, c * 128:(c + 1) * 128], antiI[:], start=True, stop=True),
                     [("mixed", c), "antiI"], [pk_])
                copy(evac_eng(), mixT[:, c, :], p_[:, 0:128], [pk_], [("mixT", c)])

            conv_silu(12, cwh_all[:, i, :, :], "cwh_all", cst[:, :, :], ck, nvalid, cbh_all[:, i, :])
            for c in range(8):
                p_, pk_ = ps()
                S.op("pe", lambda g: g.transpose(p_[:, 0:128], xbcA[:, c, :], ident[:]), [("xbcA", c), "ident"], [pk_])
                copy(evac_eng(), xtok[:, c * 128:(c + 1) * 128], p_[:, 0:128], [pk_], [("xtok", c)])
            for c in range(2):
                p_, pk_ = ps()
                S.op("pe", lambda g: g.transpose(p_[:, 0:128], xbcA[:, 8 + c, :], ident[:]), [("xbcA", 8 + c), "ident"], [pk_])
                copy(evac_eng(), btok[:, c * 128:(c + 1) * 128], p_[:, 0:128], [pk_], [("btok", c)])
            xtk = [("xtok", c) for c in range(8)]
            S.op("dve", lambda g: g.tensor_tensor(out=dtt[:, :], in0=big[:, 2048:2064], in1=dtb[:, :], op=ALU.add), bk(2048, 2064) + ["dtb"], ["dtt"])
            softplus_inplace(dtt[:, :], "dtt")
            if not full:
                S.op("dve", lambda g: g.tensor_scalar(dtt[:, :], dtt[:, :], valid[:, 0:1], None, ALU.mult), ["dtt", "valid"], ["dtt"])
                S.op("dve", lambda g: g.tensor_scalar(xtok[:, :], xtok[:, :], valid[:, 0:1], None, ALU.mult), xtk + ["valid"], xtk)
            S.op("dve", lambda g: g.tensor_tensor(out=av[:, :], in0=dtt[:, :], in1=alog[:, :], op=ALU.mult), ["dtt", "alog"], ["av"])
            decay_mats(av[:, :], "av")
            S.op("dve", lambda g: g.tensor_tensor(out=dec_w[:, :], in0=dec_w[:, :], in1=dtt[:, :], op=ALU.mult), ["dec_w", "dtt"], ["dec_w"])
            S.op("pool", lambda g: g.tensor_tensor(out=xw[:, :].rearrange("p (h d) -> p h d", d=64), in0=xtok[:, :].rearrange("p (h d) -> p h d", d=64),
                                                   in1=dec_w[:, :].unsqueeze(2).to_broadcast([128, 16, 64]), op=ALU.mult), xtk + ["dec_w"], ["xw"])
            for g_ in range(2):
                p_, pk_ = ps()
                S.op("pe", lambda g: g.matmul(p_[:, 0:128], xbcA[:, 8 + g_, :], xbcA[:, 10 + g_, :], start=True, stop=True),
                     [("xbcA", 8 + g_), ("xbcA", 10 + g_)], [pk_])
                S.op("dve", lambda g: g.tensor_tensor(out=cbm[:, g_, :], in0=p_[:, 0:128], in1=Umat[:, :], op=ALU.mult), [pk_, "Umat"], [("cbm", g_)])
            S.op("dve", lambda g: g.tensor_tensor(out=seg[:, :, :], in0=seg[:, :, :], in1=dtt[:, :].unsqueeze(2).to_broadcast([128, 16, 128]), op=ALU.mult),
                 SEGK + ["dtt", "dec_w"], SEGK)
            for g_ in range(2):
                S.op("pool", lambda g: g.tensor_tensor(out=seg[:, g_ * 8:(g_ + 1) * 8, :], in0=seg[:, g_ * 8:(g_ + 1) * 8, :],
                                                       in1=cbm[:, g_, :].unsqueeze(1).to_broadcast([128, 8, 128]), op=ALU.mult),
                     SEGK + [("cbm", g_)], SEGK)
            for g_ in range(2):
                p_, pk_ = ps()
                S.op("pe", lambda g: g.matmul(p_[:, 0:512], xbcA[:, 10 + g_, :], sst[:, g_ * 512:(g_ + 1) * 512], start=True, stop=True),
                     [("xbcA", 10 + g_), sk], [pk_])
                S.op("dve", lambda g: g.tensor_tensor(out=ysb[:, g_ * 512:(g_ + 1) * 512].rearrange("p (h d) -> p h d", d=64),
                                                      in0=p_[:, 0:512].rearrange("p (h d) -> p h d", d=64),
                                                      in1=csb[:, g_ * 8:(g_ + 1) * 8].unsqueeze(2).to_broadcast([128, 8, 64]), op=ALU.mult),
                     [pk_, "csb"], [("ysb", g_)])
            for g_ in range(2):
                p_, pk_ = ps()
                for hh in range(8):
                    h = g_ * 8 + hh
                    S.op("pe", lambda g: g.matmul(p_[:, hh * 64:(hh + 1) * 64], seg[:, h, :], xtok[:, h * 64:(h + 1) * 64], start=True, stop=True),
                         SEGK + xtk, [pk_])
                S.op("dve", lambda g: g.tensor_tensor(out=ysb[:, g_ * 512:(g_ + 1) * 512], in0=ysb[:, g_ * 512:(g_ + 1) * 512], in1=p_[:, 0:512], op=ALU.add),
                     [pk_, ("ysb", g_)], [("ysb", g_)])
            for g_ in range(2):
                p_, pk_ = ps()
                S.op("pe", lambda g: g.matmul(p_[:, 0:512], btok[:, g_ * 128:(g_ + 1) * 128], xw[:, g_ * 512:(g_ + 1) * 512], start=True, stop=True),
                     [("btok", g_), "xw"], [pk_])
                S.op("pool", lambda g: g.tensor_tensor(out=sst[:, g_ * 512:(g_ + 1) * 512].rearrange("p (h d) -> p h d", d=64),
                                                       in0=sst[:, g_ * 512:(g_ + 1) * 512].rearrange("p (h d) -> p h d", d=64),
                                                       in1=dec_b[:, g_ * 8:(g_ + 1) * 8].unsqueeze(2).to_broadcast([128, 8, 64]), op=ALU.mult),
                     [sk, "dec_b"], [sk])
                S.op("dve", lambda g: g.tensor_tensor(out=sst[:, g_ * 512:(g_ + 1) * 512], in0=sst[:, g_ * 512:(g_ + 1) * 512], in1=p_[:, 0:512], op=ALU.add),
                     [pk_, sk], [sk])
            yk = [("ysb", 0), ("ysb", 1)]
            S.op("pool", lambda g: g.tensor_tensor(out=xw[:, :].rearrange("p (h d) -> p h d", d=64), in0=xtok[:, :].rearrange("p (h d) -> p h d", d=64),
                                                   in1=dsk[:, :].unsqueeze(2).to_broadcast([128, 16, 64]), op=ALU.mult), xtk + ["dsk", "xw"], ["xw"])
            S.op("dve", lambda g: g.tensor_tensor(out=ysb[:, :], in0=ysb[:, :], in1=xw[:, :], op=ALU.add), yk + ["xw"], yk)
            S.op("act", lambda g: g.activation(out=big[:, 1024:2048], in_=big[:, 1024:2048], func=AF.Silu), bk(1024, 2048), bk(1024, 2048))
            S.op("dve", lambda g: g.tensor_tensor(out=ysb[:, :], in0=ysb[:, :], in1=big[:, 1024:2048], op=ALU.mult), yk + bk(1024, 2048), yk)
            load_bcast(gpost[:, :], "Bb", ssm_norm_w, i * SSM_DI, SSM_DI)
            snw = gpost
            for g_ in range(2):
                rmsnorm_stats(ysb[:, g_ * 512:(g_ + 1) * 512], [("ysb", g_)], 512, 2 + g_)
                S.op("dve", lambda g: g.scalar_tensor_tensor(out=mixed[:, 512 + g_ * 512:1024 + g_ * 512], in0=ysb[:, g_ * 512:(g_ + 1) * 512],
                                                             scalar=stat[:, 2 + g_:3 + g_], in1=snw[:, g_ * 512:(g_ + 1) * 512], op0=ALU.mult, op1=ALU.mult),
                     [("ysb", g_), ("stat", 2 + g_), "Bb"], [("mixed", 4 + 4 * g_ + c) for c in range(4)])
            transposes_to(mixT, "mixT", 4, mixed[:, 512:1536], [("mixed", 4 + c) for c in range(8)], 8)

            def ev_out(p_, pk_, c, cw):
                copy(evac_eng(), big[:, 3072 + c:3072 + c + cw], p_[:, 0:cw], [pk_], bk(3072 + c, 3072 + c + cw))
            proj_tok(w_hyb_out, i, HYB_MIX, D, 0, D, mixT, [("mixT", c) for c in range(12)], ev_out)
            post_norm_residual(norm_mix_post, l, big[:, 3072:4096], bk(3072, 4096))

        gdtb = dtb
        galog = alog
        gnw = sb("gnw", [128, 128])
        knT = xtok[:, :].rearrange("p (h d) -> p h d", d=128)
        QKT = xw[:, :].rearrange("p (h d) -> p h d", d=128)
        def mk_set(parts):
            return parts
        gs = []
        gs_alias = []
        vwf = vw[:, :, :].rearrange("p a b -> p (a b)")
        for base, al in [(ktw, ["ktw"]), (vwf, ["vw"]), (Bb, ["Bb"])]:
            dct = {}
            for n_, nm in enumerate(["s0", "s1", "s2", "s3", "s4", "s5"]):
                dct[nm] = base[:, n_ * 128:(n_ + 1) * 128]
            dct["XA"] = base[:, 768:1024]
            dct["XB"] = base[:, 1024:1280]
            gs.append(dct)
            gs_alias.append(al)
        dct = {}
        for n_, nm in enumerate(["s0", "s1", "s2", "s3", "s4", "s5"]):
            dct[nm] = ktw[:, 1280 + n_ * 128:1280 + (n_ + 1) * 128]
        dct["XA"] = vwf[:, 1280:1536]
        dct["XB"] = vwf[:, 1536:1792]
        gs.append(dct)
        gs_alias.append(["ktw", "vw"])

        def gdn_load_params(i):
            load_bcast(gdtb[:, :], "dtb", gdn_dt_bias, i * 16, 16)
            load_bcast(galog[:, :], "alog", gdn_a_log, i * 16, 16)
            S.op("act", lambda g: g.activation(out=galog[:, :], in_=galog[:, :], func=AF.Exp), ["alog"], ["alog"])
            S.op("dve", lambda g: g.tensor_scalar(galog[:, :], galog[:, :], -1.0, None, ALU.mult), ["alog"], ["alog"])
            load_bcast(gnw[:, :], "gnw", gdn_norm_w, i * 128, 128)

        def gdn_layer(l, i, sq, tpos, nvalid):
            full = (nvalid == 128)
            cst = gconv_h[i]
            ck = "gconv_h%d" % i
            Sst = gdn_st[i]
            prenorm(0, l, False)
            S.op("pool", lambda g: g.memset(ktw[0:1, 0:1], 0.0), [], ["ktw"])
            S.op("pool", lambda g: g.memset(vw[0:1, 0:1, 0:1], 0.0), [], ["vw"])
            S.op("pool", lambda g: g.memset(Bb[0:1, 0:1], 0.0), [], ["Bb"])

            def ev_zba(p_, pk_, c, cw):
                copy(evac_eng(), big[:, c:c + cw], p_[:, 0:cw], [pk_], bk(c, c + cw))
            proj_tok(w_gdn_in, i, D, G_IN, 4096, G_IN, hT, hkeys, ev_zba)
            for grp in range(4):
                def ev_x(p_, pk_, j):
                    copy(evac_eng(), xbcT[:, j, 3:131], p_[:, 0:128], [pk_], [("xbcT", j)])
                proj_feat(w_gdn_in, i, D, G_IN, grp * 1024, 1024, hT, hkeys, ev_x)
                conv_silu(8, cwg_all[:, i, grp * 8:(grp + 1) * 8, :], "cwg_all", cst[:, grp * 8:(grp + 1) * 8, :], ck, nvalid, None)
                for j in range(8):
                    ct = grp * 8 + j
                    p_, pk_ = ps()
                    S.op("pe", lambda g: g.transpose(p_[:, 0:128], xbcA[:, j, :], ident[:]), [("xbcA", j), "ident"], [pk_])
                    copy(evac_eng(), big[:, ct * 128:(ct + 1) * 128], p_[:, 0:128], [pk_], bk(ct * 128, (ct + 1) * 128))
            S.op("pool", lambda g: g.tensor_tensor(out=Lb[:, 0:2048], in0=big[:, 0:2048], in1=big[:, 0:2048], op=ALU.mult), bk(0, 2048), ["Lb"])
            S.op("dve", lambda g: g.reduce_sum(out=rn[:, :], in_=Lb[:, 0:2048].rearrange("p (h d) -> p h d", d=128), axis=AX.X), ["Lb"], ["rn"])
            S.op("act", lambda g: g.activation(out=rn[:, :], in_=rn[:, :], func=AF.Ln, bias=c_eps, scale=1.0), ["rn", "ccol"], ["rn"])
            S.op("act", lambda g: g.activation(out=rn[:, :], in_=rn[:, :], func=AF.Exp, scale=-0.5), ["rn"], ["rn"])
            S.op("dve", lambda g: g.tensor_scalar(rn[:, 0:8], rn[:, 0:8], 128.0 ** -0.5, None, ALU.mult), ["rn"], ["rn"])
            if not full:
                S.op("dve", lambda g: g.tensor_scalar(rn[:, :], rn[:, :], valid[:, 0:1], None, ALU.mult), ["rn", "valid"], ["rn"])
                S.op("dve", lambda g: g.tensor_scalar(big[:, 2048:4096], big[:, 2048:4096], valid[:, 0:1], None, ALU.mult),
                     bk(2048, 4096) + ["valid"], bk(2048, 4096))
            S.op("dve", lambda g: g.tensor_tensor(out=big[:, 0:2048].rearrange("p (h d) -> p h d", d=128), in0=big[:, 0:2048].rearrange("p (h d) -> p h d", d=128),
                                                  in1=rn[:, :].unsqueeze(2).to_broadcast([128, 16, 128]), op=ALU.mult), bk(0, 2048) + ["rn"], bk(0, 2048))
            S.op("act", lambda g: g.activation(out=beta[:, :], in_=big[:, 6144:6160], func=AF.Exp, scale=-1.0), bk(6144, 6160), ["beta"])
            S.op("dve", lambda g: g.tensor_scalar(beta[:, :], beta[:, :], 1.0, None, ALU.add), ["beta"], ["beta"])
            S.op("dve", lambda g: g.reciprocal(beta[:, :], beta[:, :]), ["beta"], ["beta"])
            S.op("dve", lambda g: g.tensor_tensor(out=dtt[:, :], in0=big[:, 6160:6176], in1=gdtb[:, :], op=ALU.add), bk(6160, 6176) + ["dtb"], ["dtt"])
            softplus_inplace(dtt[:, :], "dtt")
            S.op("dve", lambda g: g.tensor_tensor(out=av[:, :], in0=dtt[:, :], in1=galog[:, :], op=ALU.mult), ["dtt", "alog"], ["av"])
            if not full:
                S.op("dve", lambda g: g.tensor_scalar(beta[:, :], beta[:, :], valid[:, 0:1], None, ALU.mult), ["beta", "valid"], ["beta"])
                S.op("dve", lambda g: g.tensor_scalar(av[:, :], av[:, :], valid[:, 0:1], None, ALU.mult), ["av", "valid"], ["av"])
            decay_mats(av[:, :], "av")
            S.op("pool", lambda g: g.tensor_tensor(out=aU[:, :, :], in0=seg[:, :, :], in1=SUmat[:, :].unsqueeze(1).to_broadcast([128, 16, 128]), op=ALU.mult),
                 SEGK + ["SUmat", "Lb"], ["Lb"])
            S.op("dve", lambda g: g.tensor_tensor(out=seg[:, :, :], in0=seg[:, :, :], in1=Umat[:, :].unsqueeze(1).to_broadcast([128, 16, 128]), op=ALU.mult),
                 SEGK + ["Umat", "dec_w", "Lb"], SEGK)
            for hq in range(8):
                p_, pk_ = ps()
                S.op("pe", lambda g: g.transpose(p_[:, 0:128], big[:, 1024 + hq * 128:1152 + hq * 128], ident[:]), bk(1024, 2048) + ["ident"], [pk_])
                copy(evac_eng(), knT[:, hq, :], p_[:, 0:128], [pk_], [("xtok", hq)])
                p_, pk_ = ps()
                S.op("pe", lambda g: g.transpose(p_[:, 0:128], big[:, hq * 128:(hq + 1) * 128], ident[:]), bk(0, 1024) + ["ident"], [pk_])
                copy(evac_eng(), stage[:, hq % 4, :], p_[:, 0:128], [pk_], [("stage", hq % 4)])
                p_, pk_ = ps()
                S.op("pe", lambda g: g.matmul(p_[:, 0:128], knT[:, hq, :], stage[:, hq % 4, :], start=True, stop=True), [("xtok", hq), ("stage", hq % 4)], [pk_])
                copy(evac_eng(), QKT[:, hq, :], p_[:, 0:128], [pk_], ["xw"])
            def head_gen(h, par):
                hq = h // 2
                G = gs[par]
                ALS = gs_alias[par]

                def K(nm):
                    return "g%s%d" % (nm, par)

                def SO(e, fn, reads, writes):
                    S.op(e, fn, list(reads) + ALS, writes)

                def CP(e, out, in_, reads, writes):
                    copy(e, out, in_, list(reads) + ALS, writes)
                kcols = bk(1024 + hq * 128, 1152 + hq * 128)
                qcols = bk(hq * 128, (hq + 1) * 128)
                vcols = bk(2048 + h * 128, 2176 + h * 128)
                k_n = big[:, 1024 + hq * 128:1152 + hq * 128]
                q_n = big[:, hq * 128:(hq + 1) * 128]
                v_h = big[:, 2048 + h * 128:2176 + h * 128]
                SO("dve", lambda g: g.tensor_scalar(G["s0"], k_n, beta[:, h:h + 1], None, ALU.mult), kcols + ["beta"], [K("s0")])
                SO("dve", lambda g: g.tensor_scalar(G["XA"][:, 0:128], v_h, beta[:, h:h + 1], None, ALU.mult), vcols + ["beta"], [K("XA")])
                SO("dve", lambda g: g.tensor_scalar(G["XA"][:, 128:256], G["s0"], csb[:, h:h + 1], None, ALU.mult), [K("s0"), "csb", K("XA")], [K("XA")])
                p_, pk_ = ps()
                SO("pe", lambda g: g.transpose(p_[:, 0:128], G["s0"], ident[:]), [K("s0"), "ident"], [pk_])
                CP(evac_eng(), G["s1"], p_[:, 0:128], [pk_], [K("s1")])
                yield
                p_, pk_ = ps()
                SO("pe", lambda g: g.matmul(p_[:, 0:128], knT[:, hq, :], G["s1"], start=True, stop=True), [("xtok", hq), K("s1")], [pk_])
                SO("dve", lambda g: g.tensor_tensor(out=G["s2"], in0=p_[:, 0:128], in1=aU[:, h, :], op=ALU.mult), [pk_, "Lb"], [K("s2")])
                yield
                p_, pk_ = ps()
                SO("pe", lambda g: g.transpose(p_[:, 0:128], G["s2"], ident[:]), [K("s2"), "ident"], [pk_])
                CP(evac_eng(), G["s3"], p_[:, 0:128], [pk_], [K("s3")])
                p_, pk_ = ps()
                SO("pe", lambda g: g.matmul(p_[:, 0:256], G["s2"], G["XA"], start=True, stop=True), [K("s2"), K("XA")], [pk_])
                SO("dve", lambda g: g.tensor_tensor(out=G["XB"], in0=G["XA"], in1=p_[:, 0:256], op=ALU.subtract), [pk_, K("XA")], [K("XB")])
                yield
                Xc, Xn = "XB", "XA"
                Mc, Mn, MTc, MTn = "s3", "s5", "s2", "s4"
                for lvl in range(1, 7):
                    p_, pk_ = ps()
                    SO("pe", lambda g: g.matmul(p_[:, 0:128], G[Mc], G[MTc], start=True, stop=True), [K(Mc), K(MTc)], [pk_])
                    CP(evac_eng(), G[MTn], p_[:, 0:128], [pk_], [K(MTn)])
                    if lvl < 6:
                        p_, pk_ = ps()
                        SO("pe", lambda g: g.matmul(p_[:, 0:128], G[MTc], G[Mc], start=True, stop=True), [K(Mc), K(MTc)], [pk_])
                        CP(evac_eng(), G[Mn], p_[:, 0:128], [pk_], [K(Mn)])
                    yield
                    p_, pk_ = ps()
                    SO("pe", lambda g: g.matmul(p_[:, 0:256], G[MTn], G[Xc], start=True, stop=True), [K(MTn), K(Xc)], [pk_])
                    SO("dve", lambda g: g.tensor_tensor(out=G[Xn], in0=G[Xc], in1=p_[:, 0:256], op=ALU.add), [pk_, K(Xc)], [K(Xn)])
                    yield
                    Xc, Xn = Xn, Xc
                    Mc, Mn = Mn, Mc
                    MTc, MTn = MTn, MTc
                X = G[Xc]
                p_, pk_ = ps()
                SO("pe", lambda g: g.transpose(p_[:, 0:128], X[:, 128:256], ident[:]), [K(Xc), "ident"], [pk_])
                CP(evac_eng(), G["s0"], p_[:, 0:128], [pk_], [K("s0")])
                SO("dve", lambda g: g.tensor_scalar(G["s3"], q_n, csb[:, h:h + 1], None, ALU.mult), qcols + ["csb"], [K("s3")])
                yield
                stk = ("gdn_st", i, h)
                p_, pk_ = ps()
                SO("pe", lambda g: g.matmul(p_[:, 0:128], G["s0"], Sst[:, h, :], start=True, stop=True), [K("s0"), stk], [pk_])
                SO("dve", lambda g: g.tensor_tensor(out=G["s1"], in0=X[:, 0:128], in1=p_[:, 0:128], op=ALU.subtract), [pk_, K(Xc)], [K("s1")])
                p_, pk_ = ps()
                SO("pe", lambda g: g.transpose(p_[:, 0:128], G["s3"], ident[:]), [K("s3"), "ident"], [pk_])
                CP(evac_eng(), G["s5"], p_[:, 0:128], [pk_], [K("s5")])
                SO("pool", lambda g: g.tensor_tensor(out=G["s2"], in0=QKT[:, hq, :], in1=seg[:, h, :], op=ALU.mult),
                   ["xw"] + SEGK, [K("s2")])
                SO("dve", lambda g: g.tensor_scalar(G["s4"], k_n, dec_w[:, h:h + 1], None, ALU.mult), kcols + ["dec_w"], [K("s4")])
                yield
                p_, pk_ = ps()
                SO("pe", lambda g: g.matmul(p_[:, 0:128], G["s5"], Sst[:, h, :], start=True, stop=False), [K("s5"), stk], [pk_])
                SO("pe", lambda g: g.matmul(p_[:, 0:128], G["s2"], G["s1"], start=False, stop=True), [K("s2"), K("s1")], [pk_])
                CP(evac_eng(), mixed[:, h * 128:(h + 1) * 128], p_[:, 0:128], [pk_], [("mixed", h)])
                p_, pk_ = ps()
                SO("pe", lambda g: g.matmul(p_[:, 0:128], G["s4"], G["s1"], start=True, stop=True), [K("s4"), K("s1")], [pk_])
                SO("dve", lambda g: g.scalar_tensor_tensor(out=Sst[:, h, :], in0=Sst[:, h, :], scalar=dec_b[:, h:h + 1], in1=p_[:, 0:128],
                                                           op0=ALU.mult, op1=ALU.add), [pk_, stk, "dec_b"], [stk])
                yield

            NGRP = len(gs)
            for h0 in range(0, 16, NGRP):
                gens = [head_gen(h0 + j, j) for j in range(min(NGRP, 16 - h0))]
                alive = list(gens)
                while alive:
                    nxt = []
                    for g_ in alive:
                        try:
                            next(g_)
                            nxt.append(g_)
                        except StopIteration:
                            pass
                    alive = nxt
            mk = [("mixed", h) for h in range(16)]
            S.op("pool", lambda g: g.tensor_tensor(out=Lb[:, 0:2048], in0=mixed[:, :], in1=mixed[:, :], op=ALU.mult), mk, ["Lb"])
            S.op("dve", lambda g: g.reduce_sum(out=rn[:, :], in_=Lb[:, 0:2048].rearrange("p (h d) -> p h d", d=128), axis=AX.X), ["Lb"], ["rn"])
            S.op("act", lambda g: g.activation(out=rn[:, :], in_=rn[:, :], func=AF.Ln, bias=c_eps, scale=1.0 / 128), ["rn", "ccol"], ["rn"])
            S.op("act", lambda g: g.activation(out=rn[:, :], in_=rn[:, :], func=AF.Exp, scale=-0.5), ["rn"], ["rn"])
            S.op("dve", lambda g: g.tensor_tensor(out=mixed[:, :].rearrange("p (h d) -> p h d", d=128), in0=mixed[:, :].rearrange("p (h d) -> p h d", d=128),
                                                  in1=rn[:, :].unsqueeze(2).to_broadcast([128, 16, 128]), op=ALU.mult), mk + ["rn"], mk)
            S.op("pool", lambda g: g.tensor_tensor(out=mixed[:, :].rearrange("p (h d) -> p h d", d=128), in0=mixed[:, :].rearrange("p (h d) -> p h d", d=128),
                                                   in1=gnw[:, :].unsqueeze(1).to_broadcast([128, 16, 128]), op=ALU.mult), mk + ["gnw"], mk)
            S.op("act", lambda g: g.activation(out=big[:, 4096:6144], in_=big[:, 4096:6144], func=AF.Silu), bk(4096, 6144), bk(4096, 6144))
            S.op("dve", lambda g: g.tensor_tensor(out=mixed[:, :], in0=mixed[:, :], in1=big[:, 4096:6144], op=ALU.mult), mk + bk(4096, 6144), mk)
            transposes_to(mixT, "mixT", 0, mixed, mk, 16)

            def ev_out(p_, pk_, c, cw):
                copy(evac_eng(), big[:, c:c + cw], p_[:, 0:cw], [pk_], bk(c, c + cw))
            proj_tok(w_gdn_out, i, G_VW, D, 0, D, mixT, [("mixT", c) for c in range(16)], ev_out)
            post_norm_residual(norm_mix_post, l, big[:, 0:1024], bk(0, 1024))

        def conv_state_load(dst, dkey, nct, src_tensor, off, width):
            S.dma("sp", tm3[0:3, 0:width], bass.AP(src_tensor, off, [[width, 3], [1, width]]), [], BIGK)
            for j in range(nct):
                p_, pk_ = ps()
                S.op("pe", lambda g: g.transpose(p_[:, 0:3], tm3[0:3, j * 128:(j + 1) * 128], ident[0:3, 0:3]), BIGK + ["ident"], [pk_])
                copy("dve", dst[:, j, :], p_[:, 0:3], [pk_], [dkey])

        def conv_state_store(src, skey, nct, dst_tensor, off, width):
            for j in range(nct):
                p_, pk_ = ps()
                S.op("pe", lambda g: g.transpose(p_[0:3, 0:128], src[:, j, :], ident[:]), [skey, "ident"], [pk_])
                copy("dve", tm3[0:3, j * 128:(j + 1) * 128], p_[0:3, 0:128], [pk_], BIGK)
            S.dma("sp", bass.AP(dst_tensor, off, [[width, 3], [1, width]]), tm3[0:3, 0:width], BIGK, [("out", dst_tensor.name)])

        def ssm_state_load(i, src_tensor, off):
            S.dma("sp", Lb[:, 0:1024].rearrange("p (b n) -> p b n", n=128), bass.AP(src_tensor, off, [[128, 128], [128 * 128, 8], [1, 128]]), [], ["Lb"])
            for b in range(8):
                p_, pk_ = ps()
                S.op("pe", lambda g: g.transpose(p_[:, 0:128], Lb[:, b * 128:(b + 1) * 128], ident[:]), ["Lb", "ident"], [pk_])
                copy(evac_eng(), ssm_st[i][:, b * 128:(b + 1) * 128], p_[:, 0:128], [pk_], ["ssm_st%d" % i])

        def ssm_state_store(i, dst_tensor, off):
            for b in range(8):
                p_, pk_ = ps()
                S.op("pe", lambda g: g.transpose(p_[:, 0:128], ssm_st[i][:, b * 128:(b + 1) * 128], ident[:]), ["ssm_st%d" % i, "ident"], [pk_])
                copy(evac_eng(), Lb[:, b * 128:(b + 1) * 128], p_[:, 0:128], [pk_], ["Lb"])
            S.dma("sp", bass.AP(dst_tensor, off, [[128, 128], [128 * 128, 8], [1, 128]]), Lb[:, 0:1024].rearrange("p (b n) -> p b n", n=128),
                  ["Lb"], [("out", dst_tensor.name)])

        def flat_copy(dst_tensor, doff, src_tensor, soff, nelem, rkeys, wkeys):
            assert nelem % 128 == 0
            per = nelem // 128
            S.dma("sp", bass.AP(dst_tensor, doff, [[per, 128], [1, per]]), bass.AP(src_tensor, soff, [[per, 128], [1, per]]), rkeys, wkeys)

        def seq_begin(sq, kind, sidx):
            for i in range(NHYB):
                ck = "conv_h%d" % i
                if kind == "p":
                    S.op("pool", lambda g: g.memset(conv_h[i][:, :, :], 0.0), [], [ck])
                    S.op("pool", lambda g: g.memset(ssm_st[i][:, :], 0.0), [], ["ssm_st%d" % i])
                else:
                    conv_state_load(conv_h[i], ck, 12, st_sconv, (i * NS1 + sidx) * 3 * SSM_XBC, SSM_XBC)
                    ssm_state_load(i, st_ssm, (i * NS1 + sidx) * 1024 * 128)
                    flat_copy(k_scr, (i * NSEQ + sq) * SCR_ROWS * A_W, cache_k, (i * NS1 + sidx) * WIN * A_W, WIN * A_W, [], [("k_scr", i, sq)])
                    flat_copy(v_scr, (i * NSEQ + sq) * SCR_ROWS * A_W, cache_v, (i * NS1 + sidx) * WIN * A_W, WIN * A_W, [], [("v_scr", i, sq)])
                    scr_base = (i * NSEQ + sq) * 128 * 4 * SCR_ROWS
                    for t in range(16):
                        S.dma("sp", Lb[:, 0:512], bass.AP(cache_k, ((i * NS1 + sidx) * WIN + t * 128) * A_W, [[A_W, 128], [1, A_W]]), [], ["Lb"])
                        for pr in range(4):
                            p_, pk_ = ps()
                            S.op("pe", lambda g: g.transpose(p_[:, 0:128], Lb[:, pr * 128:(pr + 1) * 128], ident[:]), ["Lb", "ident"], [pk_])
                            copy(evac_eng(), stage[:, pr, :], p_[:, 0:128], [pk_], [("stage", pr)])
                        S.dma("sp", bass.AP(kt_scr, scr_base + t * 128, [[4 * SCR_ROWS, 128], [SCR_ROWS, 4], [1, 128]]), stage[:, :, :],
                              [("stage", pr) for pr in range(4)], [("kt_scr", i, sq)])
            for i in range(NGDN):
                ck = "gconv_h%d" % i
                if kind == "p":
                    S.op("pool", lambda g: g.memset(gconv_h[i][:, :, :], 0.0), [], [ck])
                    S.op("pool", lambda g: g.memset(gdn_st[i][:, :, :], 0.0), [], [("gdn_st", i, h) for h in range(16)])
                else:
                    conv_state_load(gconv_h[i], ck, 32, st_gconv, (i * NS1 + sidx) * 3 * G_QKV, G_QKV)
                    S.dma("sp", gdn_st[i][:, :, :], bass.AP(st_gdn, (i * NS1 + sidx) * 16 * 128 * 128, [[128, 128], [128 * 128, 16], [1, 128]]),
                          [], [("gdn_st", i, h) for h in range(16)])

        def seq_end(sq, kind, sidx):
            for i in range(NHYB):
                ck = "conv_h%d" % i
                if kind == "p":
                    conv_state_store(conv_h[i], ck, 12, o_psc, i * 3 * SSM_XBC, SSM_XBC)
                    ssm_state_store(i, o_pss, i * 1024 * 128)
                    flat_copy(o_pk, i * KEEP * A_W, k_scr, ((i * NSEQ + sq) * SCR_ROWS + SEQ - KEEP) * A_W, KEEP * A_W, [("k_scr", i, sq)], [("out", "pk", i)])
                    flat_copy(o_pv, i * KEEP * A_W, v_scr, ((i * NSEQ + sq) * SCR_ROWS + SEQ - KEEP) * A_W, KEEP * A_W, [("v_scr", i, sq)], [("out", "pv", i)])
                else:
                    conv_state_store(conv_h[i], ck, 12, o_ssc, (i * NS1 + sidx) * 3 * SSM_XBC, SSM_XBC)
                    ssm_state_store(i, o_sss, (i * NS1 + sidx) * 1024 * 128)
                    flat_copy(o_sk, (i * NS1 + sidx) * WIN * A_W, k_scr, ((i * NSEQ + sq) * SCR_ROWS + 1) * A_W, WIN * A_W, [("k_scr", i, sq)], [("out", "sk", i, sidx)])
                    flat_copy(o_sv, (i * NS1 + sidx) * WIN * A_W, v_scr, ((i * NSEQ + sq) * SCR_ROWS + 1) * A_W, WIN * A_W, [("v_scr", i, sq)], [("out", "sv", i, sidx)])
            for i in range(NGDN):
                ck = "gconv_h%d" % i
                stk = [("gdn_st", i, h) for h in range(16)]
                if kind == "p":
                    conv_state_store(gconv_h[i], ck, 32, o_pgc, i * 3 * G_QKV, G_QKV)
                    S.dma("sp", bass.AP(o_pgs, i * 16 * 128 * 128, [[128, 128], [128 * 128, 16], [1, 128]]), gdn_st[i][:, :, :], stk, [("out", "pgs", i)])
                else:
                    conv_state_store(gconv_h[i], ck, 32, o_sgc, (i * NS1 + sidx) * 3 * G_QKV, G_QKV)
                    S.dma("sp", bass.AP(o_sgs, (i * NS1 + sidx) * 16 * 128 * 128, [[128, 128], [128 * 128, 16], [1, 128]]), gdn_st[i][:, :, :], stk,
                          [("out", "sgs", i, sidx)])

        for sq, (kind, sidx, t0, ntl) in enumerate(seqs):
            nvalid = 128 if kind == "p" else 1
            if kind == "s":
                S.op("pool", lambda g: g.memset(valid[:, :], 0.0), [], ["valid"])
                S.op("pool", lambda g: g.memset(valid[0:1, :], 1.0), ["valid"], ["valid"])
            seq_begin(sq, kind, sidx)
            for tl in range(ntl):
                tpos = t0 + tl
                if kind == "p":
                    S.dma("sp", xres[:, :], x_prompt[tl * 128:(tl + 1) * 128, :], [], ["xres"])
                else:
                    S.op("pool", lambda g: g.memset(xres[:, :], 0.0), [], ["xres"])
                    S.dma("sp", xres[0:1, :], x_sample[sidx:sidx + 1, :], ["xres"], ["xres"])
                for l in range(depth):
                    i = l // 2
                    if l % 2 == 0:
                        hybrid_load_params(i)
                        hybrid_layer(l, i, sq, tpos, nvalid)
                    else:
                        gdn_load_params(i)
                        gdn_layer(l, i, sq, tpos, nvalid)
                    ffn(l)
                if kind == "p":
                    S.dma("sp", y_prompt[tl * 128:(tl + 1) * 128, :], xres[:, :], ["xres"], ["y_prompt"])
                else:
                    S.dma("sp", y_sample[sidx:sidx + 1, :], xres[0:1, :], ["xres"], ["y_sample"])
            seq_end(sq, kind, sidx)

        for slot in S.dma_sems:
            if slot[1] > 0:
                S._wait("sp", (slot[0], slot[1]))
        for e in S.eng:
            if e != "sp" and S.cnt[e] > 0:
                S._wait("sp", (S.sem[e], S.cnt[e]))
        print("instructions:", S.ninstr, "waits:", S.nwait, "sems:", S.nsem, "sbuf_bytes/partition:", sb_total[0])
    return nc


_NC_CACHE = {}


def kernel(x_prompt, x_sample, cache_attn_k, cache_attn_v, state_ssm_conv, state_ssm, state_gdn_conv, state_gdn,
           rel_bias, norm_mix_pre, norm_mix_post, norm_ffn_pre, norm_ffn_post, w_hyb_in, ssm_conv_w, ssm_conv_b,
           ssm_dt_bias, ssm_a_log, ssm_d, ssm_norm_w, w_hyb_out, w_gdn_in, gdn_conv_w, gdn_dt_bias, gdn_a_log,
           gdn_norm_w, w_gdn_out, w_ffn_gate, w_ffn_up, w_ffn_down):
    f = lambda a: np.ascontiguousarray(np.asarray(a, dtype=np.float32))
    x_prompt = f(x_prompt)
    B, SEQ, _ = x_prompt.shape
    x_sample = f(x_sample)
    DB = x_sample.shape[0]
    depth = np.asarray(norm_mix_pre).shape[0]
    n_ptiles = SEQ // 128
    assert DB % NCORES == 0
    n_samp = DB // NCORES
    key = (n_ptiles, n_samp, depth)
    if key not in _NC_CACHE:
        _NC_CACHE[key] = build_nc(n_ptiles, n_samp, depth=depth)
    nc = _NC_CACHE[key]
    NHYB = (depth + 1) // 2
    NGDN = depth // 2
    shared = dict(
        rel_bias=f(rel_bias), oh_tab=attn_tables(), norm_mix_pre=f(norm_mix_pre), norm_mix_post=f(norm_mix_post),
        norm_ffn_pre=f(norm_ffn_pre), norm_ffn_post=f(norm_ffn_post), w_hyb_in=f(w_hyb_in), ssm_conv_w=f(ssm_conv_w),
        ssm_conv_b=f(ssm_conv_b), ssm_dt_bias=f(ssm_dt_bias), ssm_a_log=f(ssm_a_log), ssm_d=f(ssm_d), ssm_norm_w=f(ssm_norm_w),
        w_hyb_out=f(w_hyb_out), w_gdn_in=f(w_gdn_in), gdn_conv_w=f(gdn_conv_w), gdn_dt_bias=f(gdn_dt_bias),
        gdn_a_log=f(gdn_a_log), gdn_norm_w=f(gdn_norm_w), w_gdn_out=f(w_gdn_out), w_ffn_gate=f(w_ffn_gate),
        w_ffn_up=f(w_ffn_up), w_ffn_down=f(w_ffn_down))
    ck = f(cache_attn_k).reshape(NHYB, DB, WIN, A_W)
    cv = f(cache_attn_v).reshape(NHYB, DB, WIN, A_W)
    sc = f(state_ssm_conv)
    ss = f(state_ssm).reshape(NHYB, DB, 1024, 128)
    gc = f(state_gdn_conv)
    gst = f(state_gdn)
    in_maps = []
    for c in range(NCORES):
        sl = slice(c * n_samp, (c + 1) * n_samp)
        m = dict(shared)
        m["x_prompt"] = np.ascontiguousarray(x_prompt[c % B])
        m["x_sample"] = np.ascontiguousarray(x_sample[sl, 0, :])
        m["cache_k"] = np.ascontiguousarray(ck[:, sl])
        m["cache_v"] = np.ascontiguousarray(cv[:, sl])
        m["st_sconv"] = np.ascontiguousarray(sc[:, sl])
        m["st_ssm"] = np.ascontiguousarray(ss[:, sl])
        m["st_gconv"] = np.ascontiguousarray(gc[:, sl])
        m["st_gdn"] = np.ascontiguousarray(gst[:, sl])
        in_maps.append(m)
    res = run_bass_kernel_spmd(nc, in_maps, core_ids=list(range(NCORES)))
    R = res.results
    KEEP = min(WIN, SEQ)
    pc = list(range(B))
    y_prompt = np.stack([R[c]["y_prompt"] for c in pc], 0)
    y_sample = np.concatenate([R[c]["y_sample"] for c in range(NCORES)], 0)[:, None, :]
    pk = np.stack([R[c]["o_pk"] for c in pc], 1).reshape(NHYB, B, KEEP, 8, 64)
    pv = np.stack([R[c]["o_pv"] for c in pc], 1).reshape(NHYB, B, KEEP, 8, 64)
    psc = np.stack([R[c]["o_psc"] for c in pc], 1)
    pss = np.stack([R[c]["o_pss"] for c in pc], 1).reshape(NHYB, B, 16, 64, 128)
    pgc = np.stack([R[c]["o_pgc"] for c in pc], 1)
    pgs = np.stack([R[c]["o_pgs"] for c in pc], 1)
    cat = lambda nm: np.concatenate([R[c][nm] for c in range(NCORES)], 1)
    sk = cat("o_sk").reshape(NHYB, DB, WIN, 8, 64)
    sv = cat("o_sv").reshape(NHYB, DB, WIN, 8, 64)
    ssc = cat("o_ssc")
    sss = cat("o_sss").reshape(NHYB, DB, 16, 64, 128)
    sgc = cat("o_sgc")
    sgs = cat("o_sgs")
    return (y_prompt, y_sample, pk, pv, psc, pss, pgc, pgs, sk, sv, ssc, sss, sgc, sgs)
```

```python
import math
from contextlib import ExitStack
import numpy as np
import concourse.bass as bass
import concourse.mybir as mybir
from concourse.bass_utils import run_bass_kernel_spmd

F32 = mybir.dt.float32
F32R = mybir.dt.float32r
AF = mybir.ActivationFunctionType
ALU = mybir.AluOpType
AX = mybir.AxisListType

D = 1024
EPS = 1e-6
A_W = 512
WIN = 2048
NKT = 17
SSM_DI = 1024
SSM_XBC = 1536
HYB_IN = 4112
HYB_MIX = 1536
G_VW = 2048
G_QKV = 4096
G_IN = 6176
D_FF = 2816
NEG = -30000.0
NCORES = 8
TABL = 2304


def rel_buckets(dist):
    max_exact = 16
    n = np.maximum(dist, 1).astype(np.float32)
    large = max_exact + (np.log(n / max_exact) / math.log(2048 / max_exact) * (32 - max_exact)).astype(np.int32)
    large = np.minimum(large, 31)
    return np.where(dist < max_exact, dist, large).astype(np.int32)


def attn_tables():
    dist = 2175 - np.arange(TABL)
    valid = (dist >= 0) & (dist <= 2048)
    dc = np.clip(dist, 0, 2048)
    cnt = ((dc <= 128).astype(np.float64) + ((dc % 4 == 0) & (dc <= 512)) + ((dc % 16 == 0) & (dc <= 2048)))
    cnt = np.where(valid, cnt, 0.0)
    logc = np.where(cnt > 0, np.log(np.maximum(cnt, 1e-9)), NEG).astype(np.float32)
    bk = rel_buckets(dc)
    oh = np.zeros((33, TABL), np.float32)
    oh[bk, np.arange(TABL)] = np.where(cnt > 0, 1.0, 0.0)
    oh[32, :] = logc
    return oh


class Sched:
    def __init__(self, nc, es):
        self.nc = nc
        self.es = es
        self.eng = {"pe": nc.tensor, "act": nc.scalar, "dve": nc.vector, "pool": nc.gpsimd, "sp": nc.sync}
        self.sem = {}
        self.cnt = {}
        self.nsem = 0
        self.pe_sems = set()
        for e in self.eng:
            self._new_sem(e)
        self.dma_sems = []
        for i in range(48):
            self.dma_sems.append([es.enter_context(nc.semaphore("dq%d" % i)), 0])
        self.dma_rr = 0
        self.waited = {e: {} for e in self.eng}
        self.lastw = {}
        self.readers = {}
        self.ninstr = 0
        self.nwait = 0

    def _new_sem(self, e):
        self.sem[e] = self.es.enter_context(self.nc.semaphore("s_%s_%d" % (e, self.nsem)))
        if e == "pe":
            self.pe_sems.add(id(self.sem[e]))
        self.nsem += 1
        self.cnt[e] = 0

    def _wait(self, e, dep):
        sem, val = dep
        if e == "pe" and id(sem) in self.pe_sems:
            return
        w = self.waited[e]
        k = id(sem)
        if w.get(k, 0) >= val:
            return
        w[k] = val
        self.eng[e].wait_ge(sem, val)
        self.nwait += 1

    def _deps(self, e, reads, writes):
        for k in reads:
            d = self.lastw.get(k)
            if d is not None:
                self._wait(e, d)
        for k in writes:
            d = self.lastw.get(k)
            if d is not None:
                self._wait(e, d)
            r = self.readers.get(k)
            if r:
                for d in r.values():
                    self._wait(e, d)

    def _commit(self, tok, reads, writes):
        for k in writes:
            self.lastw[k] = tok
            self.readers[k] = {}
        for k in reads:
            r = self.readers.setdefault(k, {})
            r[id(tok[0])] = tok

    def op(self, e, fn, reads=(), writes=()):
        self._deps(e, reads, writes)
        if self.cnt[e] >= 30000:
            self._new_sem(e)
        inst = fn(self.eng[e])
        inst.then_inc(self.sem[e], 1)
        self.cnt[e] += 1
        self.ninstr += 1
        tok = (self.sem[e], self.cnt[e])
        self._commit(tok, reads, writes)

    def dma(self, e, out, in_, reads=(), writes=(), slow=False):
        self._deps(e, reads, writes)
        slot = self.dma_sems[self.dma_rr]
        self.dma_rr = (self.dma_rr + 1) % len(self.dma_sems)
        if slot[1] > 0:
            self._wait(e, (slot[0], slot[1]))
        if slot[1] >= 30000:
            slot[0] = self.es.enter_context(self.nc.semaphore("dqx%d" % self.nsem))
            self.nsem += 1
            slot[1] = 0
        if slow:
            self.eng[e].dma_start(out=out, in_=in_, allow_slow_non_contiguous=True).then_inc(slot[0], 16)
        else:
            self.eng[e].dma_start(out=out, in_=in_).then_inc(slot[0], 16)
        slot[1] += 16
        self.ninstr += 1
        tok = (slot[0], slot[1])
        self._commit(tok, reads, writes)


def build_nc(n_ptiles, n_samp, depth=4, mm_r=True, dbg=None):
    nc = bass.Bass("TRN2", target_bir_lowering=False)
    SEQ = n_ptiles * 128
    NHYB = (depth + 1) // 2
    NGDN = depth // 2
    NG1 = max(NGDN, 1)
    NS1 = max(n_samp, 1)
    KEEP = min(WIN, SEQ)
    MMDT = F32R if mm_r else F32
    wq = "pool" if mm_r else "sp"

    def din(name, shape):
        return nc.dram_tensor(name, list(shape), F32, kind="ExternalInput")

    def dout(name, shape):
        return nc.dram_tensor(name, list(shape), F32, kind="ExternalOutput")

    def dscr(name, shape):
        return nc.dram_tensor(name, list(shape), F32, kind="Internal")

    x_prompt = din("x_prompt", [SEQ, D])
    x_sample = din("x_sample", [NS1, D])
    cache_k = din("cache_k", [NHYB, NS1, WIN, A_W])
    cache_v = din("cache_v", [NHYB, NS1, WIN, A_W])
    st_sconv = din("st_sconv", [NHYB, NS1, 3, SSM_XBC])
    st_ssm = din("st_ssm", [NHYB, NS1, 1024, 128])
    st_gconv = din("st_gconv", [NG1, NS1, 3, G_QKV])
    st_gdn = din("st_gdn", [NG1, NS1, 16, 128, 128])
    rel_bias = din("rel_bias", [32, 8])
    oh_tab = din("oh_tab", [33, TABL])
    norm_mix_pre = din("norm_mix_pre", [depth, D])
    norm_mix_post = din("norm_mix_post", [depth, D])
    norm_ffn_pre = din("norm_ffn_pre", [depth, D])
    norm_ffn_post = din("norm_ffn_post", [depth, D])
    w_hyb_in = din("w_hyb_in", [NHYB, D, HYB_IN])
    ssm_conv_w = din("ssm_conv_w", [NHYB, 4, SSM_XBC])
    ssm_conv_b = din("ssm_conv_b", [NHYB, SSM_XBC])
    ssm_dt_bias = din("ssm_dt_bias", [NHYB, 16])
    ssm_a_log = din("ssm_a_log", [NHYB, 16])
    ssm_d = din("ssm_d", [NHYB, 16])
    ssm_norm_w = din("ssm_norm_w", [NHYB, SSM_DI])
    w_hyb_out = din("w_hyb_out", [NHYB, HYB_MIX, D])
    w_gdn_in = din("w_gdn_in", [NG1, D, G_IN])
    gdn_conv_w = din("gdn_conv_w", [NG1, 4, G_QKV])
    gdn_dt_bias = din("gdn_dt_bias", [NG1, 16])
    gdn_a_log = din("gdn_a_log", [NG1, 16])
    gdn_norm_w = din("gdn_norm_w", [NG1, 128])
    w_gdn_out = din("w_gdn_out", [NG1, G_VW, D])
    w_ffn_gate = din("w_ffn_gate", [depth, D, D_FF])
    w_ffn_up = din("w_ffn_up", [depth, D, D_FF])
    w_ffn_down = din("w_ffn_down", [depth, D_FF, D])

    y_prompt = dout("y_prompt", [SEQ, D])
    y_sample = dout("y_sample", [NS1, D])
    o_pk = dout("o_pk", [NHYB, KEEP, A_W])
    o_pv = dout("o_pv", [NHYB, KEEP, A_W])
    o_psc = dout("o_psc", [NHYB, 3, SSM_XBC])
    o_pss = dout("o_pss", [NHYB, 1024, 128])
    o_pgc = dout("o_pgc", [NG1, 3, G_QKV])
    o_pgs = dout("o_pgs", [NG1, 16, 128, 128])
    o_sk = dout("o_sk", [NHYB, NS1, WIN, A_W])
    o_sv = dout("o_sv", [NHYB, NS1, WIN, A_W])
    o_ssc = dout("o_ssc", [NHYB, NS1, 3, SSM_XBC])
    o_sss = dout("o_sss", [NHYB, NS1, 1024, 128])
    o_sgc = dout("o_sgc", [NG1, NS1, 3, G_QKV])
    o_sgs = dout("o_sgs", [NG1, NS1, 16, 128, 128])
    dbg_out = {}
    if dbg:
        for nm, shp in dbg.items():
            dbg_out[nm] = dout("dbg_" + nm, shp)

    seqs = [("p", 0, 0, n_ptiles)] + [("s", s, 16, 1) for s in range(n_samp)]
    NSEQ = len(seqs)
    SCR_ROWS = max(SEQ, WIN + 128)
    kt_scr = dscr("kt_scr", [NHYB, NSEQ, 128, 4, SCR_ROWS])
    k_scr = dscr("k_scr", [NHYB, NSEQ, SCR_ROWS, A_W])
    v_scr = dscr("v_scr", [NHYB, NSEQ, SCR_ROWS, A_W])
    ftab = dscr("ftab", [8, TABL])

    es = ExitStack()
    with es:
        S = Sched(nc, es)

        sb_total = [0]

        def sb(name, shape, dt=F32):
            sb_total[0] += int(np.prod(shape[1:])) * 4
            return es.enter_context(nc.sbuf_tensor(name, list(shape), dt))

        psum = [es.enter_context(nc.psum_tensor("ps%d" % i, [128, 512], F32)) for i in range(8)]
        ps_rr = [0]

        def ps():
            i = ps_rr[0]
            ps_rr[0] = (i + 1) % 6
            return psum[i], "ps%d" % i

        acc_rr = [0]

        def ps_acc():
            acc_rr[0] ^= 1
            i = 6 + acc_rr[0]
            return psum[i], "ps%d" % i

        ev_rr = [0]

        def evac_eng():
            ev_rr[0] ^= 1
            return "act" if ev_rr[0] else "dve"

        def copy(e, out, in_, reads, writes):
            if e == "act":
                S.op("act", lambda g: g.activation(out=out, in_=in_, func=AF.Copy), reads, writes)
            else:
                S.op(e, lambda g: g.tensor_copy(out, in_), reads, writes)

        def dump(name, ap, keys):
            if name in dbg_out:
                t = dbg_out[name]
                S.dma("sp", t.ap() if hasattr(t, "ap") else t[:], ap, keys, ["dbg_" + name])

        def mask_const(name, pattern, op, base, cm):
            t = sb(name, [128, 128])
            S.op("pool", lambda g: g.memset(t[:], 1.0), [], [name])
            S.op("pool", lambda g: g.affine_select(out=t[:], in_=t[:], pattern=pattern, compare_op=op, fill=0.0,
                                                   base=base, channel_multiplier=cm), [name], [name])
            return t

        ident = mask_const("ident", [[-1, 128]], ALU.is_equal, 0, 1)
        antiI = mask_const("antiI", [[1, 128]], ALU.is_equal, -127, 1)
        Umat = mask_const("Umat", [[1, 128]], ALU.is_ge, 0, -1)
        SUmat = mask_const("SUmat", [[1, 128]], ALU.is_gt, 0, -1)
        SLmat = mask_const("SLmat", [[-1, 128]], ALU.is_gt, 0, 1)
        ones = sb("ones", [128, 128])
        S.op("pool", lambda g: g.memset(ones[:], 1.0), [], ["ones"])
        ccol = sb("ccol", [128, 4])
        S.op("pool", lambda g: g.memset(ccol[:, 0:1], EPS), [], ["ccol"])
        S.op("pool", lambda g: g.memset(ccol[:, 1:2], 1.0), ["ccol"], ["ccol"])
        S.op("pool", lambda g: g.memset(ccol[:, 2:3], 0.0), ["ccol"], ["ccol"])
        c_eps = ccol[:, 0:1]
        c_one = ccol[:, 1:2]
        CONSTS = ["ident", "antiI", "Umat", "SUmat", "SLmat", "ones", "ccol"]

        xres = sb("xres", [128, D])
        hT = sb("hT", [128, 8, 128], MMDT)
        stat = sb("stat", [128, 8])
        WBUF = 4096
        NWB = 3
        wbuf = [sb("wbuf%d" % i, [128, WBUF], MMDT) for i in range(NWB)]
        wb_rr = [0]
        big = sb("big", [128, 6176])
        mixed = sb("mixed", [128, 2048])
        mixT = sb("mixT", [128, 22, 128], MMDT)
        hTr = mixT[:, 12:20, :]
        valid = sb("valid", [128, 1])
        BIGK = [("big", i) for i in range(13)]

        def bk(c0, c1):
            return [("big", i) for i in range(c0 // 512, (c1 - 1) // 512 + 1)]

        conv_h = [sb("conv_h%d" % i, [128, 12, 3]) for i in range(NHYB)]
        ssm_st = [sb("ssm_st%d" % i, [128, 1024]) for i in range(NHYB)]
        gconv_h = [sb("gconv_h%d" % i, [128, 32, 3]) for i in range(NGDN)]
        gdn_st = [sb("gdn_st%d" % i, [128, 16, 128]) for i in range(NGDN)]

        qT = sb("qT", [128, 4, 128])
        kTt = sb("kTt", [128, 4, 128])
        xbcT = sb("xbcT", [128, 12, 131])
        xbcA = sb("xbcA", [128, 12, 128])
        dtb = sb("dtb", [128, 16])
        alog = sb("alog", [128, 16])
        dsk = sb("dsk", [128, 16])
        dtt = sb("dtt", [128, 16])
        av = sb("av", [128, 16])
        csb = sb("csb", [128, 16])
        dec_b = sb("dec_b", [128, 16])
        dec_w = sb("dec_w", [128, 16])
        beta = sb("beta", [128, 16])
        rn = sb("rn", [128, 16])
        seg = sb("seg", [128, 16, 128])
        cbm = sb("cbm", [128, 2, 128])
        xtok = sb("xtok", [128, 1024])
        xw = sb("xw", [128, 1024])
        btok = sb("btok", [128, 256])
        ysb = sb("ysb", [128, 1024])
        ktw = sb("ktw", [128, NKT * 128])
        vw = sb("vw", [128, NKT, 128])
        Lb = sb("Lb", [128, NKT * 128])
        aU = Lb[:, 0:2048].rearrange("p (h i) -> p h i", i=128)
        Bb = sb("Bb", [128, NKT * 128])
        ETb = sb("ETb", [128, 4, 128])
        mx = sb("mx", [128, 4])
        xn = Lb[:, 0:1024]
        gpost = Bb[:, 0:1024]
        stage = sb("stage", [128, 4, 128])
        SEGK = [("seg", q4) for q4 in range(4)]
        tm3 = big

        rb = big[0:33, 0:8]
        ohs = big[0:33, 512:512 + TABL]
        fsb = big[0:8, 3072:3072 + TABL]
        S.dma("sp", big[0:32, 0:8], rel_bias[:, :], [], [("big", 0)])
        S.op("pool", lambda g: g.memset(big[32:33, 0:8], 1.0), [], [("big", 0)])
        S.dma("sp", ohs, oh_tab[:, :], [], bk(512, 512 + TABL))
        for c0 in range(0, TABL, 512):
            cw = min(512, TABL - c0)
            p_, pk_ = ps()
            S.op("pe", lambda g: g.matmul(p_[0:8, 0:cw], rb, big[0:33, 512 + c0:512 + c0 + cw], start=True, stop=True),
                 BIGK, [pk_])
            copy("dve", big[0:8, 3072 + c0:3072 + c0 + cw], p_[0:8, 0:cw], [pk_], bk(3072 + c0, 3072 + c0 + cw))
        S.dma("sp", ftab[:, :], fsb, BIGK, ["ftab"])

        gv_all = sb("gv_all", [128, depth * 2, 8])
        for l_ in range(depth):
            S.dma("sp", gv_all[:, 2 * l_, :], bass.AP(norm_mix_pre, l_ * D, [[1, 128], [128, 8]]), [], ["gv_all"], slow=True)
            S.dma("sp", gv_all[:, 2 * l_ + 1, :], bass.AP(norm_ffn_pre, l_ * D, [[1, 128], [128, 8]]), [], ["gv_all"], slow=True)
        cwh_all = sb("cwh_all", [128, NHYB, 12, 4])
        cbh_all = sb("cbh_all", [128, NHYB, 12])
        for i_ in range(NHYB):
            for j_ in range(12):
                S.dma("sp", cwh_all[:, i_, j_, :], bass.AP(ssm_conv_w, i_ * 4 * SSM_XBC + j_ * 128, [[1, 128], [SSM_XBC, 4]]), [], ["cwh_all"], slow=True)
            S.dma("sp", cbh_all[:, i_, :], bass.AP(ssm_conv_b, i_ * SSM_XBC, [[1, 128], [128, 12]]), [], ["cbh_all"], slow=True)
        cwg_all = sb("cwg_all", [128, NG1, 32, 4])
        for i_ in range(NGDN):
            for j_ in range(32):
                S.dma("sp", cwg_all[:, i_, j_, :], bass.AP(gdn_conv_w, i_ * 4 * G_QKV + j_ * 128, [[1, 128], [G_QKV, 4]]), [], ["cwg_all"], slow=True)

        def load_bcast(dst, dkey, src_tensor, off, n):
            S.dma("sp", dst, bass.AP(src_tensor, off, [[0, 128], [1, n]]), [], [dkey])

        def rstd_from_ssq(col, n):
            S.op("act", lambda g: g.activation(out=stat[:, col:col + 1], in_=stat[:, col:col + 1], func=AF.Ln, bias=c_eps, scale=1.0 / n),
                 [("stat", col), "ccol"], [("stat", col)])
            S.op("act", lambda g: g.activation(out=stat[:, col:col + 1], in_=stat[:, col:col + 1], func=AF.Exp, scale=-0.5),
                 [("stat", col)], [("stat", col)])

        def rmsnorm_stats(src, skeys, n, col):
            S.op("act", lambda g: g.activation(out=Lb[:, 1024:1024 + n], in_=src, func=AF.Square, accum_out=stat[:, col:col + 1]),
                 skeys, ["Lb", ("stat", col)])
            rstd_from_ssq(col, n)

        def prenorm(which, l, need_rev):
            gvec = gv_all[:, 2 * l + which, :]
            rmsnorm_stats(xres[:, :], ["xres"], D, 0)
            S.op("dve", lambda g: g.tensor_scalar(xn[:, :], xres[:, :], stat[:, 0:1], None, ALU.mult),
                 ["xres", ("stat", 0)], ["Lb"])
            for rev in ([False, True] if need_rev else [False]):
                dst = hTr if rev else hT
                dk = "hT"
                ko = 12 if rev else 0
                dk = "mixT" if rev else "hT"
                for kc in range(8):
                    p_, pk_ = ps()
                    if rev:
                        S.op("pe", lambda g: g.matmul(p_[:, 0:128], xn[:, kc * 128:(kc + 1) * 128], antiI[:], start=True, stop=True),
                             ["Lb", "antiI"], [pk_])
                    else:
                        S.op("pe", lambda g: g.transpose(p_[:, 0:128], xn[:, kc * 128:(kc + 1) * 128], ident[:]), ["Lb", "ident"], [pk_])
                    if kc % 2:
                        S.op("dve", lambda g: g.tensor_scalar(dst[:, kc, :], p_[:, 0:128], gvec[:, kc:kc + 1], None, ALU.mult),
                             [pk_, "gv_all"], [(dk, ko + kc)])
                    else:
                        S.op("act", lambda g: g.activation(out=dst[:, kc, :], in_=p_[:, 0:128], func=AF.Copy, scale=gvec[:, kc:kc + 1]),
                             [pk_, "gv_all"], [(dk, ko + kc)])

        def load_w(W, l, K, N, c0, cw):
            KC = K // 128
            i = wb_rr[0]
            wb_rr[0] = (i + 1) % NWB
            wv = wbuf[i][:, 0:KC * cw].rearrange("p (k c) -> p k c", c=cw)
            src = bass.AP(W, l * K * N + c0, [[N, 128], [128 * N, KC], [1, cw]])
            S.dma(wq, wv, src, [], ["wbuf%d" % i])
            return wv, "wbuf%d" % i

        def proj_tok(W, l, K, N, c0, c1, src, skeys, evac):
            KC = K // 128
            cwmax = 512 if KC * 512 <= WBUF else (256 if KC * 256 <= WBUF else 128)
            c = c0
            while c < c1:
                cw = min(cwmax, c1 - c)
                wv, wk = load_w(W, l, K, N, c, cw)
                p_, pk_ = ps()
                for kc in range(KC):
                    S.op("pe", lambda g: g.matmul(p_[:, 0:cw], src[:, kc, :], wv[:, kc, :], start=(kc == 0), stop=(kc == KC - 1)),
                         [wk] + skeys, [pk_])
                evac(p_, pk_, c, cw)
                c += cw

        def proj_feat(W, l, K, N, c0, ncols, src, skeys, evac):
            KC = K // 128
            cwmax = 512 if KC * 512 <= WBUF else (256 if KC * 256 <= WBUF else 128)
            c = c0
            while c < c0 + ncols:
                cw = min(cwmax, c0 + ncols - c)
                wv, wk = load_w(W, l, K, N, c, cw)
                for j in range(cw // 128):
                    p_, pk_ = ps()
                    for kc in range(KC):
                        S.op("pe", lambda g: g.matmul(p_[:, 0:128], wv[:, kc, j * 128:(j + 1) * 128], src[:, kc, :],
                                                      start=(kc == 0), stop=(kc == KC - 1)), [wk] + skeys, [pk_])
                    evac(p_, pk_, (c - c0) // 128 + j)
                c += cw

        hkeys = [("hT", k) for k in range(8)]
        hrkeys = [("mixT", 12 + k) for k in range(8)]

        def post_norm_residual(gain_dram, l, src, skeys):
            load_bcast(gpost[:, :], "Bb", gain_dram, l * D, D)
            rmsnorm_stats(src, skeys, D, 1)
            S.op("dve", lambda g: g.scalar_tensor_tensor(out=xn[:, :], in0=src, scalar=stat[:, 1:2], in1=gpost[:, :],
                                                         op0=ALU.mult, op1=ALU.mult), skeys + [("stat", 1), "Bb"], ["Lb"])
            S.op("dve", lambda g: g.tensor_tensor(out=xres[:, :], in0=xres[:, :], in1=xn[:, :], op=ALU.add),
                 ["xres", "Lb"], ["xres"])

        def transposes_to(dst, dkey, j0, src, skeys, n, dtcast=True):
            for c in range(n):
                p_, pk_ = ps()
                S.op("pe", lambda g: g.transpose(p_[:, 0:128], src[:, c * 128:(c + 1) * 128], ident[:]), skeys + ["ident"], [pk_])
                copy(evac_eng(), dst[:, j0 + c, :], p_[:, 0:128], [pk_], [(dkey, j0 + c)])

        def ffn(l):
            prenorm(1, l, False)

            def ev_gate(p_, pk_, c, cw):
                S.op("act", lambda g: g.activation(out=big[:, c:c + cw], in_=p_[:, 0:cw], func=AF.Silu), [pk_], bk(c, c + cw))
            proj_tok(w_ffn_gate, l, D, D_FF, 0, D_FF, hT, hkeys, ev_gate)

            def ev_up(p_, pk_, c, cw):
                S.op("dve", lambda g: g.tensor_tensor(out=big[:, c:c + cw], in0=big[:, c:c + cw], in1=p_[:, 0:cw], op=ALU.mult),
                     [pk_] + bk(c, c + cw), bk(c, c + cw))
                for b_ in range(c // 128, (c + cw) // 128):
                    p2, pk2 = ps()
                    S.op("pe", lambda g: g.transpose(p2[:, 0:128], big[:, b_ * 128:(b_ + 1) * 128], ident[:]), bk(c, c + cw) + ["ident"], [pk2])
                    copy(evac_eng(), mixT[:, b_, :], p2[:, 0:128], [pk2], [("mixT", b_)])
            proj_tok(w_ffn_up, l, D, D_FF, 0, D_FF, hT, hkeys, ev_up)

            def ev_down(p_, pk_, c, cw):
                copy(evac_eng(), mixed[:, c:c + cw], p_[:, 0:cw], [pk_], [("mixed", c // 128)])
            proj_tok(w_ffn_down, l, D_FF, D, 0, D, mixT, [("mixT", c) for c in range(22)], ev_down)
            post_norm_residual(norm_ffn_post, l, mixed[:, 0:D], [("mixed", c) for c in range(8)])

        def decay_mats(gsrc, gkey):
            p_, pk_ = ps()
            S.op("pe", lambda g: g.matmul(p_[:, 0:16], Umat[:, :], gsrc, start=True, stop=True), ["Umat", gkey], [pk_])
            S.op("act", lambda g: g.activation(out=csb[:, :], in_=p_[:, 0:16], func=AF.Exp), [pk_], ["csb"])
            p_, pk_ = ps()
            S.op("pe", lambda g: g.matmul(p_[:, 0:16], ones[:, :], gsrc, start=True, stop=True), ["ones", gkey], [pk_])
            S.op("act", lambda g: g.activation(out=dec_b[:, :], in_=p_[:, 0:16], func=AF.Exp), [pk_], ["dec_b"])
            S.op("dve", lambda g: g.tensor_tensor(out=aU[:, :, :], in0=Umat[:, :].unsqueeze(1).to_broadcast([128, 16, 128]),
                                                   in1=gsrc.unsqueeze(2).to_broadcast([128, 16, 128]), op=ALU.mult),
                 ["Umat", gkey], ["Lb"])
            for q4 in range(4):
                p_, pk_ = ps()
                S.op("pe", lambda g: g.matmul(p_[:, 0:512], SLmat[:, :], aU[:, q4 * 4:(q4 + 1) * 4, :].rearrange("p h i -> p (h i)"),
                                              start=True, stop=True), ["SLmat", "Lb"], [pk_])
                S.op("act", lambda g: g.activation(out=seg[:, q4 * 4:(q4 + 1) * 4, :].rearrange("p h i -> p (h i)"), in_=p_[:, 0:512], func=AF.Exp),
                     [pk_], [("seg", q4)])
            S.op("dve", lambda g: g.tensor_copy(dec_w[:, :], seg[:, :, 127]), SEGK, ["dec_w"])

        def softplus_inplace(t, key):
            S.op("act", lambda g: g.activation(out=t, in_=t, func=AF.Exp), [key], [key])
            S.op("act", lambda g: g.activation(out=t, in_=t, func=AF.Ln, bias=c_one, scale=1.0), [key, "ccol"], [key])

        def conv_silu(nct, cwv, cwkey, hist, hkey, nvalid, bias_sb, base=0):
            xk = [("xbcT", base + j) for j in range(nct)]
            S.op("dve", lambda g: g.tensor_copy(xbcT[:, base:base + nct, 0:3], hist), [hkey] + xk, xk)
            S.op("dve", lambda g: g.tensor_copy(hist, xbcT[:, base:base + nct, nvalid:nvalid + 3]), xk, [hkey])
            for j in range(nct):
                bj = base + j
                S.op("dve", lambda g: g.tensor_scalar(xbcA[:, bj, :], xbcT[:, bj, 0:128], cwv[:, j, 0:1], None, ALU.mult),
                     [("xbcT", bj), cwkey], [("xbcA", bj)])
                for t in range(1, 4):
                    S.op("dve", lambda g: g.scalar_tensor_tensor(out=xbcA[:, bj, :], in0=xbcT[:, bj, t:t + 128], scalar=cwv[:, j, t:t + 1],
                                                                 in1=xbcA[:, bj, :], op0=ALU.mult, op1=ALU.add),
                         [("xbcT", bj), ("xbcA", bj), cwkey], [("xbcA", bj)])
                if bias_sb is not None:
                    S.op("act", lambda g: g.activation(out=xbcA[:, bj, :], in_=xbcA[:, bj, :], func=AF.Silu, bias=bias_sb[:, j:j + 1], scale=1.0),
                         [("xbcA", bj), "cbh_all"], [("xbcA", bj)])
                else:
                    S.op("act", lambda g: g.activation(out=xbcA[:, bj, :], in_=xbcA[:, bj, :], func=AF.Silu), [("xbcA", bj)], [("xbcA", bj)])

        def hybrid_load_params(i):
            load_bcast(dtb[:, :], "dtb", ssm_dt_bias, i * 16, 16)
            load_bcast(alog[:, :], "alog", ssm_a_log, i * 16, 16)
            S.op("act", lambda g: g.activation(out=alog[:, :], in_=alog[:, :], func=AF.Exp), ["alog"], ["alog"])
            S.op("dve", lambda g: g.tensor_scalar(alog[:, :], alog[:, :], -1.0, None, ALU.mult), ["alog"], ["alog"])
            load_bcast(dsk[:, :], "dsk", ssm_d, i * 16, 16)

        def hybrid_layer(l, i, sq, tpos, nvalid):
            full = (nvalid == 128)
            cst = conv_h[i]
            sst = ssm_st[i]
            ck = "conv_h%d" % i
            sk = "ssm_st%d" % i
            prenorm(0, l, True)

            def ev_q(p_, pk_, j):
                copy(evac_eng(), qT[:, j, :], p_[:, 0:128], [pk_], [("qT", j)])
            proj_feat(w_hyb_in, i, D, HYB_IN, 0, 512, hTr, hrkeys, ev_q)

            def ev_kT(p_, pk_, j):
                copy(evac_eng(), kTt[:, j, :], p_[:, 0:128], [pk_], [("kTt", j)])
            proj_feat(w_hyb_in, i, D, HYB_IN, 512, 512, hT, hkeys, ev_kT)
            scr_base = (i * NSEQ + sq) * 128 * 4 * SCR_ROWS
            S.dma("sp", bass.AP(kt_scr, scr_base + tpos * 128, [[4 * SCR_ROWS, 128], [SCR_ROWS, 4], [1, 128]]),
                  kTt[:, :, :], [("kTt", j) for j in range(4)], [("kt_scr", i, sq)])

            def ev_kvz(p_, pk_, c, cw):
                copy(evac_eng(), big[:, c - 512:c - 512 + cw], p_[:, 0:cw], [pk_], bk(c - 512, c - 512 + cw))
            proj_tok(w_hyb_in, i, D, HYB_IN, 512, 2560, hT, hkeys, ev_kvz)
            if not full:
                S.op("dve", lambda g: g.tensor_scalar(big[:, 0:1024], big[:, 0:1024], valid[:, 0:1], None, ALU.mult),
                     bk(0, 1024) + ["valid"], bk(0, 1024))
            S.dma("sp", k_scr[i, sq, tpos * 128:(tpos + 1) * 128, :], big[:, 0:512], bk(0, 512), [("k_scr", i, sq)])
            S.dma("sp", v_scr[i, sq, tpos * 128:(tpos + 1) * 128, :], big[:, 512:1024], bk(512, 1024), [("v_scr", i, sq)])

            def ev_dt(p_, pk_, c, cw):
                copy("dve", big[:, 2048:2064], p_[:, 0:16], [pk_], bk(2048, 2064))
            proj_tok(w_hyb_in, i, D, HYB_IN, 4096, 4112, hT, hkeys, ev_dt)

            def ev_xbc(p_, pk_, j):
                copy(evac_eng(), xbcT[:, j, 3:131], p_[:, 0:128], [pk_], [("xbcT", j)])
            proj_feat(w_hyb_in, i, D, HYB_IN, 2560, 1536, hT, hkeys, ev_xbc)

            t_lo = max(0, tpos - 16)
            nk = tpos - t_lo + 1
            W_ = nk * 128
            off_c = (NKT - nk) * 128
            for pr in range(4):
                S.dma("sp", ktw[:, 0:W_], bass.AP(kt_scr, scr_base + pr * SCR_ROWS + t_lo * 128, [[4 * SCR_ROWS, 128], [1, W_]]),
                      [("kt_scr", i, sq)], ["ktw"])
                S.dma("sp", vw[:, 0:nk, :], bass.AP(v_scr, ((i * NSEQ + sq) * SCR_ROWS + t_lo * 128) * A_W + pr * 128,
                                                    [[A_W, 128], [128 * A_W, nk], [1, 128]]), [("v_scr", i, sq)], ["vw"])
                for hh in range(2):
                    h = pr * 2 + hh
                    pb = hh * 64
                    S.dma("sp", Bb[:, 0:W_], bass.AP(ftab, h * TABL + off_c, [[1, 128], [1, W_]]), ["ftab"], ["Bb"])
                    for c0 in range(0, W_, 512):
                        cw = min(512, W_ - c0)
                        p_, pk_ = ps()
                        S.op("pe", lambda g: g.matmul(p_[:, 0:cw], qT[pb:pb + 64, pr, :], ktw[pb:pb + 64, c0:c0 + cw], start=True, stop=True),
                             [("qT", pr), "ktw"], [pk_])
                        S.op("dve", lambda g: g.scalar_tensor_tensor(out=Lb[:, c0:c0 + cw], in0=p_[:, 0:cw], scalar=0.125, in1=Bb[:, c0:c0 + cw],
                                                                     op0=ALU.mult, op1=ALU.add), [pk_, "Bb"], ["Lb"])
                    S.op("dve", lambda g: g.reduce_max(out=mx[:, 0:1], in_=Lb[:, 0:W_], axis=AX.X), ["Lb"], ["mx0"])
                    S.op("dve", lambda g: g.tensor_scalar(mx[:, 1:2], mx[:, 0:1], -1.0, None, ALU.mult), ["mx0"], ["mx1"])
                    S.op("act", lambda g: g.activation(out=Lb[:, 0:W_], in_=Lb[:, 0:W_], func=AF.Exp, bias=mx[:, 1:2], scale=1.0,
                                                       accum_out=mx[:, 2:3]), ["Lb", "mx1"], ["Lb", "mx2"])
                    S.op("dve", lambda g: g.reciprocal(mx[:, 3:4], mx[:, 2:3]), ["mx2"], ["mx3"])
                    po, pok = ps_acc()
                    for kt in range(nk):
                        p_, pk_ = ps()
                        S.op("pe", lambda g: g.transpose(p_[:, 0:128], Lb[:, kt * 128:(kt + 1) * 128], ident[:]), ["Lb", "ident"], [pk_])
                        copy(evac_eng(), ETb[:, kt % 4, :], p_[:, 0:128], [pk_], [("ETb", kt % 4)])
                        S.op("pe", lambda g: g.matmul(po[:, 0:64], ETb[:, kt % 4, :], vw[:, kt, pb:pb + 64], start=(kt == 0), stop=(kt == nk - 1)),
                             [("ETb", kt % 4), "vw"], [pok])
                    S.op("dve", lambda g: g.tensor_scalar(mixed[:, h * 64:(h + 1) * 64], po[:, 0:64], mx[:, 3:4], None, ALU.mult),
                         [pok, "mx3"], [("mixed", h // 2)])
            for c in range(4):
                p_, pk_ = ps()
                S.op("pe", lambda g: g.matmul(p_[:, 0:128], mixed[:, c * 128:(c + 1) * 128], antiI[:], start=True, stop=True),
                     [("mixed", c), "antiI"], [pk_])
                copy(evac_eng(), mixT[:, c, :], p_[:, 0:128], [pk_], [("mixT", c)])

            conv_silu(12, cwh_all[:, i, :, :], "cwh_all", cst[:, :, :], ck, nvalid, cbh_all[:, i, :])
            for c in range(8):
                p_, pk_ = ps()
                S.op("pe", lambda g: g.transpose(p_[:, 0:128], xbcA[:, c, :], ident[:]), [("xbcA", c), "ident"], [pk_])
                copy(evac_eng(), xtok[:, c * 128:(c + 1) * 128], p_[:, 0:128], [pk_], [("xtok", c)])
            for c in range(2):
                p_, pk_ = ps()
                S.op("pe", lambda g: g.transpose(p_[:, 0:128], xbcA[:, 8 + c, :], ident[:]), [("xbcA", 8 + c), "ident"], [pk_])
                copy(evac_eng(), btok[:, c * 128:(c + 1) * 128], p_[:, 0:128], [pk_], [("btok", c)])
            xtk = [("xtok", c) for c in range(8)]
            S.op("dve", lambda g: g.tensor_tensor(out=dtt[:, :], in0=big[:, 2048:2064], in1=dtb[:, :], op=ALU.add), bk(2048, 2064) + ["dtb"], ["dtt"])
            softplus_inplace(dtt[:, :], "dtt")
            if not full:
                S.op("dve", lambda g: g.tensor_scalar(dtt[:, :], dtt[:, :], valid[:, 0:1], None, ALU.mult), ["dtt", "valid"], ["dtt"])
                S.op("dve", lambda g: g.tensor_scalar(xtok[:, :], xtok[:, :], valid[:, 0:1], None, ALU.mult), xtk + ["valid"], xtk)
            S.op("dve", lambda g: g.tensor_tensor(out=av[:, :], in0=dtt[:, :], in1=alog[:, :], op=ALU.mult), ["dtt", "alog"], ["av"])
            decay_mats(av[:, :], "av")
            S.op("dve", lambda g: g.tensor_tensor(out=dec_w[:, :], in0=dec_w[:, :], in1=dtt[:, :], op=ALU.mult), ["dec_w", "dtt"], ["dec_w"])
            S.op("dve", lambda g: g.tensor_tensor(out=xw[:, :].rearrange("p (h d) -> p h d", d=64), in0=xtok[:, :].rearrange("p (h d) -> p h d", d=64),
                                                   in1=dec_w[:, :].unsqueeze(2).to_broadcast([128, 16, 64]), op=ALU.mult), xtk + ["dec_w"], ["xw"])
            for g_ in range(2):
                p_, pk_ = ps()
                S.op("pe", lambda g: g.matmul(p_[:, 0:128], xbcA[:, 8 + g_, :], xbcA[:, 10 + g_, :], start=True, stop=True),
                     [("xbcA", 8 + g_), ("xbcA", 10 + g_)], [pk_])
                S.op("dve", lambda g: g.tensor_tensor(out=cbm[:, g_, :], in0=p_[:, 0:128], in1=Umat[:, :], op=ALU.mult), [pk_, "Umat"], [("cbm", g_)])
            S.op("dve", lambda g: g.tensor_tensor(out=seg[:, :, :], in0=seg[:, :, :], in1=dtt[:, :].unsqueeze(2).to_broadcast([128, 16, 128]), op=ALU.mult),
                 SEGK + ["dtt", "dec_w"], SEGK)
            for g_ in range(2):
                S.op("dve", lambda g: g.tensor_tensor(out=seg[:, g_ * 8:(g_ + 1) * 8, :], in0=seg[:, g_ * 8:(g_ + 1) * 8, :],
                                                       in1=cbm[:, g_, :].unsqueeze(1).to_broadcast([128, 8, 128]), op=ALU.mult),
                     SEGK + [("cbm", g_)], SEGK)
            for g_ in range(2):
                p_, pk_ = ps()
                S.op("pe", lambda g: g.matmul(p_[:, 0:512], xbcA[:, 10 + g_, :], sst[:, g_ * 512:(g_ + 1) * 512], start=True, stop=True),
                     [("xbcA", 10 + g_), sk], [pk_])
                S.op("dve", lambda g: g.tensor_tensor(out=ysb[:, g_ * 512:(g_ + 1) * 512].rearrange("p (h d) -> p h d", d=64),
                                                      in0=p_[:, 0:512].rearrange("p (h d) -> p h d", d=64),
                                                      in1=csb[:, g_ * 8:(g_ + 1) * 8].unsqueeze(2).to_broadcast([128, 8, 64]), op=ALU.mult),
                     [pk_, "csb"], [("ysb", g_)])
            for g_ in range(2):
                p_, pk_ = ps()
                for hh in range(8):
                    h = g_ * 8 + hh
                    S.op("pe", lambda g: g.matmul(p_[:, hh * 64:(hh + 1) * 64], seg[:, h, :], xtok[:, h * 64:(h + 1) * 64], start=True, stop=True),
                         SEGK + xtk, [pk_])
                S.op("dve", lambda g: g.tensor_tensor(out=ysb[:, g_ * 512:(g_ + 1) * 512], in0=ysb[:, g_ * 512:(g_ + 1) * 512], in1=p_[:, 0:512], op=ALU.add),
                     [pk_, ("ysb", g_)], [("ysb", g_)])
            for g_ in range(2):
                p_, pk_ = ps()
                S.op("pe", lambda g: g.matmul(p_[:, 0:512], btok[:, g_ * 128:(g_ + 1) * 128], xw[:, g_ * 512:(g_ + 1) * 512], start=True, stop=True),
                     [("btok", g_), "xw"], [pk_])
                S.op("dve", lambda g: g.tensor_tensor(out=sst[:, g_ * 512:(g_ + 1) * 512].rearrange("p (h d) -> p h d", d=64),
                                                       in0=sst[:, g_ * 512:(g_ + 1) * 512].rearrange("p (h d) -> p h d", d=64),
                                                       in1=dec_b[:, g_ * 8:(g_ + 1) * 8].unsqueeze(2).to_broadcast([128, 8, 64]), op=ALU.mult),
                     [sk, "dec_b"], [sk])
                S.op("dve", lambda g: g.tensor_tensor(out=sst[:, g_ * 512:(g_ + 1) * 512], in0=sst[:, g_ * 512:(g_ + 1) * 512], in1=p_[:, 0:512], op=ALU.add),
                     [pk_, sk], [sk])
            yk = [("ysb", 0), ("ysb", 1)]
            S.op("dve", lambda g: g.tensor_tensor(out=xw[:, :].rearrange("p (h d) -> p h d", d=64), in0=xtok[:, :].rearrange("p (h d) -> p h d", d=64),
                                                   in1=dsk[:, :].unsqueeze(2).to_broadcast([128, 16, 64]), op=ALU.mult), xtk + ["dsk", "xw"], ["xw"])
            S.op("dve", lambda g: g.tensor_tensor(out=ysb[:, :], in0=ysb[:, :], in1=xw[:, :], op=ALU.add), yk + ["xw"], yk)
            S.op("act", lambda g: g.activation(out=big[:, 1024:2048], in_=big[:, 1024:2048], func=AF.Silu), bk(1024, 2048), bk(1024, 2048))
            S.op("dve", lambda g: g.tensor_tensor(out=ysb[:, :], in0=ysb[:, :], in1=big[:, 1024:2048], op=ALU.mult), yk + bk(1024, 2048), yk)
            load_bcast(gpost[:, :], "Bb", ssm_norm_w, i * SSM_DI, SSM_DI)
            snw = gpost
            for g_ in range(2):
                rmsnorm_stats(ysb[:, g_ * 512:(g_ + 1) * 512], [("ysb", g_)], 512, 2 + g_)
                S.op("dve", lambda g: g.scalar_tensor_tensor(out=mixed[:, 512 + g_ * 512:1024 + g_ * 512], in0=ysb[:, g_ * 512:(g_ + 1) * 512],
                                                             scalar=stat[:, 2 + g_:3 + g_], in1=snw[:, g_ * 512:(g_ + 1) * 512], op0=ALU.mult, op1=ALU.mult),
                     [("ysb", g_), ("stat", 2 + g_), "Bb"], [("mixed", 4 + 4 * g_ + c) for c in range(4)])
            transposes_to(mixT, "mixT", 4, mixed[:, 512:1536], [("mixed", 4 + c) for c in range(8)], 8)

            def ev_out(p_, pk_, c, cw):
                copy(evac_eng(), big[:, 3072 + c:3072 + c + cw], p_[:, 0:cw], [pk_], bk(3072 + c, 3072 + c + cw))
            proj_tok(w_hyb_out, i, HYB_MIX, D, 0, D, mixT, [("mixT", c) for c in range(12)], ev_out)
            post_norm_residual(norm_mix_post, l, big[:, 3072:4096], bk(3072, 4096))

        gdtb = dtb
        galog = alog
        gnw = sb("gnw", [128, 128])
        knT = xtok[:, :].rearrange("p (h d) -> p h d", d=128)
        QKT = xw[:, :].rearrange("p (h d) -> p h d", d=128)
        def mk_set(parts):
            return parts
        gs = []
        gs_alias = []
        vwf = vw[:, :, :].rearrange("p a b -> p (a b)")
        for base, al in [(ktw, ["ktw"]), (vwf, ["vw"]), (Bb, ["Bb"])]:
            dct = {}
            for n_, nm in enumerate(["s0", "s1", "s2", "s3", "s4", "s5"]):
                dct[nm] = base[:, n_ * 128:(n_ + 1) * 128]
            dct["XA"] = base[:, 768:1024]
            dct["XB"] = base[:, 1024:1280]
            gs.append(dct)
            gs_alias.append(al)
        dct = {}
        for n_, nm in enumerate(["s0", "s1", "s2", "s3", "s4", "s5"]):
            dct[nm] = ktw[:, 1280 + n_ * 128:1280 + (n_ + 1) * 128]
        dct["XA"] = vwf[:, 1280:1536]
        dct["XB"] = vwf[:, 1536:1792]
        gs.append(dct)
        gs_alias.append(["ktw", "vw"])

        def gdn_load_params(i):
            load_bcast(gdtb[:, :], "dtb", gdn_dt_bias, i * 16, 16)
            load_bcast(galog[:, :], "alog", gdn_a_log, i * 16, 16)
            S.op("act", lambda g: g.activation(out=galog[:, :], in_=galog[:, :], func=AF.Exp), ["alog"], ["alog"])
            S.op("dve", lambda g: g.tensor_scalar(galog[:, :], galog[:, :], -1.0, None, ALU.mult), ["alog"], ["alog"])
            load_bcast(gnw[:, :], "gnw", gdn_norm_w, i * 128, 128)

        def gdn_layer(l, i, sq, tpos, nvalid):
            full = (nvalid == 128)
            cst = gconv_h[i]
            ck = "gconv_h%d" % i
            Sst = gdn_st[i]
            prenorm(0, l, False)
            S.op("pool", lambda g: g.memset(ktw[0:1, 0:1], 0.0), [], ["ktw"])
            S.op("pool", lambda g: g.memset(vw[0:1, 0:1, 0:1], 0.0), [], ["vw"])
            S.op("pool", lambda g: g.memset(Bb[0:1, 0:1], 0.0), [], ["Bb"])

            def ev_zba(p_, pk_, c, cw):
                copy(evac_eng(), big[:, c:c + cw], p_[:, 0:cw], [pk_], bk(c, c + cw))
            proj_tok(w_gdn_in, i, D, G_IN, 4096, G_IN, hT, hkeys, ev_zba)
            def proj_k(k):
                slot = (k % 3) * 4

                def ev_x(p_, pk_, j):
                    copy(evac_eng(), xbcT[:, slot + j, 3:131], p_[:, 0:128], [pk_], [("xbcT", slot + j)])
                proj_feat(w_gdn_in, i, D, G_IN, k * 512, 512, hT, hkeys, ev_x)

            def conv_k(k):
                slot = (k % 3) * 4
                conv_silu(4, cwg_all[:, i, k * 4:(k + 1) * 4, :], "cwg_all", cst[:, k * 4:(k + 1) * 4, :], ck, nvalid, None, base=slot)

            def tr_k(k):
                slot = (k % 3) * 4
                for j in range(4):
                    ct = k * 4 + j
                    p_, pk_ = ps()
                    S.op("pe", lambda g: g.transpose(p_[:, 0:128], xbcA[:, slot + j, :], ident[:]), [("xbcA", slot + j), "ident"], [pk_])
                    copy(evac_eng(), big[:, ct * 128:(ct + 1) * 128], p_[:, 0:128], [pk_], bk(ct * 128, (ct + 1) * 128))
            proj_k(0)
            for k in range(8):
                conv_k(k)
                if k + 1 < 8:
                    proj_k(k + 1)
                tr_k(k)
            S.op("dve", lambda g: g.tensor_tensor(out=Lb[:, 0:2048], in0=big[:, 0:2048], in1=big[:, 0:2048], op=ALU.mult), bk(0, 2048), ["Lb"])
            S.op("dve", lambda g: g.reduce_sum(out=rn[:, :], in_=Lb[:, 0:2048].rearrange("p (h d) -> p h d", d=128), axis=AX.X), ["Lb"], ["rn"])
            S.op("act", lambda g: g.activation(out=rn[:, :], in_=rn[:, :], func=AF.Ln, bias=c_eps, scale=1.0), ["rn", "ccol"], ["rn"])
            S.op("act", lambda g: g.activation(out=rn[:, :], in_=rn[:, :], func=AF.Exp, scale=-0.5), ["rn"], ["rn"])
            S.op("dve", lambda g: g.tensor_scalar(rn[:, 0:8], rn[:, 0:8], 128.0 ** -0.5, None, ALU.mult), ["rn"], ["rn"])
            if not full:
                S.op("dve", lambda g: g.tensor_scalar(rn[:, :], rn[:, :], valid[:, 0:1], None, ALU.mult), ["rn", "valid"], ["rn"])
                S.op("dve", lambda g: g.tensor_scalar(big[:, 2048:4096], big[:, 2048:4096], valid[:, 0:1], None, ALU.mult),
                     bk(2048, 4096) + ["valid"], bk(2048, 4096))
            S.op("dve", lambda g: g.tensor_tensor(out=big[:, 0:2048].rearrange("p (h d) -> p h d", d=128), in0=big[:, 0:2048].rearrange("p (h d) -> p h d", d=128),
                                                  in1=rn[:, :].unsqueeze(2).to_broadcast([128, 16, 128]), op=ALU.mult), bk(0, 2048) + ["rn"], bk(0, 2048))
            S.op("act", lambda g: g.activation(out=beta[:, :], in_=big[:, 6144:6160], func=AF.Exp, scale=-1.0), bk(6144, 6160), ["beta"])
            S.op("dve", lambda g: g.tensor_scalar(beta[:, :], beta[:, :], 1.0, None, ALU.add), ["beta"], ["beta"])
            S.op("dve", lambda g: g.reciprocal(beta[:, :], beta[:, :]), ["beta"], ["beta"])
            S.op("dve", lambda g: g.tensor_tensor(out=dtt[:, :], in0=big[:, 6160:6176], in1=gdtb[:, :], op=ALU.add), bk(6160, 6176) + ["dtb"], ["dtt"])
            softplus_inplace(dtt[:, :], "dtt")
            S.op("dve", lambda g: g.tensor_tensor(out=av[:, :], in0=dtt[:, :], in1=galog[:, :], op=ALU.mult), ["dtt", "alog"], ["av"])
            if not full:
                S.op("dve", lambda g: g.tensor_scalar(beta[:, :], beta[:, :], valid[:, 0:1], None, ALU.mult), ["beta", "valid"], ["beta"])
                S.op("dve", lambda g: g.tensor_scalar(av[:, :], av[:, :], valid[:, 0:1], None, ALU.mult), ["av", "valid"], ["av"])
            decay_mats(av[:, :], "av")
            S.op("dve", lambda g: g.tensor_tensor(out=aU[:, :, :], in0=seg[:, :, :], in1=SUmat[:, :].unsqueeze(1).to_broadcast([128, 16, 128]), op=ALU.mult),
                 SEGK + ["SUmat", "Lb"], ["Lb"])
            S.op("dve", lambda g: g.tensor_tensor(out=seg[:, :, :], in0=seg[:, :, :], in1=Umat[:, :].unsqueeze(1).to_broadcast([128, 16, 128]), op=ALU.mult),
                 SEGK + ["Umat", "dec_w", "Lb"], SEGK)
            for hq in range(8):
                p_, pk_ = ps()
                S.op("pe", lambda g: g.transpose(p_[:, 0:128], big[:, 1024 + hq * 128:1152 + hq * 128], ident[:]), bk(1024, 2048) + ["ident"], [pk_])
                copy(evac_eng(), knT[:, hq, :], p_[:, 0:128], [pk_], [("xtok", hq)])
                p_, pk_ = ps()
                S.op("pe", lambda g: g.transpose(p_[:, 0:128], big[:, hq * 128:(hq + 1) * 128], ident[:]), bk(0, 1024) + ["ident"], [pk_])
                copy(evac_eng(), stage[:, hq % 4, :], p_[:, 0:128], [pk_], [("stage", hq % 4)])
                p_, pk_ = ps()
                S.op("pe", lambda g: g.matmul(p_[:, 0:128], knT[:, hq, :], stage[:, hq % 4, :], start=True, stop=True), [("xtok", hq), ("stage", hq % 4)], [pk_])
                copy(evac_eng(), QKT[:, hq, :], p_[:, 0:128], [pk_], ["xw"])
            def head_gen(h, par):
                hq = h // 2
                G = gs[par]
                ALS = gs_alias[par]

                def K(nm):
                    return "g%s%d" % (nm, par)

                def SO(e, fn, reads, writes):
                    S.op(e, fn, list(reads) + ALS, writes)

                def CP(e, out, in_, reads, writes):
                    copy(e, out, in_, list(reads) + ALS, writes)
                kcols = bk(1024 + hq * 128, 1152 + hq * 128)
                qcols = bk(hq * 128, (hq + 1) * 128)
                vcols = bk(2048 + h * 128, 2176 + h * 128)
                k_n = big[:, 1024 + hq * 128:1152 + hq * 128]
                q_n = big[:, hq * 128:(hq + 1) * 128]
                v_h = big[:, 2048 + h * 128:2176 + h * 128]
                SO("dve", lambda g: g.tensor_scalar(G["s0"], k_n, beta[:, h:h + 1], None, ALU.mult), kcols + ["beta"], [K("s0")])
                SO("dve", lambda g: g.tensor_scalar(G["XA"][:, 0:128], v_h, beta[:, h:h + 1], None, ALU.mult), vcols + ["beta"], [K("XA")])
                SO("dve", lambda g: g.tensor_scalar(G["XA"][:, 128:256], G["s0"], csb[:, h:h + 1], None, ALU.mult), [K("s0"), "csb", K("XA")], [K("XA")])
                p_, pk_ = ps()
                SO("pe", lambda g: g.transpose(p_[:, 0:128], G["s0"], ident[:]), [K("s0"), "ident"], [pk_])
                CP(evac_eng(), G["s1"], p_[:, 0:128], [pk_], [K("s1")])
                yield
                p_, pk_ = ps()
                SO("pe", lambda g: g.matmul(p_[:, 0:128], knT[:, hq, :], G["s1"], start=True, stop=True), [("xtok", hq), K("s1")], [pk_])
                SO("dve", lambda g: g.tensor_tensor(out=G["s2"], in0=p_[:, 0:128], in1=aU[:, h, :], op=ALU.mult), [pk_, "Lb"], [K("s2")])
                yield
                p_, pk_ = ps()
                SO("pe", lambda g: g.transpose(p_[:, 0:128], G["s2"], ident[:]), [K("s2"), "ident"], [pk_])
                CP(evac_eng(), G["s3"], p_[:, 0:128], [pk_], [K("s3")])
                p_, pk_ = ps()
                SO("pe", lambda g: g.matmul(p_[:, 0:256], G["s2"], G["XA"], start=True, stop=True), [K("s2"), K("XA")], [pk_])
                SO("dve", lambda g: g.tensor_tensor(out=G["XB"], in0=G["XA"], in1=p_[:, 0:256], op=ALU.subtract), [pk_, K("XA")], [K("XB")])
                yield
                Xc, Xn = "XB", "XA"
                Mc, Mn, MTc, MTn = "s3", "s5", "s2", "s4"
                for lvl in range(1, 7):
                    p_, pk_ = ps()
                    SO("pe", lambda g: g.matmul(p_[:, 0:128], G[Mc], G[MTc], start=True, stop=True), [K(Mc), K(MTc)], [pk_])
                    CP(evac_eng(), G[MTn], p_[:, 0:128], [pk_], [K(MTn)])
                    if lvl < 6:
                        p_, pk_ = ps()
                        SO("pe", lambda g: g.matmul(p_[:, 0:128], G[MTc], G[Mc], start=True, stop=True), [K(Mc), K(MTc)], [pk_])
                        CP(evac_eng(), G[Mn], p_[:, 0:128], [pk_], [K(Mn)])
                    yield
                    p_, pk_ = ps()
                    SO("pe", lambda g: g.matmul(p_[:, 0:256], G[MTn], G[Xc], start=True, stop=True), [K(MTn), K(Xc)], [pk_])
                    SO("dve", lambda g: g.tensor_tensor(out=G[Xn], in0=G[Xc], in1=p_[:, 0:256], op=ALU.add), [pk_, K(Xc)], [K(Xn)])
                    yield
                    Xc, Xn = Xn, Xc
                    Mc, Mn = Mn, Mc
                    MTc, MTn = MTn, MTc
                X = G[Xc]
                p_, pk_ = ps()
                SO("pe", lambda g: g.transpose(p_[:, 0:128], X[:, 128:256], ident[:]), [K(Xc), "ident"], [pk_])
                CP(evac_eng(), G["s0"], p_[:, 0:128], [pk_], [K("s0")])
                SO("dve", lambda g: g.tensor_scalar(G["s3"], q_n, csb[:, h:h + 1], None, ALU.mult), qcols + ["csb"], [K("s3")])
                yield
                stk = ("gdn_st", i, h)
                p_, pk_ = ps()
                SO("pe", lambda g: g.matmul(p_[:, 0:128], G["s0"], Sst[:, h, :], start=True, stop=True), [K("s0"), stk], [pk_])
                SO("dve", lambda g: g.tensor_tensor(out=G["s1"], in0=X[:, 0:128], in1=p_[:, 0:128], op=ALU.subtract), [pk_, K(Xc)], [K("s1")])
                p_, pk_ = ps()
                SO("pe", lambda g: g.transpose(p_[:, 0:128], G["s3"], ident[:]), [K("s3"), "ident"], [pk_])
                CP(evac_eng(), G["s5"], p_[:, 0:128], [pk_], [K("s5")])
                SO("dve", lambda g: g.tensor_tensor(out=G["s2"], in0=QKT[:, hq, :], in1=seg[:, h, :], op=ALU.mult),
                   ["xw"] + SEGK, [K("s2")])
                SO("dve", lambda g: g.tensor_scalar(G["s4"], k_n, dec_w[:, h:h + 1], None, ALU.mult), kcols + ["dec_w"], [K("s4")])
                yield
                p_, pk_ = ps()
                SO("pe", lambda g: g.matmul(p_[:, 0:128], G["s5"], Sst[:, h, :], start=True, stop=False), [K("s5"), stk], [pk_])
                SO("pe", lambda g: g.matmul(p_[:, 0:128], G["s2"], G["s1"], start=False, stop=True), [K("s2"), K("s1")], [pk_])
                CP(evac_eng(), mixed[:, h * 128:(h + 1) * 128], p_[:, 0:128], [pk_], [("mixed", h)])
                p_, pk_ = ps()
                SO("pe", lambda g: g.matmul(p_[:, 0:128], G["s4"], G["s1"], start=True, stop=True), [K("s4"), K("s1")], [pk_])
                SO("dve", lambda g: g.scalar_tensor_tensor(out=Sst[:, h, :], in0=Sst[:, h, :], scalar=dec_b[:, h:h + 1], in1=p_[:, 0:128],
                                                           op0=ALU.mult, op1=ALU.add), [pk_, stk, "dec_b"], [stk])
                yield

            NGRP = len(gs)
            for h0 in range(0, 16, NGRP):
                gens = [head_gen(h0 + j, j) for j in range(min(NGRP, 16 - h0))]
                alive = list(gens)
                while alive:
                    nxt = []
                    for g_ in alive:
                        try:
                            next(g_)
                            nxt.append(g_)
                        except StopIteration:
                            pass
                    alive = nxt
            mk = [("mixed", h) for h in range(16)]
            S.op("dve", lambda g: g.tensor_tensor(out=Lb[:, 0:2048], in0=mixed[:, :], in1=mixed[:, :], op=ALU.mult), mk, ["Lb"])
            S.op("dve", lambda g: g.reduce_sum(out=rn[:, :], in_=Lb[:, 0:2048].rearrange("p (h d) -> p h d", d=128), axis=AX.X), ["Lb"], ["rn"])
            S.op("act", lambda g: g.activation(out=rn[:, :], in_=rn[:, :], func=AF.Ln, bias=c_eps, scale=1.0 / 128), ["rn", "ccol"], ["rn"])
            S.op("act", lambda g: g.activation(out=rn[:, :], in_=rn[:, :], func=AF.Exp, scale=-0.5), ["rn"], ["rn"])
            S.op("dve", lambda g: g.tensor_tensor(out=mixed[:, :].rearrange("p (h d) -> p h d", d=128), in0=mixed[:, :].rearrange("p (h d) -> p h d", d=128),
                                                  in1=rn[:, :].unsqueeze(2).to_broadcast([128, 16, 128]), op=ALU.mult), mk + ["rn"], mk)
            S.op("dve", lambda g: g.tensor_tensor(out=mixed[:, :].rearrange("p (h d) -> p h d", d=128), in0=mixed[:, :].rearrange("p (h d) -> p h d", d=128),
                                                   in1=gnw[:, :].unsqueeze(1).to_broadcast([128, 16, 128]), op=ALU.mult), mk + ["gnw"], mk)
            S.op("act", lambda g: g.activation(out=big[:, 4096:6144], in_=big[:, 4096:6144], func=AF.Silu), bk(4096, 6144), bk(4096, 6144))
            S.op("dve", lambda g: g.tensor_tensor(out=mixed[:, :], in0=mixed[:, :], in1=big[:, 4096:6144], op=ALU.mult), mk + bk(4096, 6144), mk)
            transposes_to(mixT, "mixT", 0, mixed, mk, 16)

            def ev_out(p_, pk_, c, cw):
                copy(evac_eng(), big[:, c:c + cw], p_[:, 0:cw], [pk_], bk(c, c + cw))
            proj_tok(w_gdn_out, i, G_VW, D, 0, D, mixT, [("mixT", c) for c in range(16)], ev_out)
            post_norm_residual(norm_mix_post, l, big[:, 0:1024], bk(0, 1024))

        def conv_state_load(dst, dkey, nct, src_tensor, off, width):
            S.dma("sp", tm3[0:3, 0:width], bass.AP(src_tensor, off, [[width, 3], [1, width]]), [], BIGK)
            for j in range(nct):
                p_, pk_ = ps()
                S.op("pe", lambda g: g.transpose(p_[:, 0:3], tm3[0:3, j * 128:(j + 1) * 128], ident[0:3, 0:3]), BIGK + ["ident"], [pk_])
                copy("dve", dst[:, j, :], p_[:, 0:3], [pk_], [dkey])

        def conv_state_store(src, skey, nct, dst_tensor, off, width):
            for j in range(nct):
                p_, pk_ = ps()
                S.op("pe", lambda g: g.transpose(p_[0:3, 0:128], src[:, j, :], ident[:]), [skey, "ident"], [pk_])
                copy("dve", tm3[0:3, j * 128:(j + 1) * 128], p_[0:3, 0:128], [pk_], BIGK)
            S.dma("sp", bass.AP(dst_tensor, off, [[width, 3], [1, width]]), tm3[0:3, 0:width], BIGK, [("out", dst_tensor.name)])

        def ssm_state_load(i, src_tensor, off):
            S.dma("sp", Lb[:, 0:1024].rearrange("p (b n) -> p b n", n=128), bass.AP(src_tensor, off, [[128, 128], [128 * 128, 8], [1, 128]]), [], ["Lb"])
            for b in range(8):
                p_, pk_ = ps()
                S.op("pe", lambda g: g.transpose(p_[:, 0:128], Lb[:, b * 128:(b + 1) * 128], ident[:]), ["Lb", "ident"], [pk_])
                copy(evac_eng(), ssm_st[i][:, b * 128:(b + 1) * 128], p_[:, 0:128], [pk_], ["ssm_st%d" % i])

        def ssm_state_store(i, dst_tensor, off):
            for b in range(8):
                p_, pk_ = ps()
                S.op("pe", lambda g: g.transpose(p_[:, 0:128], ssm_st[i][:, b * 128:(b + 1) * 128], ident[:]), ["ssm_st%d" % i, "ident"], [pk_])
                copy(evac_eng(), Lb[:, b * 128:(b + 1) * 128], p_[:, 0:128], [pk_], ["Lb"])
            S.dma("sp", bass.AP(dst_tensor, off, [[128, 128], [128 * 128, 8], [1, 128]]), Lb[:, 0:1024].rearrange("p (b n) -> p b n", n=128),
                  ["Lb"], [("out", dst_tensor.name)])

        def flat_copy(dst_tensor, doff, src_tensor, soff, nelem, rkeys, wkeys):
            assert nelem % 128 == 0
            per = nelem // 128
            S.dma("sp", bass.AP(dst_tensor, doff, [[per, 128], [1, per]]), bass.AP(src_tensor, soff, [[per, 128], [1, per]]), rkeys, wkeys)

        def seq_begin(sq, kind, sidx):
            for i in range(NHYB):
                ck = "conv_h%d" % i
                if kind == "p":
                    S.op("pool", lambda g: g.memset(conv_h[i][:, :, :], 0.0), [], [ck])
                    S.op("pool", lambda g: g.memset(ssm_st[i][:, :], 0.0), [], ["ssm_st%d" % i])
                else:
                    conv_state_load(conv_h[i], ck, 12, st_sconv, (i * NS1 + sidx) * 3 * SSM_XBC, SSM_XBC)
                    ssm_state_load(i, st_ssm, (i * NS1 + sidx) * 1024 * 128)
                    flat_copy(k_scr, (i * NSEQ + sq) * SCR_ROWS * A_W, cache_k, (i * NS1 + sidx) * WIN * A_W, WIN * A_W, [], [("k_scr", i, sq)])
                    flat_copy(v_scr, (i * NSEQ + sq) * SCR_ROWS * A_W, cache_v, (i * NS1 + sidx) * WIN * A_W, WIN * A_W, [], [("v_scr", i, sq)])
                    scr_base = (i * NSEQ + sq) * 128 * 4 * SCR_ROWS
                    for t in range(16):
                        S.dma("sp", Lb[:, 0:512], bass.AP(cache_k, ((i * NS1 + sidx) * WIN + t * 128) * A_W, [[A_W, 128], [1, A_W]]), [], ["Lb"])
                        for pr in range(4):
                            p_, pk_ = ps()
                            S.op("pe", lambda g: g.transpose(p_[:, 0:128], Lb[:, pr * 128:(pr + 1) * 128], ident[:]), ["Lb", "ident"], [pk_])
                            copy(evac_eng(), stage[:, pr, :], p_[:, 0:128], [pk_], [("stage", pr)])
                        S.dma("sp", bass.AP(kt_scr, scr_base + t * 128, [[4 * SCR_ROWS, 128], [SCR_ROWS, 4], [1, 128]]), stage[:, :, :],
                              [("stage", pr) for pr in range(4)], [("kt_scr", i, sq)])
            for i in range(NGDN):
                ck = "gconv_h%d" % i
                if kind == "p":
                    S.op("pool", lambda g: g.memset(gconv_h[i][:, :, :], 0.0), [], [ck])
                    S.op("pool", lambda g: g.memset(gdn_st[i][:, :, :], 0.0), [], [("gdn_st", i, h) for h in range(16)])
                else:
                    conv_state_load(gconv_h[i], ck, 32, st_gconv, (i * NS1 + sidx) * 3 * G_QKV, G_QKV)
                    S.dma("sp", gdn_st[i][:, :, :], bass.AP(st_gdn, (i * NS1 + sidx) * 16 * 128 * 128, [[128, 128], [128 * 128, 16], [1, 128]]),
                          [], [("gdn_st", i, h) for h in range(16)])

        def seq_end(sq, kind, sidx):
            for i in range(NHYB):
                ck = "conv_h%d" % i
                if kind == "p":
                    conv_state_store(conv_h[i], ck, 12, o_psc, i * 3 * SSM_XBC, SSM_XBC)
                    ssm_state_store(i, o_pss, i * 1024 * 128)
                    flat_copy(o_pk, i * KEEP * A_W, k_scr, ((i * NSEQ + sq) * SCR_ROWS + SEQ - KEEP) * A_W, KEEP * A_W, [("k_scr", i, sq)], [("out", "pk", i)])
                    flat_copy(o_pv, i * KEEP * A_W, v_scr, ((i * NSEQ + sq) * SCR_ROWS + SEQ - KEEP) * A_W, KEEP * A_W, [("v_scr", i, sq)], [("out", "pv", i)])
                else:
                    conv_state_store(conv_h[i], ck, 12, o_ssc, (i * NS1 + sidx) * 3 * SSM_XBC, SSM_XBC)
                    ssm_state_store(i, o_sss, (i * NS1 + sidx) * 1024 * 128)
                    flat_copy(o_sk, (i * NS1 + sidx) * WIN * A_W, k_scr, ((i * NSEQ + sq) * SCR_ROWS + 1) * A_W, WIN * A_W, [("k_scr", i, sq)], [("out", "sk", i, sidx)])
                    flat_copy(o_sv, (i * NS1 + sidx) * WIN * A_W, v_scr, ((i * NSEQ + sq) * SCR_ROWS + 1) * A_W, WIN * A_W, [("v_scr", i, sq)], [("out", "sv", i, sidx)])
            for i in range(NGDN):
                ck = "gconv_h%d" % i
                stk = [("gdn_st", i, h) for h in range(16)]
                if kind == "p":
                    conv_state_store(gconv_h[i], ck, 32, o_pgc, i * 3 * G_QKV, G_QKV)
                    S.dma("sp", bass.AP(o_pgs, i * 16 * 128 * 128, [[128, 128], [128 * 128, 16], [1, 128]]), gdn_st[i][:, :, :], stk, [("out", "pgs", i)])
                else:
                    conv_state_store(gconv_h[i], ck, 32, o_sgc, (i * NS1 + sidx) * 3 * G_QKV, G_QKV)
                    S.dma("sp", bass.AP(o_sgs, (i * NS1 + sidx) * 16 * 128 * 128, [[128, 128], [128 * 128, 16], [1, 128]]), gdn_st[i][:, :, :], stk,
                          [("out", "sgs", i, sidx)])

        for sq, (kind, sidx, t0, ntl) in enumerate(seqs):
            nvalid = 128 if kind == "p" else 1
            if kind == "s":
                S.op("pool", lambda g: g.memset(valid[:, :], 0.0), [], ["valid"])
                S.op("pool", lambda g: g.memset(valid[0:1, :], 1.0), ["valid"], ["valid"])
            seq_begin(sq, kind, sidx)
            for tl in range(ntl):
                tpos = t0 + tl
                if kind == "p":
                    S.dma("sp", xres[:, :], x_prompt[tl * 128:(tl + 1) * 128, :], [], ["xres"])
                else:
                    S.op("pool", lambda g: g.memset(xres[:, :], 0.0), [], ["xres"])
                    S.dma("sp", xres[0:1, :], x_sample[sidx:sidx + 1, :], ["xres"], ["xres"])
                for l in range(depth):
                    i = l // 2
                    if l % 2 == 0:
                        hybrid_load_params(i)
                        hybrid_layer(l, i, sq, tpos, nvalid)
                    else:
                        gdn_load_params(i)
                        gdn_layer(l, i, sq, tpos, nvalid)
                    ffn(l)
                if kind == "p":
                    S.dma("sp", y_prompt[tl * 128:(tl + 1) * 128, :], xres[:, :], ["xres"], ["y_prompt"])
                else:
                    S.dma("sp", y_sample[sidx:sidx + 1, :], xres[0:1, :], ["xres"], ["y_sample"])
            seq_end(sq, kind, sidx)

        for slot in S.dma_sems:
            if slot[1] > 0:
                S._wait("sp", (slot[0], slot[1]))
        for e in S.eng:
            if e != "sp" and S.cnt[e] > 0:
                S._wait("sp", (S.sem[e], S.cnt[e]))
        print("instructions:", S.ninstr, "waits:", S.nwait, "sems:", S.nsem, "sbuf_bytes/partition:", sb_total[0])
    return nc


_NC_CACHE = {}


def kernel(x_prompt, x_sample, cache_attn_k, cache_attn_v, state_ssm_conv, state_ssm, state_gdn_conv, state_gdn,
           rel_bias, norm_mix_pre, norm_mix_post, norm_ffn_pre, norm_ffn_post, w_hyb_in, ssm_conv_w, ssm_conv_b,
           ssm_dt_bias, ssm_a_log, ssm_d, ssm_norm_w, w_hyb_out, w_gdn_in, gdn_conv_w, gdn_dt_bias, gdn_a_log,
           gdn_norm_w, w_gdn_out, w_ffn_gate, w_ffn_up, w_ffn_down):
    f = lambda a: np.ascontiguousarray(np.asarray(a, dtype=np.float32))
    x_prompt = f(x_prompt)
    B, SEQ, _ = x_prompt.shape
    x_sample = f(x_sample)
    DB = x_sample.shape[0]
    depth = np.asarray(norm_mix_pre).shape[0]
    n_ptiles = SEQ // 128
    assert DB % NCORES == 0
    n_samp = DB // NCORES
    key = (n_ptiles, n_samp, depth)
    if key not in _NC_CACHE:
        _NC_CACHE[key] = build_nc(n_ptiles, n_samp, depth=depth)
    nc = _NC_CACHE[key]
    NHYB = (depth + 1) // 2
    NGDN = depth // 2
    shared = dict(
        rel_bias=f(rel_bias), oh_tab=attn_tables(), norm_mix_pre=f(norm_mix_pre), norm_mix_post=f(norm_mix_post),
        norm_ffn_pre=f(norm_ffn_pre), norm_ffn_post=f(norm_ffn_post), w_hyb_in=f(w_hyb_in), ssm_conv_w=f(ssm_conv_w),
        ssm_conv_b=f(ssm_conv_b), ssm_dt_bias=f(ssm_dt_bias), ssm_a_log=f(ssm_a_log), ssm_d=f(ssm_d), ssm_norm_w=f(ssm_norm_w),
        w_hyb_out=f(w_hyb_out), w_gdn_in=f(w_gdn_in), gdn_conv_w=f(gdn_conv_w), gdn_dt_bias=f(gdn_dt_bias),
        gdn_a_log=f(gdn_a_log), gdn_norm_w=f(gdn_norm_w), w_gdn_out=f(w_gdn_out), w_ffn_gate=f(w_ffn_gate),
        w_ffn_up=f(w_ffn_up), w_ffn_down=f(w_ffn_down))
    ck = f(cache_attn_k).reshape(NHYB, DB, WIN, A_W)
    cv = f(cache_attn_v).reshape(NHYB, DB, WIN, A_W)
    sc = f(state_ssm_conv)
    ss = f(state_ssm).reshape(NHYB, DB, 1024, 128)
    gc = f(state_gdn_conv)
    gst = f(state_gdn)
    in_maps = []
    for c in range(NCORES):
        sl = slice(c * n_samp, (c + 1) * n_samp)
        m = dict(shared)
        m["x_prompt"] = np.ascontiguousarray(x_prompt[c % B])
        m["x_sample"] = np.ascontiguousarray(x_sample[sl, 0, :])
        m["cache_k"] = np.ascontiguousarray(ck[:, sl])
        m["cache_v"] = np.ascontiguousarray(cv[:, sl])
        m["st_sconv"] = np.ascontiguousarray(sc[:, sl])
        m["st_ssm"] = np.ascontiguousarray(ss[:, sl])
        m["st_gconv"] = np.ascontiguousarray(gc[:, sl])
        m["st_gdn"] = np.ascontiguousarray(gst[:, sl])
        in_maps.append(m)
    res = run_bass_kernel_spmd(nc, in_maps, core_ids=list(range(NCORES)))
    R = res.results
    KEEP = min(WIN, SEQ)
    pc = list(range(B))
    y_prompt = np.stack([R[c]["y_prompt"] for c in pc], 0)
    y_sample = np.concatenate([R[c]["y_sample"] for c in range(NCORES)], 0)[:, None, :]
    pk = np.stack([R[c]["o_pk"] for c in pc], 1).reshape(NHYB, B, KEEP, 8, 64)
    pv = np.stack([R[c]["o_pv"] for c in pc], 1).reshape(NHYB, B, KEEP, 8, 64)
    psc = np.stack([R[c]["o_psc"] for c in pc], 1)
    pss = np.stack([R[c]["o_pss"] for c in pc], 1).reshape(NHYB, B, 16, 64, 128)
    pgc = np.stack([R[c]["o_pgc"] for c in pc], 1)
    pgs = np.stack([R[c]["o_pgs"] for c in pc], 1)
    cat = lambda nm: np.concatenate([R[c][nm] for c in range(NCORES)], 1)
    sk = cat("o_sk").reshape(NHYB, DB, WIN, 8, 64)
    sv = cat("o_sv").reshape(NHYB, DB, WIN, 8, 64)
    ssc = cat("o_ssc")
    sss = cat("o_sss").reshape(NHYB, DB, 16, 64, 128)
    sgc = cat("o_sgc")
    sgs = cat("o_sgs")
    return (y_prompt, y_sample, pk, pv, psc, pss, pgc, pgs, sk, sv, ssc, sss, sgc, sgs)
```

```python
import math
from contextlib import ExitStack
import numpy as np
import concourse.bass as bass
import concourse.mybir as mybir
from concourse.bass_utils import run_bass_kernel_spmd

F32 = mybir.dt.float32
F32R = mybir.dt.float32r
AF = mybir.ActivationFunctionType
ALU = mybir.AluOpType
AX = mybir.AxisListType

D = 1024
EPS = 1e-6
A_W = 512
WIN = 2048
NKT = 17
SSM_DI = 1024
SSM_XBC = 1536
HYB_IN = 4112
HYB_MIX = 1536
G_VW = 2048
G_QKV = 4096
G_IN = 6176
D_FF = 2816
NEG = -30000.0
NCORES = 8
TABL = 2304


def rel_buckets(dist):
    max_exact = 16
    n = np.maximum(dist, 1).astype(np.float32)
    large = max_exact + (np.log(n / max_exact) / math.log(2048 / max_exact) * (32 - max_exact)).astype(np.int32)
    large = np.minimum(large, 31)
    return np.where(dist < max_exact, dist, large).astype(np.int32)


def attn_tables():
    dist = 2175 - np.arange(TABL)
    valid = (dist >= 0) & (dist <= 2048)
    dc = np.clip(dist, 0, 2048)
    cnt = ((dc <= 128).astype(np.float64) + ((dc % 4 == 0) & (dc <= 512)) + ((dc % 16 == 0) & (dc <= 2048)))
    cnt = np.where(valid, cnt, 0.0)
    logc = np.where(cnt > 0, np.log(np.maximum(cnt, 1e-9)), NEG).astype(np.float32)
    bk = rel_buckets(dc)
    oh = np.zeros((33, TABL), np.float32)
    oh[bk, np.arange(TABL)] = np.where(cnt > 0, 1.0, 0.0)
    oh[32, :] = logc
    return oh


class Sched:
    def __init__(self, nc, es):
        self.nc = nc
        self.es = es
        self.eng = {"pe": nc.tensor, "act": nc.scalar, "dve": nc.vector, "pool": nc.gpsimd, "sp": nc.sync}
        self.sem = {}
        self.cnt = {}
        self.nsem = 0
        self.pe_sems = set()
        for e in self.eng:
            self._new_sem(e)
        self.dma_sems = []
        for i in range(48):
            self.dma_sems.append([es.enter_context(nc.semaphore("dq%d" % i)), 0])
        self.dma_rr = 0
        self.waited = {e: {} for e in self.eng}
        self.lastw = {}
        self.readers = {}
        self.ninstr = 0
        self.nwait = 0

    def _new_sem(self, e):
        self.sem[e] = self.es.enter_context(self.nc.semaphore("s_%s_%d" % (e, self.nsem)))
        if e == "pe":
            self.pe_sems.add(id(self.sem[e]))
        self.nsem += 1
        self.cnt[e] = 0

    def _wait(self, e, dep):
        sem, val = dep
        if e == "pe" and id(sem) in self.pe_sems:
            return
        w = self.waited[e]
        k = id(sem)
        if w.get(k, 0) >= val:
            return
        w[k] = val
        self.eng[e].wait_ge(sem, val)
        self.nwait += 1

    def _deps(self, e, reads, writes):
        for k in reads:
            d = self.lastw.get(k)
            if d is not None:
                self._wait(e, d)
        for k in writes:
            d = self.lastw.get(k)
            if d is not None:
                self._wait(e, d)
            r = self.readers.get(k)
            if r:
                for d in r.values():
                    self._wait(e, d)

    def _commit(self, tok, reads, writes):
        for k in writes:
            self.lastw[k] = tok
            self.readers[k] = {}
        for k in reads:
            r = self.readers.setdefault(k, {})
            r[id(tok[0])] = tok

    def op(self, e, fn, reads=(), writes=()):
        self._deps(e, reads, writes)
        if self.cnt[e] >= 30000:
            self._new_sem(e)
        inst = fn(self.eng[e])
        inst.then_inc(self.sem[e], 1)
        self.cnt[e] += 1
        self.ninstr += 1
        tok = (self.sem[e], self.cnt[e])
        self._commit(tok, reads, writes)

    def dma(self, e, out, in_, reads=(), writes=(), slow=False):
        self._deps(e, reads, writes)
        slot = self.dma_sems[self.dma_rr]
        self.dma_rr = (self.dma_rr + 1) % len(self.dma_sems)
        if slot[1] > 0:
            self._wait(e, (slot[0], slot[1]))
        if slot[1] >= 30000:
            slot[0] = self.es.enter_context(self.nc.semaphore("dqx%d" % self.nsem))
            self.nsem += 1
            slot[1] = 0
        if slow:
            self.eng[e].dma_start(out=out, in_=in_, allow_slow_non_contiguous=True).then_inc(slot[0], 16)
        else:
            self.eng[e].dma_start(out=out, in_=in_).then_inc(slot[0], 16)
        slot[1] += 16
        self.ninstr += 1
        tok = (slot[0], slot[1])
        self._commit(tok, reads, writes)


def build_nc(n_ptiles, n_samp, depth=4, mm_r=True, dbg=None):
    nc = bass.Bass("TRN2", target_bir_lowering=False)
    SEQ = n_ptiles * 128
    NHYB = (depth + 1) // 2
    NGDN = depth // 2
    NG1 = max(NGDN, 1)
    NS1 = max(n_samp, 1)
    KEEP = min(WIN, SEQ)
    MMDT = F32R if mm_r else F32
    wq = "pool" if mm_r else "sp"

    def din(name, shape):
        return nc.dram_tensor(name, list(shape), F32, kind="ExternalInput")

    def dout(name, shape):
        return nc.dram_tensor(name, list(shape), F32, kind="ExternalOutput")

    def dscr(name, shape):
        return nc.dram_tensor(name, list(shape), F32, kind="Internal")

    x_prompt = din("x_prompt", [SEQ, D])
    x_sample = din("x_sample", [NS1, D])
    cache_k = din("cache_k", [NHYB, NS1, WIN, A_W])
    cache_v = din("cache_v", [NHYB, NS1, WIN, A_W])
    st_sconv = din("st_sconv", [NHYB, NS1, 3, SSM_XBC])
    st_ssm = din("st_ssm", [NHYB, NS1, 1024, 128])
    st_gconv = din("st_gconv", [NG1, NS1, 3, G_QKV])
    st_gdn = din("st_gdn", [NG1, NS1, 16, 128, 128])
    rel_bias = din("rel_bias", [32, 8])
    oh_tab = din("oh_tab", [33, TABL])
    norm_mix_pre = din("norm_mix_pre", [depth, D])
    norm_mix_post = din("norm_mix_post", [depth, D])
    norm_ffn_pre = din("norm_ffn_pre", [depth, D])
    norm_ffn_post = din("norm_ffn_post", [depth, D])
    w_hyb_in = din("w_hyb_in", [NHYB, D, HYB_IN])
    ssm_conv_w = din("ssm_conv_w", [NHYB, 4, SSM_XBC])
    ssm_conv_b = din("ssm_conv_b", [NHYB, SSM_XBC])
    ssm_dt_bias = din("ssm_dt_bias", [NHYB, 16])
    ssm_a_log = din("ssm_a_log", [NHYB, 16])
    ssm_d = din("ssm_d", [NHYB, 16])
    ssm_norm_w = din("ssm_norm_w", [NHYB, SSM_DI])
    w_hyb_out = din("w_hyb_out", [NHYB, HYB_MIX, D])
    w_gdn_in = din("w_gdn_in", [NG1, D, G_IN])
    gdn_conv_w = din("gdn_conv_w", [NG1, 4, G_QKV])
    gdn_dt_bias = din("gdn_dt_bias", [NG1, 16])
    gdn_a_log = din("gdn_a_log", [NG1, 16])
    gdn_norm_w = din("gdn_norm_w", [NG1, 128])
    w_gdn_out = din("w_gdn_out", [NG1, G_VW, D])
    w_ffn_gate = din("w_ffn_gate", [depth, D, D_FF])
    w_ffn_up = din("w_ffn_up", [depth, D, D_FF])
    w_ffn_down = din("w_ffn_down", [depth, D_FF, D])

    y_prompt = dout("y_prompt", [SEQ, D])
    y_sample = dout("y_sample", [NS1, D])
    o_pk = dout("o_pk", [NHYB, KEEP, A_W])
    o_pv = dout("o_pv", [NHYB, KEEP, A_W])
    o_psc = dout("o_psc", [NHYB, 3, SSM_XBC])
    o_pss = dout("o_pss", [NHYB, 1024, 128])
    o_pgc = dout("o_pgc", [NG1, 3, G_QKV])
    o_pgs = dout("o_pgs", [NG1, 16, 128, 128])
    o_sk = dout("o_sk", [NHYB, NS1, WIN, A_W])
    o_sv = dout("o_sv", [NHYB, NS1, WIN, A_W])
    o_ssc = dout("o_ssc", [NHYB, NS1, 3, SSM_XBC])
    o_sss = dout("o_sss", [NHYB, NS1, 1024, 128])
    o_sgc = dout("o_sgc", [NG1, NS1, 3, G_QKV])
    o_sgs = dout("o_sgs", [NG1, NS1, 16, 128, 128])
    dbg_out = {}
    if dbg:
        for nm, shp in dbg.items():
            dbg_out[nm] = dout("dbg_" + nm, shp)

    seqs = [("p", 0, 0, n_ptiles)] + [("s", s, 16, 1) for s in range(n_samp)]
    NSEQ = len(seqs)
    SCR_ROWS = max(SEQ, WIN + 128)
    kt_scr = dscr("kt_scr", [NHYB, NSEQ, 128, 4, SCR_ROWS])
    k_scr = dscr("k_scr", [NHYB, NSEQ, SCR_ROWS, A_W])
    v_scr = dscr("v_scr", [NHYB, NSEQ, SCR_ROWS, A_W])
    ftab = dscr("ftab", [8, TABL])

    es = ExitStack()
    with es:
        S = Sched(nc, es)

        sb_total = [0]

        def sb(name, shape, dt=F32):
            sb_total[0] += int(np.prod(shape[1:])) * 4
            return es.enter_context(nc.sbuf_tensor(name, list(shape), dt))

        psum = [es.enter_context(nc.psum_tensor("ps%d" % i, [128, 512], F32)) for i in range(8)]
        ps_rr = [0]

        def ps():
            i = ps_rr[0]
            ps_rr[0] = (i + 1) % 6
            return psum[i], "ps%d" % i

        acc_rr = [0]

        def ps_acc():
            acc_rr[0] ^= 1
            i = 6 + acc_rr[0]
            return psum[i], "ps%d" % i

        ev_rr = [0]

        def evac_eng():
            ev_rr[0] ^= 1
            return "act" if ev_rr[0] else "dve"

        def copy(e, out, in_, reads, writes):
            if e == "act":
                S.op("act", lambda g: g.activation(out=out, in_=in_, func=AF.Copy), reads, writes)
            else:
                S.op(e, lambda g: g.tensor_copy(out, in_), reads, writes)

        def dump(name, ap, keys):
            if name in dbg_out:
                t = dbg_out[name]
                S.dma("sp", t.ap() if hasattr(t, "ap") else t[:], ap, keys, ["dbg_" + name])

        def mask_const(name, pattern, op, base, cm):
            t = sb(name, [128, 128])
            S.op("pool", lambda g: g.memset(t[:], 1.0), [], [name])
            S.op("pool", lambda g: g.affine_select(out=t[:], in_=t[:], pattern=pattern, compare_op=op, fill=0.0,
                                                   base=base, channel_multiplier=cm), [name], [name])
            return t

        ident = mask_const("ident", [[-1, 128]], ALU.is_equal, 0, 1)
        antiI = mask_const("antiI", [[1, 128]], ALU.is_equal, -127, 1)
        Umat = mask_const("Umat", [[1, 128]], ALU.is_ge, 0, -1)
        SUmat = mask_const("SUmat", [[1, 128]], ALU.is_gt, 0, -1)
        SLmat = mask_const("SLmat", [[-1, 128]], ALU.is_gt, 0, 1)
        ones = sb("ones", [128, 128])
        S.op("pool", lambda g: g.memset(ones[:], 1.0), [], ["ones"])
        ccol = sb("ccol", [128, 4])
        S.op("pool", lambda g: g.memset(ccol[:, 0:1], EPS), [], ["ccol"])
        S.op("pool", lambda g: g.memset(ccol[:, 1:2], 1.0), ["ccol"], ["ccol"])
        S.op("pool", lambda g: g.memset(ccol[:, 2:3], 0.0), ["ccol"], ["ccol"])
        c_eps = ccol[:, 0:1]
        c_one = ccol[:, 1:2]
        CONSTS = ["ident", "antiI", "Umat", "SUmat", "SLmat", "ones", "ccol"]

        xres = sb("xres", [128, D])
        hT = sb("hT", [128, 8, 128], MMDT)
        stat = sb("stat", [128, 8])
        WBUF = 4096
        NWB = 3
        wbuf = [sb("wbuf%d" % i, [128, WBUF], MMDT) for i in range(NWB)]
        wb_rr = [0]
        big = sb("big", [128, 6176])
        mixed = sb("mixed", [128, 2048])
        mixT = sb("mixT", [128, 22, 128], MMDT)
        hTr = mixT[:, 12:20, :]
        valid = sb("valid", [128, 1])
        BIGK = [("big", i) for i in range(13)]

        def bk(c0, c1):
            return [("big", i) for i in range(c0 // 512, (c1 - 1) // 512 + 1)]

        conv_h = [sb("conv_h%d" % i, [128, 12, 3]) for i in range(NHYB)]
        ssm_st = [sb("ssm_st%d" % i, [128, 1024]) for i in range(NHYB)]
        gconv_h = [sb("gconv_h%d" % i, [128, 32, 3]) for i in range(NGDN)]
        gdn_st = [sb("gdn_st%d" % i, [128, 16, 128]) for i in range(NGDN)]

        qT = sb("qT", [128, 4, 128])
        kTt = sb("kTt", [128, 4, 128])
        xbcT = sb("xbcT", [128, 12, 131])
        xbcA = sb("xbcA", [128, 12, 128])
        dtb = sb("dtb", [128, 16])
        alog = sb("alog", [128, 16])
        dsk = sb("dsk", [128, 16])
        dtt = sb("dtt", [128, 16])
        av = sb("av", [128, 16])
        csb = sb("csb", [128, 16])
        dec_b = sb("dec_b", [128, 16])
        dec_w = sb("dec_w", [128, 16])
        beta = sb("beta", [128, 16])
        rn = sb("rn", [128, 16])
        seg = sb("seg", [128, 16, 128])
        cbm = sb("cbm", [128, 2, 128])
        xtok = sb("xtok", [128, 1024])
        xw = sb("xw", [128, 1024])
        btok = sb("btok", [128, 256])
        ysb = sb("ysb", [128, 1024])
        ktw = sb("ktw", [128, NKT * 128])
        vw = sb("vw", [128, NKT, 128])
        Lb = sb("Lb", [128, NKT * 128])
        aU = Lb[:, 0:2048].rearrange("p (h i) -> p h i", i=128)
        Bb = sb("Bb", [128, NKT * 128])
        ETb = sb("ETb", [128, 4, 128])
        mx = sb("mx", [128, 4])
        xn = Lb[:, 0:1024]
        gpost = Bb[:, 0:1024]
        stage = sb("stage", [128, 4, 128])
        SEGK = [("seg", q4) for q4 in range(4)]
        tm3 = big

        rb = big[0:33, 0:8]
        ohs = big[0:33, 512:512 + TABL]
        fsb = big[0:8, 3072:3072 + TABL]
        S.dma("sp", big[0:32, 0:8], rel_bias[:, :], [], [("big", 0)])
        S.op("pool", lambda g: g.memset(big[32:33, 0:8], 1.0), [], [("big", 0)])
        S.dma("sp", ohs, oh_tab[:, :], [], bk(512, 512 + TABL))
        for c0 in range(0, TABL, 512):
            cw = min(512, TABL - c0)
            p_, pk_ = ps()
            S.op("pe", lambda g: g.matmul(p_[0:8, 0:cw], rb, big[0:33, 512 + c0:512 + c0 + cw], start=True, stop=True),
                 BIGK, [pk_])
            copy("dve", big[0:8, 3072 + c0:3072 + c0 + cw], p_[0:8, 0:cw], [pk_], bk(3072 + c0, 3072 + c0 + cw))
        S.dma("sp", ftab[:, :], fsb, BIGK, ["ftab"])

        gv_all = sb("gv_all", [128, depth * 2, 8])
        for l_ in range(depth):
            S.dma("sp", gv_all[:, 2 * l_, :], bass.AP(norm_mix_pre, l_ * D, [[1, 128], [128, 8]]), [], ["gv_all"], slow=True)
            S.dma("sp", gv_all[:, 2 * l_ + 1, :], bass.AP(norm_ffn_pre, l_ * D, [[1, 128], [128, 8]]), [], ["gv_all"], slow=True)
        cwh_all = sb("cwh_all", [128, NHYB, 12, 4])
        cbh_all = sb("cbh_all", [128, NHYB, 12])
        for i_ in range(NHYB):
            for j_ in range(12):
                S.dma("sp", cwh_all[:, i_, j_, :], bass.AP(ssm_conv_w, i_ * 4 * SSM_XBC + j_ * 128, [[1, 128], [SSM_XBC, 4]]), [], ["cwh_all"], slow=True)
            S.dma("sp", cbh_all[:, i_, :], bass.AP(ssm_conv_b, i_ * SSM_XBC, [[1, 128], [128, 12]]), [], ["cbh_all"], slow=True)
        cwg_all = sb("cwg_all", [128, NG1, 32, 4])
        for i_ in range(NGDN):
            for j_ in range(32):
                S.dma("sp", cwg_all[:, i_, j_, :], bass.AP(gdn_conv_w, i_ * 4 * G_QKV + j_ * 128, [[1, 128], [G_QKV, 4]]), [], ["cwg_all"], slow=True)

        def load_bcast(dst, dkey, src_tensor, off, n):
            S.dma("sp", dst, bass.AP(src_tensor, off, [[0, 128], [1, n]]), [], [dkey])

        def rstd_from_ssq(col, n):
            S.op("act", lambda g: g.activation(out=stat[:, col:col + 1], in_=stat[:, col:col + 1], func=AF.Ln, bias=c_eps, scale=1.0 / n),
                 [("stat", col), "ccol"], [("stat", col)])
            S.op("act", lambda g: g.activation(out=stat[:, col:col + 1], in_=stat[:, col:col + 1], func=AF.Exp, scale=-0.5),
                 [("stat", col)], [("stat", col)])

        def rmsnorm_stats(src, skeys, n, col):
            S.op("act", lambda g: g.activation(out=Lb[:, 1024:1024 + n], in_=src, func=AF.Square, accum_out=stat[:, col:col + 1]),
                 skeys, ["Lb", ("stat", col)])
            rstd_from_ssq(col, n)

        def prenorm(which, l, need_rev):
            gvec = gv_all[:, 2 * l + which, :]
            rmsnorm_stats(xres[:, :], ["xres"], D, 0)
            S.op("dve", lambda g: g.tensor_scalar(xn[:, :], xres[:, :], stat[:, 0:1], None, ALU.mult),
                 ["xres", ("stat", 0)], ["Lb"])
            for rev in ([False, True] if need_rev else [False]):
                dst = hTr if rev else hT
                dk = "hT"
                ko = 12 if rev else 0
                dk = "mixT" if rev else "hT"
                for kc in range(8):
                    p_, pk_ = ps()
                    if rev:
                        S.op("pe", lambda g: g.matmul(p_[:, 0:128], xn[:, kc * 128:(kc + 1) * 128], antiI[:], start=True, stop=True),
                             ["Lb", "antiI"], [pk_])
                    else:
                        S.op("pe", lambda g: g.transpose(p_[:, 0:128], xn[:, kc * 128:(kc + 1) * 128], ident[:]), ["Lb", "ident"], [pk_])
                    if kc % 2:
                        S.op("dve", lambda g: g.tensor_scalar(dst[:, kc, :], p_[:, 0:128], gvec[:, kc:kc + 1], None, ALU.mult),
                             [pk_, "gv_all"], [(dk, ko + kc)])
                    else:
                        S.op("act", lambda g: g.activation(out=dst[:, kc, :], in_=p_[:, 0:128], func=AF.Copy, scale=gvec[:, kc:kc + 1]),
                             [pk_, "gv_all"], [(dk, ko + kc)])

        def load_w(W, l, K, N, c0, cw, k0=0, KH=None):
            KC = K // 128 if KH is None else KH
            i = wb_rr[0]
            wb_rr[0] = (i + 1) % NWB
            wv = wbuf[i][:, 0:KC * cw].rearrange("p (k c) -> p k c", c=cw)
            src = bass.AP(W, l * K * N + k0 * 128 * N + c0, [[N, 128], [128 * N, KC], [1, cw]])
            S.dma(wq, wv, src, [], ["wbuf%d" % i])
            return wv, "wbuf%d" % i

        def proj_tok(W, l, K, N, c0, c1, src, skeys, evac):
            KC = K // 128
            halves = 2 if (KC * 256 > WBUF and KC % 2 == 0) else 1
            KH = KC // halves
            cwmax = 512 if KH * 512 <= WBUF else (256 if KH * 256 <= WBUF else 128)
            c = c0
            while c < c1:
                cw = min(cwmax, c1 - c)
                p_, pk_ = ps()
                for hf in range(halves):
                    wv, wk = load_w(W, l, K, N, c, cw, hf * KH, KH)
                    for kc in range(KH):
                        kk = hf * KH + kc
                        S.op("pe", lambda g: g.matmul(p_[:, 0:cw], src[:, kk, :], wv[:, kc, :], start=(kk == 0), stop=(kk == KC - 1)),
                             [wk] + skeys, [pk_])
                evac(p_, pk_, c, cw)
                c += cw

        def proj_feat(W, l, K, N, c0, ncols, src, skeys, evac):
            KC = K // 128
            cwmax = 512 if KC * 512 <= WBUF else (256 if KC * 256 <= WBUF else 128)
            c = c0
            while c < c0 + ncols:
                cw = min(cwmax, c0 + ncols - c)
                wv, wk = load_w(W, l, K, N, c, cw)
                for j in range(cw // 128):
                    p_, pk_ = ps()
                    for kc in range(KC):
                        S.op("pe", lambda g: g.matmul(p_[:, 0:128], wv[:, kc, j * 128:(j + 1) * 128], src[:, kc, :],
                                                      start=(kc == 0), stop=(kc == KC - 1)), [wk] + skeys, [pk_])
                    evac(p_, pk_, (c - c0) // 128 + j)
                c += cw

        hkeys = [("hT", k) for k in range(8)]
        hrkeys = [("mixT", 12 + k) for k in range(8)]

        def post_norm_residual(gain_dram, l, src, skeys):
            load_bcast(gpost[:, :], "Bb", gain_dram, l * D, D)
            rmsnorm_stats(src, skeys, D, 1)
            S.op("dve", lambda g: g.scalar_tensor_tensor(out=xn[:, :], in0=src, scalar=stat[:, 1:2], in1=gpost[:, :],
                                                         op0=ALU.mult, op1=ALU.mult), skeys + [("stat", 1), "Bb"], ["Lb"])
            S.op("dve", lambda g: g.tensor_tensor(out=xres[:, :], in0=xres[:, :], in1=xn[:, :], op=ALU.add),
                 ["xres", "Lb"], ["xres"])

        def transposes_to(dst, dkey, j0, src, skeys, n, dtcast=True):
            for c in range(n):
                p_, pk_ = ps()
                S.op("pe", lambda g: g.transpose(p_[:, 0:128], src[:, c * 128:(c + 1) * 128], ident[:]), skeys + ["ident"], [pk_])
                copy(evac_eng(), dst[:, j0 + c, :], p_[:, 0:128], [pk_], [(dkey, j0 + c)])

        def ffn(l):
            prenorm(1, l, False)

            def ev_gate(p_, pk_, c, cw):
                S.op("act", lambda g: g.activation(out=big[:, c:c + cw], in_=p_[:, 0:cw], func=AF.Silu), [pk_], bk(c, c + cw))
            proj_tok(w_ffn_gate, l, D, D_FF, 0, D_FF, hT, hkeys, ev_gate)

            def ev_up(p_, pk_, c, cw):
                S.op("dve", lambda g: g.tensor_tensor(out=big[:, c:c + cw], in0=big[:, c:c + cw], in1=p_[:, 0:cw], op=ALU.mult),
                     [pk_] + bk(c, c + cw), bk(c, c + cw))
                for b_ in range(c // 128, (c + cw) // 128):
                    p2, pk2 = ps()
                    S.op("pe", lambda g: g.transpose(p2[:, 0:128], big[:, b_ * 128:(b_ + 1) * 128], ident[:]), bk(c, c + cw) + ["ident"], [pk2])
                    copy(evac_eng(), mixT[:, b_, :], p2[:, 0:128], [pk2], [("mixT", b_)])
            proj_tok(w_ffn_up, l, D, D_FF, 0, D_FF, hT, hkeys, ev_up)

            def ev_down(p_, pk_, c, cw):
                copy(evac_eng(), mixed[:, c:c + cw], p_[:, 0:cw], [pk_], [("mixed", c // 128)])
            proj_tok(w_ffn_down, l, D_FF, D, 0, D, mixT, [("mixT", c) for c in range(22)], ev_down)
            post_norm_residual(norm_ffn_post, l, mixed[:, 0:D], [("mixed", c) for c in range(8)])

        def decay_mats(gsrc, gkey):
            p_, pk_ = ps()
            S.op("pe", lambda g: g.matmul(p_[:, 0:16], Umat[:, :], gsrc, start=True, stop=True), ["Umat", gkey], [pk_])
            S.op("act", lambda g: g.activation(out=csb[:, :], in_=p_[:, 0:16], func=AF.Exp), [pk_], ["csb"])
            p_, pk_ = ps()
            S.op("pe", lambda g: g.matmul(p_[:, 0:16], ones[:, :], gsrc, start=True, stop=True), ["ones", gkey], [pk_])
            S.op("act", lambda g: g.activation(out=dec_b[:, :], in_=p_[:, 0:16], func=AF.Exp), [pk_], ["dec_b"])
            S.op("dve", lambda g: g.tensor_tensor(out=aU[:, :, :], in0=Umat[:, :].unsqueeze(1).to_broadcast([128, 16, 128]),
                                                   in1=gsrc.unsqueeze(2).to_broadcast([128, 16, 128]), op=ALU.mult),
                 ["Umat", gkey], ["Lb"])
            for q4 in range(4):
                p_, pk_ = ps()
                S.op("pe", lambda g: g.matmul(p_[:, 0:512], SLmat[:, :], aU[:, q4 * 4:(q4 + 1) * 4, :].rearrange("p h i -> p (h i)"),
                                              start=True, stop=True), ["SLmat", "Lb"], [pk_])
                S.op("act", lambda g: g.activation(out=seg[:, q4 * 4:(q4 + 1) * 4, :].rearrange("p h i -> p (h i)"), in_=p_[:, 0:512], func=AF.Exp),
                     [pk_], [("seg", q4)])
            S.op("dve", lambda g: g.tensor_copy(dec_w[:, :], seg[:, :, 127]), SEGK, ["dec_w"])

        def softplus_inplace(t, key):
            S.op("act", lambda g: g.activation(out=t, in_=t, func=AF.Exp), [key], [key])
            S.op("act", lambda g: g.activation(out=t, in_=t, func=AF.Ln, bias=c_one, scale=1.0), [key, "ccol"], [key])

        def conv_silu(nct, cwv, cwkey, hist, hkey, nvalid, bias_sb, base=0):
            xk = [("xbcT", base + j) for j in range(nct)]
            S.op("dve", lambda g: g.tensor_copy(xbcT[:, base:base + nct, 0:3], hist), [hkey] + xk, xk)
            S.op("dve", lambda g: g.tensor_copy(hist, xbcT[:, base:base + nct, nvalid:nvalid + 3]), xk, [hkey])
            for j in range(nct):
                bj = base + j
                S.op("dve", lambda g: g.tensor_scalar(xbcA[:, bj, :], xbcT[:, bj, 0:128], cwv[:, j, 0:1], None, ALU.mult),
                     [("xbcT", bj), cwkey], [("xbcA", bj)])
                for t in range(1, 4):
                    S.op("dve", lambda g: g.scalar_tensor_tensor(out=xbcA[:, bj, :], in0=xbcT[:, bj, t:t + 128], scalar=cwv[:, j, t:t + 1],
                                                                 in1=xbcA[:, bj, :], op0=ALU.mult, op1=ALU.add),
                         [("xbcT", bj), ("xbcA", bj), cwkey], [("xbcA", bj)])
                if bias_sb is not None:
                    S.op("act", lambda g: g.activation(out=xbcA[:, bj, :], in_=xbcA[:, bj, :], func=AF.Silu, bias=bias_sb[:, j:j + 1], scale=1.0),
                         [("xbcA", bj), "cbh_all"], [("xbcA", bj)])
                else:
                    S.op("act", lambda g: g.activation(out=xbcA[:, bj, :], in_=xbcA[:, bj, :], func=AF.Silu), [("xbcA", bj)], [("xbcA", bj)])

        def hybrid_load_params(i):
            load_bcast(dtb[:, :], "dtb", ssm_dt_bias, i * 16, 16)
            load_bcast(alog[:, :], "alog", ssm_a_log, i * 16, 16)
            S.op("act", lambda g: g.activation(out=alog[:, :], in_=alog[:, :], func=AF.Exp), ["alog"], ["alog"])
            S.op("dve", lambda g: g.tensor_scalar(alog[:, :], alog[:, :], -1.0, None, ALU.mult), ["alog"], ["alog"])
            load_bcast(dsk[:, :], "dsk", ssm_d, i * 16, 16)

        def hybrid_layer(l, i, sq, tpos, nvalid):
            full = (nvalid == 128)
            cst = conv_h[i]
            sst = ssm_st[i]
            ck = "conv_h%d" % i
            sk = "ssm_st%d" % i
            prenorm(0, l, True)

            def ev_q(p_, pk_, j):
                copy(evac_eng(), qT[:, j, :], p_[:, 0:128], [pk_], [("qT", j)])
            proj_feat(w_hyb_in, i, D, HYB_IN, 0, 512, hTr, hrkeys, ev_q)

            def ev_kT(p_, pk_, j):
                copy(evac_eng(), kTt[:, j, :], p_[:, 0:128], [pk_], [("kTt", j)])
            proj_feat(w_hyb_in, i, D, HYB_IN, 512, 512, hT, hkeys, ev_kT)
            scr_base = (i * NSEQ + sq) * 128 * 4 * SCR_ROWS
            S.dma("sp", bass.AP(kt_scr, scr_base + tpos * 128, [[4 * SCR_ROWS, 128], [SCR_ROWS, 4], [1, 128]]),
                  kTt[:, :, :], [("kTt", j) for j in range(4)], [("kt_scr", i, sq)])

            def ev_kvz(p_, pk_, c, cw):
                copy(evac_eng(), big[:, c - 512:c - 512 + cw], p_[:, 0:cw], [pk_], bk(c - 512, c - 512 + cw))
            proj_tok(w_hyb_in, i, D, HYB_IN, 512, 2560, hT, hkeys, ev_kvz)
            if not full:
                S.op("dve", lambda g: g.tensor_scalar(big[:, 0:1024], big[:, 0:1024], valid[:, 0:1], None, ALU.mult),
                     bk(0, 1024) + ["valid"], bk(0, 1024))
            S.dma("sp", k_scr[i, sq, tpos * 128:(tpos + 1) * 128, :], big[:, 0:512], bk(0, 512), [("k_scr", i, sq)])
            S.dma("sp", v_scr[i, sq, tpos * 128:(tpos + 1) * 128, :], big[:, 512:1024], bk(512, 1024), [("v_scr", i, sq)])

            def ev_dt(p_, pk_, c, cw):
                copy("dve", big[:, 2048:2064], p_[:, 0:16], [pk_], bk(2048, 2064))
            proj_tok(w_hyb_in, i, D, HYB_IN, 4096, 4112, hT, hkeys, ev_dt)

            def ev_xbc(p_, pk_, j):
                copy(evac_eng(), xbcT[:, j, 3:131], p_[:, 0:128], [pk_], [("xbcT", j)])
            proj_feat(w_hyb_in, i, D, HYB_IN, 2560, 1536, hT, hkeys, ev_xbc)

            t_lo = max(0, tpos - 16)
            nk = tpos - t_lo + 1
            W_ = nk * 128
            off_c = (NKT - nk) * 128
            for pr in range(4):
                S.dma("sp", ktw[:, 0:W_], bass.AP(kt_scr, scr_base + pr * SCR_ROWS + t_lo * 128, [[4 * SCR_ROWS, 128], [1, W_]]),
                      [("kt_scr", i, sq)], ["ktw"])
                S.dma("sp", vw[:, 0:nk, :], bass.AP(v_scr, ((i * NSEQ + sq) * SCR_ROWS + t_lo * 128) * A_W + pr * 128,
                                                    [[A_W, 128], [128 * A_W, nk], [1, 128]]), [("v_scr", i, sq)], ["vw"])
                for hh in range(2):
                    h = pr * 2 + hh
                    pb = hh * 64
                    S.dma("sp", Bb[:, 0:W_], bass.AP(ftab, h * TABL + off_c, [[1, 128], [1, W_]]), ["ftab"], ["Bb"])
                    for c0 in range(0, W_, 512):
                        cw = min(512, W_ - c0)
                        p_, pk_ = ps()
                        S.op("pe", lambda g: g.matmul(p_[:, 0:cw], qT[pb:pb + 64, pr, :], ktw[pb:pb + 64, c0:c0 + cw], start=True, stop=True),
                             [("qT", pr), "ktw"], [pk_])
                        S.op("dve", lambda g: g.scalar_tensor_tensor(out=Lb[:, c0:c0 + cw], in0=p_[:, 0:cw], scalar=0.125, in1=Bb[:, c0:c0 + cw],
                                                                     op0=ALU.mult, op1=ALU.add), [pk_, "Bb"], ["Lb"])
                    S.op("dve", lambda g: g.reduce_max(out=mx[:, 0:1], in_=Lb[:, 0:W_], axis=AX.X), ["Lb"], ["mx0"])
                    S.op("dve", lambda g: g.tensor_scalar(mx[:, 1:2], mx[:, 0:1], -1.0, None, ALU.mult), ["mx0"], ["mx1"])
                    S.op("act", lambda g: g.activation(out=Lb[:, 0:W_], in_=Lb[:, 0:W_], func=AF.Exp, bias=mx[:, 1:2], scale=1.0,
                                                       accum_out=mx[:, 2:3]), ["Lb", "mx1"], ["Lb", "mx2"])
                    S.op("dve", lambda g: g.reciprocal(mx[:, 3:4], mx[:, 2:3]), ["mx2"], ["mx3"])
                    po, pok = ps_acc()
                    for kt in range(nk):
                        p_, pk_ = ps()
                        S.op("pe", lambda g: g.transpose(p_[:, 0:128], Lb[:, kt * 128:(kt + 1) * 128], ident[:]), ["Lb", "ident"], [pk_])
                        copy(evac_eng(), ETb[:, kt % 4, :], p_[:, 0:128], [pk_], [("ETb", kt % 4)])
                        S.op("pe", lambda g: g.matmul(po[:, 0:64], ETb[:, kt % 4, :], vw[:, kt, pb:pb + 64], start=(kt == 0), stop=(kt == nk - 1)),
                             [("ETb", kt % 4), "vw"], [pok])
                    S.op("dve", lambda g: g.tensor_scalar(mixed[:, h * 64:(h + 1) * 64], po[:, 0:64], mx[:, 3:4], None, ALU.mult),
                         [pok, "mx3"], [("mixed", h // 2)])
            for c in range(4):
                p_, pk_ = ps()
                S.op("pe", lambda g: g.matmul(p_[:, 0:128], mixed[:, c * 128:(c + 1) * 128], antiI[:], start=True, stop=True),
                     [("mixed", c), "antiI"], [pk_])
                copy(evac_eng(), mixT[:, c, :], p_[:, 0:128], [pk_], [("mixT", c)])

            conv_silu(12, cwh_all[:, i, :, :], "cwh_all", cst[:, :, :], ck, nvalid, cbh_all[:, i, :])
            for c in range(8):
                p_, pk_ = ps()
                S.op("pe", lambda g: g.transpose(p_[:, 0:128], xbcA[:, c, :], ident[:]), [("xbcA", c), "ident"], [pk_])
                copy(evac_eng(), xtok[:, c * 128:(c + 1) * 128], p_[:, 0:128], [pk_], [("xtok", c)])
            for c in range(2):
                p_, pk_ = ps()
                S.op("pe", lambda g: g.transpose(p_[:, 0:128], xbcA[:, 8 + c, :], ident[:]), [("xbcA", 8 + c), "ident"], [pk_])
                copy(evac_eng(), btok[:, c * 128:(c + 1) * 128], p_[:, 0:128], [pk_], [("btok", c)])
            xtk = [("xtok", c) for c in range(8)]
            S.op("dve", lambda g: g.tensor_tensor(out=dtt[:, :], in0=big[:, 2048:2064], in1=dtb[:, :], op=ALU.add), bk(2048, 2064) + ["dtb"], ["dtt"])
            softplus_inplace(dtt[:, :], "dtt")
            if not full:
                S.op("dve", lambda g: g.tensor_scalar(dtt[:, :], dtt[:, :], valid[:, 0:1], None, ALU.mult), ["dtt", "valid"], ["dtt"])
                S.op("dve", lambda g: g.tensor_scalar(xtok[:, :], xtok[:, :], valid[:, 0:1], None, ALU.mult), xtk + ["valid"], xtk)
            S.op("dve", lambda g: g.tensor_tensor(out=av[:, :], in0=dtt[:, :], in1=alog[:, :], op=ALU.mult), ["dtt", "alog"], ["av"])
            decay_mats(av[:, :], "av")
            S.op("dve", lambda g: g.tensor_tensor(out=dec_w[:, :], in0=dec_w[:, :], in1=dtt[:, :], op=ALU.mult), ["dec_w", "dtt"], ["dec_w"])
            S.op("dve", lambda g: g.tensor_tensor(out=xw[:, :].rearrange("p (h d) -> p h d", d=64), in0=xtok[:, :].rearrange("p (h d) -> p h d", d=64),
                                                   in1=dec_w[:, :].unsqueeze(2).to_broadcast([128, 16, 64]), op=ALU.mult), xtk + ["dec_w"], ["xw"])
            for g_ in range(2):
                p_, pk_ = ps()
                S.op("pe", lambda g: g.matmul(p_[:, 0:128], xbcA[:, 8 + g_, :], xbcA[:, 10 + g_, :], start=True, stop=True),
                     [("xbcA", 8 + g_), ("xbcA", 10 + g_)], [pk_])
                S.op("dve", lambda g: g.tensor_tensor(out=cbm[:, g_, :], in0=p_[:, 0:128], in1=Umat[:, :], op=ALU.mult), [pk_, "Umat"], [("cbm", g_)])
            S.op("dve", lambda g: g.tensor_tensor(out=seg[:, :, :], in0=seg[:, :, :], in1=dtt[:, :].unsqueeze(2).to_broadcast([128, 16, 128]), op=ALU.mult),
                 SEGK + ["dtt", "dec_w"], SEGK)
            for g_ in range(2):
                S.op("dve", lambda g: g.tensor_tensor(out=seg[:, g_ * 8:(g_ + 1) * 8, :], in0=seg[:, g_ * 8:(g_ + 1) * 8, :],
                                                       in1=cbm[:, g_, :].unsqueeze(1).to_broadcast([128, 8, 128]), op=ALU.mult),
                     SEGK + [("cbm", g_)], SEGK)
            for g_ in range(2):
                p_, pk_ = ps()
                S.op("pe", lambda g: g.matmul(p_[:, 0:512], xbcA[:, 10 + g_, :], sst[:, g_ * 512:(g_ + 1) * 512], start=True, stop=True),
                     [("xbcA", 10 + g_), sk], [pk_])
                S.op("dve", lambda g: g.tensor_tensor(out=ysb[:, g_ * 512:(g_ + 1) * 512].rearrange("p (h d) -> p h d", d=64),
                                                      in0=p_[:, 0:512].rearrange("p (h d) -> p h d", d=64),
                                                      in1=csb[:, g_ * 8:(g_ + 1) * 8].unsqueeze(2).to_broadcast([128, 8, 64]), op=ALU.mult),
                     [pk_, "csb"], [("ysb", g_)])
            for g_ in range(2):
                p_, pk_ = ps()
                for hh in range(8):
                    h = g_ * 8 + hh
                    S.op("pe", lambda g: g.matmul(p_[:, hh * 64:(hh + 1) * 64], seg[:, h, :], xtok[:, h * 64:(h + 1) * 64], start=True, stop=True),
                         SEGK + xtk, [pk_])
                S.op("dve", lambda g: g.tensor_tensor(out=ysb[:, g_ * 512:(g_ + 1) * 512], in0=ysb[:, g_ * 512:(g_ + 1) * 512], in1=p_[:, 0:512], op=ALU.add),
                     [pk_, ("ysb", g_)], [("ysb", g_)])
            for g_ in range(2):
                p_, pk_ = ps()
                S.op("pe", lambda g: g.matmul(p_[:, 0:512], btok[:, g_ * 128:(g_ + 1) * 128], xw[:, g_ * 512:(g_ + 1) * 512], start=True, stop=True),
                     [("btok", g_), "xw"], [pk_])
                S.op("dve", lambda g: g.tensor_tensor(out=sst[:, g_ * 512:(g_ + 1) * 512].rearrange("p (h d) -> p h d", d=64),
                                                       in0=sst[:, g_ * 512:(g_ + 1) * 512].rearrange("p (h d) -> p h d", d=64),
                                                       in1=dec_b[:, g_ * 8:(g_ + 1) * 8].unsqueeze(2).to_broadcast([128, 8, 64]), op=ALU.mult),
                     [sk, "dec_b"], [sk])
                S.op("dve", lambda g: g.tensor_tensor(out=sst[:, g_ * 512:(g_ + 1) * 512], in0=sst[:, g_ * 512:(g_ + 1) * 512], in1=p_[:, 0:512], op=ALU.add),
                     [pk_, sk], [sk])
            yk = [("ysb", 0), ("ysb", 1)]
            S.op("dve", lambda g: g.tensor_tensor(out=xw[:, :].rearrange("p (h d) -> p h d", d=64), in0=xtok[:, :].rearrange("p (h d) -> p h d", d=64),
                                                   in1=dsk[:, :].unsqueeze(2).to_broadcast([128, 16, 64]), op=ALU.mult), xtk + ["dsk", "xw"], ["xw"])
            S.op("dve", lambda g: g.tensor_tensor(out=ysb[:, :], in0=ysb[:, :], in1=xw[:, :], op=ALU.add), yk + ["xw"], yk)
            S.op("act", lambda g: g.activation(out=big[:, 1024:2048], in_=big[:, 1024:2048], func=AF.Silu), bk(1024, 2048), bk(1024, 2048))
            S.op("dve", lambda g: g.tensor_tensor(out=ysb[:, :], in0=ysb[:, :], in1=big[:, 1024:2048], op=ALU.mult), yk + bk(1024, 2048), yk)
            load_bcast(gpost[:, :], "Bb", ssm_norm_w, i * SSM_DI, SSM_DI)
            snw = gpost
            for g_ in range(2):
                rmsnorm_stats(ysb[:, g_ * 512:(g_ + 1) * 512], [("ysb", g_)], 512, 2 + g_)
                S.op("dve", lambda g: g.scalar_tensor_tensor(out=mixed[:, 512 + g_ * 512:1024 + g_ * 512], in0=ysb[:, g_ * 512:(g_ + 1) * 512],
                                                             scalar=stat[:, 2 + g_:3 + g_], in1=snw[:, g_ * 512:(g_ + 1) * 512], op0=ALU.mult, op1=ALU.mult),
                     [("ysb", g_), ("stat", 2 + g_), "Bb"], [("mixed", 4 + 4 * g_ + c) for c in range(4)])
            transposes_to(mixT, "mixT", 4, mixed[:, 512:1536], [("mixed", 4 + c) for c in range(8)], 8)

            def ev_out(p_, pk_, c, cw):
                copy(evac_eng(), big[:, 3072 + c:3072 + c + cw], p_[:, 0:cw], [pk_], bk(3072 + c, 3072 + c + cw))
            proj_tok(w_hyb_out, i, HYB_MIX, D, 0, D, mixT, [("mixT", c) for c in range(12)], ev_out)
            post_norm_residual(norm_mix_post, l, big[:, 3072:4096], bk(3072, 4096))

        gdtb = dtb
        galog = alog
        gnw = sb("gnw", [128, 128])
        knT = xtok[:, :].rearrange("p (h d) -> p h d", d=128)
        QKT = xw[:, :].rearrange("p (h d) -> p h d", d=128)
        def mk_set(parts):
            return parts
        gs = []
        gs_alias = []
        vwf = vw[:, :, :].rearrange("p a b -> p (a b)")
        for base, al in [(ktw, ["ktw"]), (vwf, ["vw"]), (Bb, ["Bb"])]:
            dct = {}
            for n_, nm in enumerate(["s0", "s1", "s2", "s3", "s4", "s5"]):
                dct[nm] = base[:, n_ * 128:(n_ + 1) * 128]
            dct["XA"] = base[:, 768:1024]
            dct["XB"] = base[:, 1024:1280]
            gs.append(dct)
            gs_alias.append(al)
        dct = {}
        for n_, nm in enumerate(["s0", "s1", "s2", "s3", "s4", "s5"]):
            dct[nm] = ktw[:, 1280 + n_ * 128:1280 + (n_ + 1) * 128]
        dct["XA"] = vwf[:, 1280:1536]
        dct["XB"] = vwf[:, 1536:1792]
        gs.append(dct)
        gs_alias.append(["ktw", "vw"])

        def gdn_load_params(i):
            load_bcast(gdtb[:, :], "dtb", gdn_dt_bias, i * 16, 16)
            load_bcast(galog[:, :], "alog", gdn_a_log, i * 16, 16)
            S.op("act", lambda g: g.activation(out=galog[:, :], in_=galog[:, :], func=AF.Exp), ["alog"], ["alog"])
            S.op("dve", lambda g: g.tensor_scalar(galog[:, :], galog[:, :], -1.0, None, ALU.mult), ["alog"], ["alog"])
            load_bcast(gnw[:, :], "gnw", gdn_norm_w, i * 128, 128)

        def gdn_layer(l, i, sq, tpos, nvalid):
            full = (nvalid == 128)
            cst = gconv_h[i]
            ck = "gconv_h%d" % i
            Sst = gdn_st[i]
            prenorm(0, l, False)
            S.op("pool", lambda g: g.memset(ktw[0:1, 0:1], 0.0), [], ["ktw"])
            S.op("pool", lambda g: g.memset(vw[0:1, 0:1, 0:1], 0.0), [], ["vw"])
            S.op("pool", lambda g: g.memset(Bb[0:1, 0:1], 0.0), [], ["Bb"])

            def ev_zba(p_, pk_, c, cw):
                copy(evac_eng(), big[:, c:c + cw], p_[:, 0:cw], [pk_], bk(c, c + cw))
            proj_tok(w_gdn_in, i, D, G_IN, 4096, G_IN, hT, hkeys, ev_zba)
            def proj_k(k):
                slot = (k % 3) * 4

                def ev_x(p_, pk_, j):
                    copy(evac_eng(), xbcT[:, slot + j, 3:131], p_[:, 0:128], [pk_], [("xbcT", slot + j)])
                proj_feat(w_gdn_in, i, D, G_IN, k * 512, 512, hT, hkeys, ev_x)

            def conv_k(k):
                slot = (k % 3) * 4
                conv_silu(4, cwg_all[:, i, k * 4:(k + 1) * 4, :], "cwg_all", cst[:, k * 4:(k + 1) * 4, :], ck, nvalid, None, base=slot)

            def tr_k(k):
                slot = (k % 3) * 4
                for j in range(4):
                    ct = k * 4 + j
                    p_, pk_ = ps()
                    S.op("pe", lambda g: g.transpose(p_[:, 0:128], xbcA[:, slot + j, :], ident[:]), [("xbcA", slot + j), "ident"], [pk_])
                    copy(evac_eng(), big[:, ct * 128:(ct + 1) * 128], p_[:, 0:128], [pk_], bk(ct * 128, (ct + 1) * 128))
            proj_k(0)
            for k in range(8):
                conv_k(k)
                if k + 1 < 8:
                    proj_k(k + 1)
                tr_k(k)
            S.op("dve", lambda g: g.tensor_tensor(out=Lb[:, 0:2048], in0=big[:, 0:2048], in1=big[:, 0:2048], op=ALU.mult), bk(0, 2048), ["Lb"])
            S.op("dve", lambda g: g.reduce_sum(out=rn[:, :], in_=Lb[:, 0:2048].rearrange("p (h d) -> p h d", d=128), axis=AX.X), ["Lb"], ["rn"])
            S.op("act", lambda g: g.activation(out=rn[:, :], in_=rn[:, :], func=AF.Ln, bias=c_eps, scale=1.0), ["rn", "ccol"], ["rn"])
            S.op("act", lambda g: g.activation(out=rn[:, :], in_=rn[:, :], func=AF.Exp, scale=-0.5), ["rn"], ["rn"])
            S.op("dve", lambda g: g.tensor_scalar(rn[:, 0:8], rn[:, 0:8], 128.0 ** -0.5, None, ALU.mult), ["rn"], ["rn"])
            if not full:
                S.op("dve", lambda g: g.tensor_scalar(rn[:, :], rn[:, :], valid[:, 0:1], None, ALU.mult), ["rn", "valid"], ["rn"])
                S.op("dve", lambda g: g.tensor_scalar(big[:, 2048:4096], big[:, 2048:4096], valid[:, 0:1], None, ALU.mult),
                     bk(2048, 4096) + ["valid"], bk(2048, 4096))
            S.op("dve", lambda g: g.tensor_tensor(out=big[:, 0:2048].rearrange("p (h d) -> p h d", d=128), in0=big[:, 0:2048].rearrange("p (h d) -> p h d", d=128),
                                                  in1=rn[:, :].unsqueeze(2).to_broadcast([128, 16, 128]), op=ALU.mult), bk(0, 2048) + ["rn"], bk(0, 2048))
            S.op("act", lambda g: g.activation(out=beta[:, :], in_=big[:, 6144:6160], func=AF.Exp, scale=-1.0), bk(6144, 6160), ["beta"])
            S.op("dve", lambda g: g.tensor_scalar(beta[:, :], beta[:, :], 1.0, None, ALU.add), ["beta"], ["beta"])
            S.op("dve", lambda g: g.reciprocal(beta[:, :], beta[:, :]), ["beta"], ["beta"])
            S.op("dve", lambda g: g.tensor_tensor(out=dtt[:, :], in0=big[:, 6160:6176], in1=gdtb[:, :], op=ALU.add), bk(6160, 6176) + ["dtb"], ["dtt"])
            softplus_inplace(dtt[:, :], "dtt")
            S.op("dve", lambda g: g.tensor_tensor(out=av[:, :], in0=dtt[:, :], in1=galog[:, :], op=ALU.mult), ["dtt", "alog"], ["av"])
            if not full:
                S.op("dve", lambda g: g.tensor_scalar(beta[:, :], beta[:, :], valid[:, 0:1], None, ALU.mult), ["beta", "valid"], ["beta"])
                S.op("dve", lambda g: g.tensor_scalar(av[:, :], av[:, :], valid[:, 0:1], None, ALU.mult), ["av", "valid"], ["av"])
            decay_mats(av[:, :], "av")
            S.op("dve", lambda g: g.tensor_tensor(out=aU[:, :, :], in0=seg[:, :, :], in1=SUmat[:, :].unsqueeze(1).to_broadcast([128, 16, 128]), op=ALU.mult),
                 SEGK + ["SUmat", "Lb"], ["Lb"])
            S.op("dve", lambda g: g.tensor_tensor(out=seg[:, :, :], in0=seg[:, :, :], in1=Umat[:, :].unsqueeze(1).to_broadcast([128, 16, 128]), op=ALU.mult),
                 SEGK + ["Umat", "dec_w", "Lb"], SEGK)
            for hq in range(8):
                p_, pk_ = ps()
                S.op("pe", lambda g: g.transpose(p_[:, 0:128], big[:, 1024 + hq * 128:1152 + hq * 128], ident[:]), bk(1024, 2048) + ["ident"], [pk_])
                copy(evac_eng(), knT[:, hq, :], p_[:, 0:128], [pk_], [("xtok", hq)])
                p_, pk_ = ps()
                S.op("pe", lambda g: g.transpose(p_[:, 0:128], big[:, hq * 128:(hq + 1) * 128], ident[:]), bk(0, 1024) + ["ident"], [pk_])
                copy(evac_eng(), stage[:, hq % 4, :], p_[:, 0:128], [pk_], [("stage", hq % 4)])
                p_, pk_ = ps()
                S.op("pe", lambda g: g.matmul(p_[:, 0:128], knT[:, hq, :], stage[:, hq % 4, :], start=True, stop=True), [("xtok", hq), ("stage", hq % 4)], [pk_])
                copy(evac_eng(), QKT[:, hq, :], p_[:, 0:128], [pk_], ["xw"])
            def head_gen(h, par):
                hq = h // 2
                G = gs[par]
                ALS = gs_alias[par]

                def K(nm):
                    return "g%s%d" % (nm, par)

                def SO(e, fn, reads, writes):
                    S.op(e, fn, list(reads) + ALS, writes)

                def CP(e, out, in_, reads, writes):
                    copy(e, out, in_, list(reads) + ALS, writes)
                kcols = bk(1024 + hq * 128, 1152 + hq * 128)
                qcols = bk(hq * 128, (hq + 1) * 128)
                vcols = bk(2048 + h * 128, 2176 + h * 128)
                k_n = big[:, 1024 + hq * 128:1152 + hq * 128]
                q_n = big[:, hq * 128:(hq + 1) * 128]
                v_h = big[:, 2048 + h * 128:2176 + h * 128]
                SO("dve", lambda g: g.tensor_scalar(G["s0"], k_n, beta[:, h:h + 1], None, ALU.mult), kcols + ["beta"], [K("s0")])
                SO("dve", lambda g: g.tensor_scalar(G["XA"][:, 0:128], v_h, beta[:, h:h + 1], None, ALU.mult), vcols + ["beta"], [K("XA")])
                SO("dve", lambda g: g.tensor_scalar(G["XA"][:, 128:256], G["s0"], csb[:, h:h + 1], None, ALU.mult), [K("s0"), "csb", K("XA")], [K("XA")])
                p_, pk_ = ps()
                SO("pe", lambda g: g.transpose(p_[:, 0:128], G["s0"], ident[:]), [K("s0"), "ident"], [pk_])
                CP(evac_eng(), G["s1"], p_[:, 0:128], [pk_], [K("s1")])
                yield
                p_, pk_ = ps()
                SO("pe", lambda g: g.matmul(p_[:, 0:128], knT[:, hq, :], G["s1"], start=True, stop=True), [("xtok", hq), K("s1")], [pk_])
                SO("dve", lambda g: g.tensor_tensor(out=G["s2"], in0=p_[:, 0:128], in1=aU[:, h, :], op=ALU.mult), [pk_, "Lb"], [K("s2")])
                yield
                p_, pk_ = ps()
                SO("pe", lambda g: g.transpose(p_[:, 0:128], G["s2"], ident[:]), [K("s2"), "ident"], [pk_])
                CP(evac_eng(), G["s3"], p_[:, 0:128], [pk_], [K("s3")])
                p_, pk_ = ps()
                SO("pe", lambda g: g.matmul(p_[:, 0:256], G["s2"], G["XA"], start=True, stop=True), [K("s2"), K("XA")], [pk_])
                SO("dve", lambda g: g.tensor_tensor(out=G["XB"], in0=G["XA"], in1=p_[:, 0:256], op=ALU.subtract), [pk_, K("XA")], [K("XB")])
                yield
                Xc, Xn = "XB", "XA"
                Mc, Mn, MTc, MTn = "s3", "s5", "s2", "s4"
                for lvl in range(1, 7):
                    p_, pk_ = ps()
                    SO("pe", lambda g: g.matmul(p_[:, 0:128], G[Mc], G[MTc], start=True, stop=True), [K(Mc), K(MTc)], [pk_])
                    CP(evac_eng(), G[MTn], p_[:, 0:128], [pk_], [K(MTn)])
                    if lvl < 6:
                        p_, pk_ = ps()
                        SO("pe", lambda g: g.matmul(p_[:, 0:128], G[MTc], G[Mc], start=True, stop=True), [K(Mc), K(MTc)], [pk_])
                        CP(evac_eng(), G[Mn], p_[:, 0:128], [pk_], [K(Mn)])
                    yield
                    p_, pk_ = ps()
                    SO("pe", lambda g: g.matmul(p_[:, 0:256], G[MTn], G[Xc], start=True, stop=True), [K(MTn), K(Xc)], [pk_])
                    SO("dve", lambda g: g.tensor_tensor(out=G[Xn], in0=G[Xc], in1=p_[:, 0:256], op=ALU.add), [pk_, K(Xc)], [K(Xn)])
                    yield
                    Xc, Xn = Xn, Xc
                    Mc, Mn = Mn, Mc
                    MTc, MTn = MTn, MTc
                X = G[Xc]
                p_, pk_ = ps()
                SO("pe", lambda g: g.transpose(p_[:, 0:128], X[:, 128:256], ident[:]), [K(Xc), "ident"], [pk_])
                CP(evac_eng(), G["s0"], p_[:, 0:128], [pk_], [K("s0")])
                SO("dve", lambda g: g.tensor_scalar(G["s3"], q_n, csb[:, h:h + 1], None, ALU.mult), qcols + ["csb"], [K("s3")])
                yield
                stk = ("gdn_st", i, h)
                p_, pk_ = ps()
                SO("pe", lambda g: g.matmul(p_[:, 0:128], G["s0"], Sst[:, h, :], start=True, stop=True), [K("s0"), stk], [pk_])
                SO("dve", lambda g: g.tensor_tensor(out=G["s1"], in0=X[:, 0:128], in1=p_[:, 0:128], op=ALU.subtract), [pk_, K(Xc)], [K("s1")])
                p_, pk_ = ps()
                SO("pe", lambda g: g.transpose(p_[:, 0:128], G["s3"], ident[:]), [K("s3"), "ident"], [pk_])
                CP(evac_eng(), G["s5"], p_[:, 0:128], [pk_], [K("s5")])
                SO("dve", lambda g: g.tensor_tensor(out=G["s2"], in0=QKT[:, hq, :], in1=seg[:, h, :], op=ALU.mult),
                   ["xw"] + SEGK, [K("s2")])
                SO("dve", lambda g: g.tensor_scalar(G["s4"], k_n, dec_w[:, h:h + 1], None, ALU.mult), kcols + ["dec_w"], [K("s4")])
                yield
                p_, pk_ = ps()
                SO("pe", lambda g: g.matmul(p_[:, 0:128], G["s5"], Sst[:, h, :], start=True, stop=False), [K("s5"), stk], [pk_])
                SO("pe", lambda g: g.matmul(p_[:, 0:128], G["s2"], G["s1"], start=False, stop=True), [K("s2"), K("s1")], [pk_])
                CP(evac_eng(), mixed[:, h * 128:(h + 1) * 128], p_[:, 0:128], [pk_], [("mixed", h)])
                p_, pk_ = ps()
                SO("pe", lambda g: g.matmul(p_[:, 0:128], G["s4"], G["s1"], start=True, stop=True), [K("s4"), K("s1")], [pk_])
                SO("dve", lambda g: g.scalar_tensor_tensor(out=Sst[:, h, :], in0=Sst[:, h, :], scalar=dec_b[:, h:h + 1], in1=p_[:, 0:128],
                                                           op0=ALU.mult, op1=ALU.add), [pk_, stk, "dec_b"], [stk])
                yield

            NGRP = len(gs)
            for h0 in range(0, 16, NGRP):
                gens = [head_gen(h0 + j, j) for j in range(min(NGRP, 16 - h0))]
                alive = list(gens)
                while alive:
                    nxt = []
                    for g_ in alive:
                        try:
                            next(g_)
                            nxt.append(g_)
                        except StopIteration:
                            pass
                    alive = nxt
            mk = [("mixed", h) for h in range(16)]
            S.op("dve", lambda g: g.tensor_tensor(out=Lb[:, 0:2048], in0=mixed[:, :], in1=mixed[:, :], op=ALU.mult), mk, ["Lb"])
            S.op("dve", lambda g: g.reduce_sum(out=rn[:, :], in_=Lb[:, 0:2048].rearrange("p (h d) -> p h d", d=128), axis=AX.X), ["Lb"], ["rn"])
            S.op("act", lambda g: g.activation(out=rn[:, :], in_=rn[:, :], func=AF.Ln, bias=c_eps, scale=1.0 / 128), ["rn", "ccol"], ["rn"])
            S.op("act", lambda g: g.activation(out=rn[:, :], in_=rn[:, :], func=AF.Exp, scale=-0.5), ["rn"], ["rn"])
            S.op("dve", lambda g: g.tensor_tensor(out=mixed[:, :].rearrange("p (h d) -> p h d", d=128), in0=mixed[:, :].rearrange("p (h d) -> p h d", d=128),
                                                  in1=rn[:, :].unsqueeze(2).to_broadcast([128, 16, 128]), op=ALU.mult), mk + ["rn"], mk)
            S.op("dve", lambda g: g.tensor_tensor(out=mixed[:, :].rearrange("p (h d) -> p h d", d=128), in0=mixed[:, :].rearrange("p (h d) -> p h d", d=128),
                                                   in1=gnw[:, :].unsqueeze(1).to_broadcast([128, 16, 128]), op=ALU.mult), mk + ["gnw"], mk)
            S.op("act", lambda g: g.activation(out=big[:, 4096:6144], in_=big[:, 4096:6144], func=AF.Silu), bk(4096, 6144), bk(4096, 6144))
            S.op("dve", lambda g: g.tensor_tensor(out=mixed[:, :], in0=mixed[:, :], in1=big[:, 4096:6144], op=ALU.mult), mk + bk(4096, 6144), mk)
            transposes_to(mixT, "mixT", 0, mixed, mk, 16)

            def ev_out(p_, pk_, c, cw):
                copy(evac_eng(), big[:, c:c + cw], p_[:, 0:cw], [pk_], bk(c, c + cw))
            proj_tok(w_gdn_out, i, G_VW, D, 0, D, mixT, [("mixT", c) for c in range(16)], ev_out)
            post_norm_residual(norm_mix_post, l, big[:, 0:1024], bk(0, 1024))

        def conv_state_load(dst, dkey, nct, src_tensor, off, width):
            S.dma("sp", tm3[0:3, 0:width], bass.AP(src_tensor, off, [[width, 3], [1, width]]), [], BIGK)
            for j in range(nct):
                p_, pk_ = ps()
                S.op("pe", lambda g: g.transpose(p_[:, 0:3], tm3[0:3, j * 128:(j + 1) * 128], ident[0:3, 0:3]), BIGK + ["ident"], [pk_])
                copy("dve", dst[:, j, :], p_[:, 0:3], [pk_], [dkey])

        def conv_state_store(src, skey, nct, dst_tensor, off, width):
            for j in range(nct):
                p_, pk_ = ps()
                S.op("pe", lambda g: g.transpose(p_[0:3, 0:128], src[:, j, :], ident[:]), [skey, "ident"], [pk_])
                copy("dve", tm3[0:3, j * 128:(j + 1) * 128], p_[0:3, 0:128], [pk_], BIGK)
            S.dma("sp", bass.AP(dst_tensor, off, [[width, 3], [1, width]]), tm3[0:3, 0:width], BIGK, [("out", dst_tensor.name)])

        def ssm_state_load(i, src_tensor, off):
            S.dma("sp", Lb[:, 0:1024].rearrange("p (b n) -> p b n", n=128), bass.AP(src_tensor, off, [[128, 128], [128 * 128, 8], [1, 128]]), [], ["Lb"])
            for b in range(8):
                p_, pk_ = ps()
                S.op("pe", lambda g: g.transpose(p_[:, 0:128], Lb[:, b * 128:(b + 1) * 128], ident[:]), ["Lb", "ident"], [pk_])
                copy(evac_eng(), ssm_st[i][:, b * 128:(b + 1) * 128], p_[:, 0:128], [pk_], ["ssm_st%d" % i])

        def ssm_state_store(i, dst_tensor, off):
            for b in range(8):
                p_, pk_ = ps()
                S.op("pe", lambda g: g.transpose(p_[:, 0:128], ssm_st[i][:, b * 128:(b + 1) * 128], ident[:]), ["ssm_st%d" % i, "ident"], [pk_])
                copy(evac_eng(), Lb[:, b * 128:(b + 1) * 128], p_[:, 0:128], [pk_], ["Lb"])
            S.dma("sp", bass.AP(dst_tensor, off, [[128, 128], [128 * 128, 8], [1, 128]]), Lb[:, 0:1024].rearrange("p (b n) -> p b n", n=128),
                  ["Lb"], [("out", dst_tensor.name)])

        def flat_copy(dst_tensor, doff, src_tensor, soff, nelem, rkeys, wkeys):
            assert nelem % 128 == 0
            per = nelem // 128
            S.dma("sp", bass.AP(dst_tensor, doff, [[per, 128], [1, per]]), bass.AP(src_tensor, soff, [[per, 128], [1, per]]), rkeys, wkeys)

        def seq_begin(sq, kind, sidx):
            for i in range(NHYB):
                ck = "conv_h%d" % i
                if kind == "p":
                    S.op("pool", lambda g: g.memset(conv_h[i][:, :, :], 0.0), [], [ck])
                    S.op("pool", lambda g: g.memset(ssm_st[i][:, :], 0.0), [], ["ssm_st%d" % i])
                else:
                    conv_state_load(conv_h[i], ck, 12, st_sconv, (i * NS1 + sidx) * 3 * SSM_XBC, SSM_XBC)
                    ssm_state_load(i, st_ssm, (i * NS1 + sidx) * 1024 * 128)
                    flat_copy(k_scr, (i * NSEQ + sq) * SCR_ROWS * A_W, cache_k, (i * NS1 + sidx) * WIN * A_W, WIN * A_W, [], [("k_scr", i, sq)])
                    flat_copy(v_scr, (i * NSEQ + sq) * SCR_ROWS * A_W, cache_v, (i * NS1 + sidx) * WIN * A_W, WIN * A_W, [], [("v_scr", i, sq)])
                    scr_base = (i * NSEQ + sq) * 128 * 4 * SCR_ROWS
                    for t in range(16):
                        S.dma("sp", Lb[:, 0:512], bass.AP(cache_k, ((i * NS1 + sidx) * WIN + t * 128) * A_W, [[A_W, 128], [1, A_W]]), [], ["Lb"])
                        for pr in range(4):
                            p_, pk_ = ps()
                            S.op("pe", lambda g: g.transpose(p_[:, 0:128], Lb[:, pr * 128:(pr + 1) * 128], ident[:]), ["Lb", "ident"], [pk_])
                            copy(evac_eng(), stage[:, pr, :], p_[:, 0:128], [pk_], [("stage", pr)])
                        S.dma("sp", bass.AP(kt_scr, scr_base + t * 128, [[4 * SCR_ROWS, 128], [SCR_ROWS, 4], [1, 128]]), stage[:, :, :],
                              [("stage", pr) for pr in range(4)], [("kt_scr", i, sq)])
            for i in range(NGDN):
                ck = "gconv_h%d" % i
                if kind == "p":
                    S.op("pool", lambda g: g.memset(gconv_h[i][:, :, :], 0.0), [], [ck])
                    S.op("pool", lambda g: g.memset(gdn_st[i][:, :, :], 0.0), [], [("gdn_st", i, h) for h in range(16)])
                else:
                    conv_state_load(gconv_h[i], ck, 32, st_gconv, (i * NS1 + sidx) * 3 * G_QKV, G_QKV)
                    S.dma("sp", gdn_st[i][:, :, :], bass.AP(st_gdn, (i * NS1 + sidx) * 16 * 128 * 128, [[128, 128], [128 * 128, 16], [1, 128]]),
                          [], [("gdn_st", i, h) for h in range(16)])

        def seq_end(sq, kind, sidx):
            for i in range(NHYB):
                ck = "conv_h%d" % i
                if kind == "p":
                    conv_state_store(conv_h[i], ck, 12, o_psc, i * 3 * SSM_XBC, SSM_XBC)
                    ssm_state_store(i, o_pss, i * 1024 * 128)
                    flat_copy(o_pk, i * KEEP * A_W, k_scr, ((i * NSEQ + sq) * SCR_ROWS + SEQ - KEEP) * A_W, KEEP * A_W, [("k_scr", i, sq)], [("out", "pk", i)])
                    flat_copy(o_pv, i * KEEP * A_W, v_scr, ((i * NSEQ + sq) * SCR_ROWS + SEQ - KEEP) * A_W, KEEP * A_W, [("v_scr", i, sq)], [("out", "pv", i)])
                else:
                    conv_state_store(conv_h[i], ck, 12, o_ssc, (i * NS1 + sidx) * 3 * SSM_XBC, SSM_XBC)
                    ssm_state_store(i, o_sss, (i * NS1 + sidx) * 1024 * 128)
                    flat_copy(o_sk, (i * NS1 + sidx) * WIN * A_W, k_scr, ((i * NSEQ + sq) * SCR_ROWS + 1) * A_W, WIN * A_W, [("k_scr", i, sq)], [("out", "sk", i, sidx)])
                    flat_copy(o_sv, (i * NS1 + sidx) * WIN * A_W, v_scr, ((i * NSEQ + sq) * SCR_ROWS + 1) * A_W, WIN * A_W, [("v_scr", i, sq)], [("out", "sv", i, sidx)])
            for i in range(NGDN):
                ck = "gconv_h%d" % i
                stk = [("gdn_st", i, h) for h in range(16)]
                if kind == "p":
                    conv_state_store(gconv_h[i], ck, 32, o_pgc, i * 3 * G_QKV, G_QKV)
                    S.dma("sp", bass.AP(o_pgs, i * 16 * 128 * 128, [[128, 128], [128 * 128, 16], [1, 128]]), gdn_st[i][:, :, :], stk, [("out", "pgs", i)])
                else:
                    conv_state_store(gconv_h[i], ck, 32, o_sgc, (i * NS1 + sidx) * 3 * G_QKV, G_QKV)
                    S.dma("sp", bass.AP(o_sgs, (i * NS1 + sidx) * 16 * 128 * 128, [[128, 128], [128 * 128, 16], [1, 128]]), gdn_st[i][:, :, :], stk,
                          [("out", "sgs", i, sidx)])

        for sq, (kind, sidx, t0, ntl) in enumerate(seqs):
            nvalid = 128 if kind == "p" else 1
            if kind == "s":
                S.op("pool", lambda g: g.memset(valid[:, :], 0.0), [], ["valid"])
                S.op("pool", lambda g: g.memset(valid[0:1, :], 1.0), ["valid"], ["valid"])
            seq_begin(sq, kind, sidx)
            for tl in range(ntl):
                tpos = t0 + tl
                if kind == "p":
                    S.dma("sp", xres[:, :], x_prompt[tl * 128:(tl + 1) * 128, :], [], ["xres"])
                else:
                    S.op("pool", lambda g: g.memset(xres[:, :], 0.0), [], ["xres"])
                    S.dma("sp", xres[0:1, :], x_sample[sidx:sidx + 1, :], ["xres"], ["xres"])
                for l in range(depth):
                    i = l // 2
                    if l % 2 == 0:
                        hybrid_load_params(i)
                        hybrid_layer(l, i, sq, tpos, nvalid)
                    else:
                        gdn_load_params(i)
                        gdn_layer(l, i, sq, tpos, nvalid)
                    ffn(l)
                if kind == "p":
                    S.dma("sp", y_prompt[tl * 128:(tl + 1) * 128, :], xres[:, :], ["xres"], ["y_prompt"])
                else:
                    S.dma("sp", y_sample[sidx:sidx + 1, :], xres[0:1, :], ["xres"], ["y_sample"])
            seq_end(sq, kind, sidx)

        for slot in S.dma_sems:
            if slot[1] > 0:
                S._wait("sp", (slot[0], slot[1]))
        for e in S.eng:
            if e != "sp" and S.cnt[e] > 0:
                S._wait("sp", (S.sem[e], S.cnt[e]))
        print("instructions:", S.ninstr, "waits:", S.nwait, "sems:", S.nsem, "sbuf_bytes/partition:", sb_total[0])
    return nc


_NC_CACHE = {}


def kernel(x_prompt, x_sample, cache_attn_k, cache_attn_v, state_ssm_conv, state_ssm, state_gdn_conv, state_gdn,
           rel_bias, norm_mix_pre, norm_mix_post, norm_ffn_pre, norm_ffn_post, w_hyb_in, ssm_conv_w, ssm_conv_b,
           ssm_dt_bias, ssm_a_log, ssm_d, ssm_norm_w, w_hyb_out, w_gdn_in, gdn_conv_w, gdn_dt_bias, gdn_a_log,
           gdn_norm_w, w_gdn_out, w_ffn_gate, w_ffn_up, w_ffn_down):
    f = lambda a: np.ascontiguousarray(np.asarray(a, dtype=np.float32))
    x_prompt = f(x_prompt)
    B, SEQ, _ = x_prompt.shape
    x_sample = f(x_sample)
    DB = x_sample.shape[0]
    depth = np.asarray(norm_mix_pre).shape[0]
    n_ptiles = SEQ // 128
    assert DB % NCORES == 0
    n_samp = DB // NCORES
    key = (n_ptiles, n_samp, depth)
    if key not in _NC_CACHE:
        _NC_CACHE[key] = build_nc(n_ptiles, n_samp, depth=depth)
    nc = _NC_CACHE[key]
    NHYB = (depth + 1) // 2
    NGDN = depth // 2
    shared = dict(
        rel_bias=f(rel_bias), oh_tab=attn_tables(), norm_mix_pre=f(norm_mix_pre), norm_mix_post=f(norm_mix_post),
        norm_ffn_pre=f(norm_ffn_pre), norm_ffn_post=f(norm_ffn_post), w_hyb_in=f(w_hyb_in), ssm_conv_w=f(ssm_conv_w),
        ssm_conv_b=f(ssm_conv_b), ssm_dt_bias=f(ssm_dt_bias), ssm_a_log=f(ssm_a_log), ssm_d=f(ssm_d), ssm_norm_w=f(ssm_norm_w),
        w_hyb_out=f(w_hyb_out), w_gdn_in=f(w_gdn_in), gdn_conv_w=f(gdn_conv_w), gdn_dt_bias=f(gdn_dt_bias),
        gdn_a_log=f(gdn_a_log), gdn_norm_w=f(gdn_norm_w), w_gdn_out=f(w_gdn_out), w_ffn_gate=f(w_ffn_gate),
        w_ffn_up=f(w_ffn_up), w_ffn_down=f(w_ffn_down))
    ck = f(cache_attn_k).reshape(NHYB, DB, WIN, A_W)
    cv = f(cache_attn_v).reshape(NHYB, DB, WIN, A_W)
    sc = f(state_ssm_conv)
    ss = f(state_ssm).reshape(NHYB, DB, 1024, 128)
    gc = f(state_gdn_conv)
    gst = f(state_gdn)
    in_maps = []
    for c in range(NCORES):
        sl = slice(c * n_samp, (c + 1) * n_samp)
        m = dict(shared)
        m["x_prompt"] = np.ascontiguousarray(x_prompt[c % B])
        m["x_sample"] = np.ascontiguousarray(x_sample[sl, 0, :])
        m["cache_k"] = np.ascontiguousarray(ck[:, sl])
        m["cache_v"] = np.ascontiguousarray(cv[:, sl])
        m["st_sconv"] = np.ascontiguousarray(sc[:, sl])
        m["st_ssm"] = np.ascontiguousarray(ss[:, sl])
        m["st_gconv"] = np.ascontiguousarray(gc[:, sl])
        m["st_gdn"] = np.ascontiguousarray(gst[:, sl])
        in_maps.append(m)
    res = run_bass_kernel_spmd(nc, in_maps, core_ids=list(range(NCORES)))
    R = res.results
    KEEP = min(WIN, SEQ)
    pc = list(range(B))
    y_prompt = np.stack([R[c]["y_prompt"] for c in pc], 0)
    y_sample = np.concatenate([R[c]["y_sample"] for c in range(NCORES)], 0)[:, None, :]
    pk = np.stack([R[c]["o_pk"] for c in pc], 1).reshape(NHYB, B, KEEP, 8, 64)
    pv = np.stack([R[c]["o_pv"] for c in pc], 1).reshape(NHYB, B, KEEP, 8, 64)
    psc = np.stack([R[c]["o_psc"] for c in pc], 1)
    pss = np.stack([R[c]["o_pss"] for c in pc], 1).reshape(NHYB, B, 16, 64, 128)
    pgc = np.stack([R[c]["o_pgc"] for c in pc], 1)
    pgs = np.stack([R[c]["o_pgs"] for c in pc], 1)
    cat = lambda nm: np.concatenate([R[c][nm] for c in range(NCORES)], 1)
    sk = cat("o_sk").reshape(NHYB, DB, WIN, 8, 64)
    sv = cat("o_sv").reshape(NHYB, DB, WIN, 8, 64)
    ssc = cat("o_ssc")
    sss = cat("o_sss").reshape(NHYB, DB, 16, 64, 128)
    sgc = cat("o_sgc")
    sgs = cat("o_sgs")
    return (y_prompt, y_sample, pk, pv, psc, pss, pgc, pgs, sk, sv, ssc, sss, sgc, sgs)
```
